# Optimizing a Trainium2 kernel written in Bass

```python
import math
import jax, jax.numpy as jnp
from jax import lax
import numpy as np

D_MODEL = 2048
BATCH = 8
SEQ = 4096
DEPTH = 4
DEC_BATCH = 8
DEC_SEQ = 16
PAST_LEN = 4096

CHUNK = 64
N_MIXERS = 3
N_A = (DEPTH + 2) // 3
N_B = (DEPTH + 1) // 3
N_C = DEPTH // 3
A_HEADS = 4
A_DV = D_MODEL // A_HEADS
A_DQK = A_DV // 2
A_IN = 2 * A_HEADS * A_DQK + 2 * A_HEADS * A_DV + 2 * A_HEADS
CONV_W = 3
S5_P = 16
S5_G = D_MODEL // S5_P
S5_N = 64
D_FF = -(-8 * D_MODEL // (3 * 256)) * 256
EPS = 1e-6

kernel_name = "hybrid_streaming_encoder_step"


def rmsnorm(x, g):
    xf = x.astype(jnp.float32)
    y = xf * lax.rsqrt(jnp.mean(xf * xf, axis=-1, keepdims=True) + EPS)
    return (y * g.astype(jnp.float32)).astype(x.dtype)


def to_chunks(a, lc):
    b, l = a.shape[0], a.shape[1]
    return jnp.moveaxis(a.reshape((b, l // lc, lc) + a.shape[2:]), 1, 0)


def from_chunks(a):
    a = jnp.moveaxis(a, 0, 1)
    return a.reshape((a.shape[0], a.shape[1] * a.shape[2]) + a.shape[3:])


def mlstm_chunk(carry, inp):
    C, n, m = carry
    q, k, v, li, lf = inp
    L = q.shape[1]
    b = jnp.swapaxes(jnp.cumsum(lf, axis=1), 1, 2)
    li = jnp.swapaxes(li, 1, 2)
    causal = jnp.arange(L)[:, None] >= jnp.arange(L)[None, :]
    dmat = jnp.where(causal, b[..., :, None] - b[..., None, :] + li[..., None, :], -jnp.inf)
    inter = b + m[..., None]
    m_t = jnp.maximum(inter, jnp.max(dmat, axis=-1))
    w = jnp.exp(dmat - m_t[..., None])
    s = jnp.einsum('blhd,bshd->bhls', q, k) * w
    inter_w = jnp.exp(inter - m_t)
    num = jnp.einsum('bhls,bshv->blhv', s, v) + jnp.einsum('bhl,blhd,bhdv->blhv', inter_w, q, C)
    den = jnp.sum(s, axis=-1) + inter_w * jnp.einsum('blhd,bhd->bhl', q, n)
    floor = jnp.maximum(jnp.abs(den), jnp.exp(-m_t))
    h = num / jnp.swapaxes(floor, 1, 2)[..., None]
    m_new = m_t[..., -1]
    w_last = w[..., -1, :]
    decay = inter_w[..., -1]
    C_new = decay[..., None, None] * C + jnp.einsum('bhs,bshd,bshv->bhdv', w_last, k, v)
    n_new = decay[..., None] * n + jnp.einsum('bhs,bshd->bhd', w_last, k)
    return (C_new, n_new, m_new), h


def mlstm_mixer(h, w_in, b_gates, g_hnorm, w_out, C0, n0, m0):
    f32 = jnp.float32
    bsz, L, _ = h.shape
    hk, hv = A_HEADS * A_DQK, A_HEADS * A_DV
    z = h @ w_in
    q, k, v, o, gates = jnp.split(z, [hk, 2 * hk, 2 * hk + hv, 2 * hk + 2 * hv], axis=-1)
    q = q.reshape(bsz, L, A_HEADS, A_DQK).astype(f32) * (A_DQK ** -0.5)
    k = k.reshape(bsz, L, A_HEADS, A_DQK).astype(f32)
    v = v.reshape(bsz, L, A_HEADS, A_DV).astype(f32)
    gates = gates.astype(f32) + b_gates.astype(f32)
    li = gates[..., :A_HEADS]
    lf = jax.nn.log_sigmoid(gates[..., A_HEADS:])
    lc = min(CHUNK, L)
    (C, n, m), hs = lax.scan(
        mlstm_chunk, (C0.astype(f32), n0.astype(f32), m0.astype(f32)),
        (to_chunks(q, lc), to_chunks(k, lc), to_chunks(v, lc), to_chunks(li, lc), to_chunks(lf, lc)))
    hs = rmsnorm(from_chunks(hs), g_hnorm.reshape(A_HEADS, A_DV))
    y = (hs.reshape(bsz, L, hv) * jax.nn.sigmoid(o.astype(f32))).astype(h.dtype) @ w_out
    return y, C.astype(C0.dtype), n.astype(n0.dtype), m.astype(m0.dtype)


def conv_mixer(h, w_in, w_conv, w_out, prev):
    L = h.shape[1]
    gb, gc, u = jnp.split(h @ w_in, 3, axis=-1)
    z = gc * u
    zp = jnp.concatenate([prev.astype(z.dtype), z], axis=1)
    conv = (zp[:, 0:L] * w_conv[:, 0] + zp[:, 1:L + 1] * w_conv[:, 1]
            + zp[:, 2:L + 2] * w_conv[:, 2])
    y = (gb * conv) @ w_out
    return y, zp[:, -(CONV_W - 1):].astype(prev.dtype)


def _cscan_combine(e1, e2):
    a1r, a1i, b1r, b1i = e1
    a2r, a2i, b2r, b2i = e2
    return (a1r * a2r - a1i * a2i, a1r * a2i + a1i * a2r,
            a2r * b1r - a2i * b1i + b2r, a2r * b1i + a2i * b1r + b2i)


def s5_mixer(h, a_re, a_im, b_re, b_im, c_re, c_im, d_skip, log_dt, w_out, s_re0, s_im0):
    f32 = jnp.float32
    bsz, L, _ = h.shape
    u = h.astype(f32).reshape(bsz, L, S5_G, S5_P)
    dt = jnp.exp(log_dt.astype(f32))[:, None]
    lr, lim = a_re.astype(f32), a_im.astype(f32)
    mag = jnp.exp(lr * dt)
    ab_re, ab_im = mag * jnp.cos(lim * dt), mag * jnp.sin(lim * dt)
    den = lr * lr + lim * lim
    nr = ab_re - 1.0
    fr = (nr * lr + ab_im * lim) / den
    fi = (ab_im * lr - nr * lim) / den
    br, bi = b_re.astype(f32), b_im.astype(f32)
    bb_re = fr[..., None] * br - fi[..., None] * bi
    bb_im = fr[..., None] * bi + fi[..., None] * br
    cr, ci = c_re.astype(f32), c_im.astype(f32)
    dsk = d_skip.astype(f32)

    def chunk_step(carry, uc):
        x_re, x_im = carry
        bu_re = jnp.einsum('blgp,gnp->blgn', uc, bb_re)
        bu_im = jnp.einsum('blgp,gnp->blgn', uc, bb_im)
        ar = jnp.broadcast_to(ab_re, bu_re.shape)
        ai = jnp.broadcast_to(ab_im, bu_re.shape)
        pa_re, pa_im, pb_re, pb_im = lax.associative_scan(
            _cscan_combine, (ar, ai, bu_re, bu_im), axis=1)
        st_re = pb_re + pa_re * x_re[:, None] - pa_im * x_im[:, None]
        st_im = pb_im + pa_re * x_im[:, None] + pa_im * x_re[:, None]
        y = (jnp.einsum('blgn,gpn->blgp', st_re, cr) - jnp.einsum('blgn,gpn->blgp', st_im, ci)
             + dsk * uc)
        return (st_re[:, -1], st_im[:, -1]), y

    lc = min(CHUNK, L)
    (s_re, s_im), ys = lax.scan(chunk_step, (s_re0.astype(f32), s_im0.astype(f32)), to_chunks(u, lc))
    y = jax.nn.gelu(from_chunks(ys).reshape(bsz, L, D_MODEL)).astype(h.dtype)
    za, zb = jnp.split(y @ w_out, 2, axis=-1)
    return za * jax.nn.sigmoid(zb), s_re.astype(s_re0.dtype), s_im.astype(s_im0.dtype)


def swiglu(h, wg, wu, wd):
    return (jax.nn.silu(h @ wg) * (h @ wu)) @ wd


def _trunk(x, c, st_C, st_n, st_m, st_conv, st_re, st_im, p):
    bsz = x.shape[0]
    sc = jax.nn.silu(c)
    new_C, new_n, new_m, new_conv, new_re, new_im = [], [], [], [], [], []
    for i in range(DEPTH):
        mod = (sc @ p['w_mod'][i] + p['b_mod'][i]).reshape(bsz, 6, 1, D_MODEL)
        sh1, sc1, g1, sh2, sc2, g2 = (mod[:, j] for j in range(6))
        gn = p['g_norm'][i]
        h = rmsnorm(x, gn[0]) * (1 + sc1) + sh1
        kind, j = i % N_MIXERS, i // N_MIXERS
        if kind == 0:
            out, C, n, m = mlstm_mixer(h, p['wA_in'][j], p['bA_gates'][j], p['gA_hnorm'][j],
                                       p['wA_out'][j], st_C[j], st_n[j], st_m[j])
            new_C.append(C); new_n.append(n); new_m.append(m)
        elif kind == 1:
            out, cv = conv_mixer(h, p['wB_in'][j], p['wB_conv'][j], p['wB_out'][j], st_conv[j])
            new_conv.append(cv)
        else:
            out, sr, si = s5_mixer(h, p['s5_A_re'][j], p['s5_A_im'][j], p['s5_B_re'][j],
                                   p['s5_B_im'][j], p['s5_C_re'][j], p['s5_C_im'][j],
                                   p['s5_D'][j], p['s5_log_dt'][j], p['wC_out'][j],
                                   st_re[j], st_im[j])
            new_re.append(sr); new_im.append(si)
        x = x + g1 * rmsnorm(out, gn[1])
        h = rmsnorm(x, gn[2]) * (1 + sc2) + sh2
        x = x + g2 * rmsnorm(swiglu(h, p['w_ffn_gate'][i], p['w_ffn_up'][i], p['w_ffn_down'][i]), gn[3])
    return (x, jnp.stack(new_C), jnp.stack(new_n), jnp.stack(new_m), jnp.stack(new_conv),
            jnp.stack(new_re), jnp.stack(new_im))


def setup_inputs(seed: int = 0) -> dict:
    key = jax.random.key(seed)
    ks = jax.random.split(key, 40)
    nrm = jax.random.normal
    D, F, f32 = D_MODEL, D_FF, jnp.float32
    f_bias = jnp.linspace(3.0, 6.0, A_HEADS)
    bA = jnp.concatenate([0.1 * nrm(ks[13], (N_A, A_HEADS)),
                          f_bias + 0.1 * nrm(ks[14], (N_A, A_HEADS))], axis=-1)
    return {
        "x_prompt": nrm(ks[0], (BATCH, SEQ, D), f32),
        "x_sample": nrm(ks[1], (DEC_BATCH, DEC_SEQ, D), f32),
        "state_mlstm_C": 0.3 * nrm(ks[2], (N_A, DEC_BATCH, A_HEADS, A_DQK, A_DV), f32),
        "state_mlstm_n": 0.3 * nrm(ks[3], (N_A, DEC_BATCH, A_HEADS, A_DQK), f32),
        "state_mlstm_m": 2.0 + 0.5 * nrm(ks[4], (N_A, DEC_BATCH, A_HEADS), f32),
        "state_conv": nrm(ks[5], (N_B, DEC_BATCH, CONV_W - 1, D), f32),
        "state_s5_re": 0.1 * nrm(ks[6], (N_C, DEC_BATCH, S5_G, S5_N), f32),
        "state_s5_im": 0.1 * nrm(ks[7], (N_C, DEC_BATCH, S5_G, S5_N), f32),
        "c_prompt": nrm(ks[8], (BATCH, D), f32),
        "c_sample": nrm(ks[9], (DEC_BATCH, D), f32),
        "w_mod": 0.5 * D ** -0.5 * nrm(ks[10], (DEPTH, D, 6 * D), f32),
        "b_mod": 0.02 * nrm(ks[11], (DEPTH, 6 * D), f32),
        "g_norm": 1.0 + 0.02 * nrm(ks[12], (DEPTH, 4, D), f32),
        "wA_in": D ** -0.5 * nrm(ks[15], (N_A, D, A_IN), f32),
        "bA_gates": bA.astype(f32),
        "gA_hnorm": 1.0 + 0.02 * nrm(ks[16], (N_A, A_HEADS * A_DV), f32),
        "wA_out": (A_HEADS * A_DV) ** -0.5 * nrm(ks[17], (N_A, A_HEADS * A_DV, D), f32),
        "wB_in": D ** -0.5 * nrm(ks[18], (N_B, D, 3 * D), f32),
        "wB_conv": CONV_W ** -0.5 * nrm(ks[19], (N_B, D, CONV_W), f32),
        "wB_out": D ** -0.5 * nrm(ks[20], (N_B, D, D), f32),
        "s5_A_re": -0.5 + 0.01 * nrm(ks[21], (N_C, S5_G, S5_N), f32),
        "s5_A_im": jnp.pi * jnp.arange(S5_N, dtype=f32) + 0.01 * nrm(ks[22], (N_C, S5_G, S5_N), f32),
        "s5_B_re": (2 * S5_P) ** -0.5 * nrm(ks[23], (N_C, S5_G, S5_N, S5_P), f32),
        "s5_B_im": (2 * S5_P) ** -0.5 * nrm(ks[24], (N_C, S5_G, S5_N, S5_P), f32),
        "s5_C_re": (2 * S5_N) ** -0.5 * nrm(ks[25], (N_C, S5_G, S5_P, S5_N), f32),
        "s5_C_im": (2 * S5_N) ** -0.5 * nrm(ks[26], (N_C, S5_G, S5_P, S5_N), f32),
        "s5_D": nrm(ks[27], (N_C, S5_G, S5_P), f32),
        "s5_log_dt": jax.random.uniform(ks[28], (N_C, S5_G), f32, math.log(1e-3), math.log(1e-1)),
        "wC_out": D ** -0.5 * nrm(ks[29], (N_C, D, 2 * D), f32),
        "w_ffn_gate": D ** -0.5 * nrm(ks[30], (DEPTH, D, F), f32),
        "w_ffn_up": D ** -0.5 * nrm(ks[31], (DEPTH, D, F), f32),
        "w_ffn_down": F ** -0.5 * nrm(ks[32], (DEPTH, F, D), f32),
    }


def reference(x_prompt, x_sample, state_mlstm_C, state_mlstm_n, state_mlstm_m, state_conv,
              state_s5_re, state_s5_im, c_prompt, c_sample, w_mod, b_mod, g_norm, wA_in,
              bA_gates, gA_hnorm, wA_out, wB_in, wB_conv, wB_out, s5_A_re, s5_A_im, s5_B_re,
              s5_B_im, s5_C_re, s5_C_im, s5_D, s5_log_dt, wC_out, w_ffn_gate, w_ffn_up,
              w_ffn_down):
    p = dict(w_mod=w_mod, b_mod=b_mod, g_norm=g_norm, wA_in=wA_in, bA_gates=bA_gates,
             gA_hnorm=gA_hnorm, wA_out=wA_out, wB_in=wB_in, wB_conv=wB_conv, wB_out=wB_out,
             s5_A_re=s5_A_re, s5_A_im=s5_A_im, s5_B_re=s5_B_re, s5_B_im=s5_B_im,
             s5_C_re=s5_C_re, s5_C_im=s5_C_im, s5_D=s5_D, s5_log_dt=s5_log_dt, wC_out=wC_out,
             w_ffn_gate=w_ffn_gate, w_ffn_up=w_ffn_up, w_ffn_down=w_ffn_down)
    bp, dt = x_prompt.shape[0], state_mlstm_C.dtype
    z_C = jnp.zeros((N_A, bp, A_HEADS, A_DQK, A_DV), dt)
    z_n = jnp.zeros((N_A, bp, A_HEADS, A_DQK), dt)
    z_m = jnp.zeros((N_A, bp, A_HEADS), dt)
    z_conv = jnp.zeros((N_B, bp, CONV_W - 1, D_MODEL), state_conv.dtype)
    z_re = jnp.zeros((N_C, bp, S5_G, S5_N), state_s5_re.dtype)
    z_im = jnp.zeros((N_C, bp, S5_G, S5_N), state_s5_im.dtype)
    y_prompt, pC, pn, pm, pconv, pre, pim = _trunk(x_prompt, c_prompt, z_C, z_n, z_m, z_conv,
                                                   z_re, z_im, p)
    y_sample, sC, sn, sm, sconv, sre, sim = _trunk(x_sample, c_sample, state_mlstm_C,
                                                   state_mlstm_n, state_mlstm_m, state_conv,
                                                   state_s5_re, state_s5_im, p)
    return (y_prompt, y_sample, pC, pn, pm, pconv, pre, pim, sC, sn, sm, sconv, sre, sim)
```

```python
import math
import numpy as np
import concourse.bass as bass
import concourse.mybir as mybir
from concourse.bass_utils import run_bass_kernel_spmd
from contextlib import ExitStack

F32 = mybir.dt.float32
BF16 = mybir.dt.bfloat16
AF = mybir.ActivationFunctionType
ALU = mybir.AluOpType

ENGS = ("pe", "act", "dve", "pool", "sp")
SAME_ENGINE_SYNC = {"pe": False, "act": True, "dve": True, "pool": True, "sp": False}
DT_SIZE = {F32: 4, BF16: 2}


class Track:
    def __init__(self, size, name=""):
        self.size = size
        self.name = name
        self.segs = [[0, size, None, []]]

    def _split(self, pos):
        segs = self.segs
        lo, hi = 0, len(segs)
        while lo < hi:
            mid = (lo + hi) // 2
            if segs[mid][1] <= pos:
                lo = mid + 1
            else:
                hi = mid
        if lo < len(segs):
            s = segs[lo]
            if s[0] < pos < s[1]:
                segs[lo:lo + 1] = [[s[0], pos, s[2], list(s[3])], [pos, s[1], s[2], list(s[3])]]

    def access(self, lo, hi, op_idx, write):
        assert 0 <= lo < hi <= self.size, (self.name, lo, hi, self.size)
        self._split(lo)
        self._split(hi)
        deps = set()
        segs = self.segs
        a, b = 0, len(segs)
        while a < b:
            mid = (a + b) // 2
            if segs[mid][0] < lo:
                a = mid + 1
            else:
                b = mid
        k = a
        first = k
        while k < len(segs) and segs[k][1] <= hi:
            s = segs[k]
            if s[2] is not None:
                deps.add(s[2])
            if write:
                deps.update(s[3])
            else:
                s[3].append(op_idx)
            k += 1
        if write:
            segs[first:k] = [[lo, hi, op_idx, []]]
        return deps


class Buf:
    def __init__(self, prog, name, shape, dtype, space="sbuf", arena=None, off=0, ap=None):
        self.prog = prog
        self.name = name
        self.shape = list(shape)
        self.dtype = dtype
        self.space = space
        self.esz = DT_SIZE[dtype]
        self.fsize = int(np.prod(self.shape[1:]))
        nc = prog.nc
        self.arena = arena
        self.off = off
        if arena is not None:
            n16 = self.fsize * self.esz // 2
            assert off % 4 == 0 and off + n16 * 2 <= arena.fsize * 2, (name, off, n16, arena.fsize)
            base = arena.t[:, off // 2: off // 2 + n16]
            v = base.bitcast(dtype) if dtype != BF16 else base
            if len(self.shape) > 2:
                names = " ".join("d%d" % i for i in range(1, len(self.shape)))
                kw = {"d%d" % i: self.shape[i] for i in range(1, len(self.shape))}
                v = v.rearrange("p (%s) -> p %s" % (names, names), **kw)
            self.t = v
            self.track = arena.track
        elif space == "sbuf":
            self.t = prog.stack.enter_context(nc.sbuf_tensor(name, self.shape, dtype))
            self.track = Track(self.fsize * self.esz, name)
        elif space == "psum":
            self.t = prog.stack.enter_context(nc.psum_tensor(name, self.shape, dtype))
            self.track = Track(self.fsize * self.esz, name)
        else:
            self.t = ap
            self.track = Track(self.fsize, name)
            self.esz = 1

    def __getitem__(self, idx):
        return self.t[idx]

    def r(self, lo=0, hi=None):
        hi = self.fsize if hi is None else hi
        return (self.track, self.off + lo * self.esz, self.off + hi * self.esz)

    def sub(self, name, off, shape, dtype):
        return Buf(self.prog, name, shape, dtype, arena=self, off=off)


class Op:
    __slots__ = ("eng", "fn", "deps", "is_dma", "signal", "count", "sem", "idx")


class Prog:
    def __init__(self, nc):
        self.nc = nc
        self.stack = ExitStack()
        self.ops = []
        self.n_dma_sems = {"pool": 24, "sp": 40, "act": 4}

    def sb(self, name, shape, dtype=F32):
        return Buf(self, name, shape, dtype, "sbuf")

    def ps(self, name, shape, dtype=F32):
        return Buf(self, name, shape, dtype, "psum")

    def dram(self, name, nblocks):
        return Buf(self, name, [1, nblocks], BF16, "dram")

    def op(self, eng, fn, reads=(), writes=(), dma=False):
        o = Op()
        o.eng = eng
        o.fn = fn
        o.is_dma = dma
        o.idx = len(self.ops)
        o.signal = False
        o.count = None
        o.sem = None
        deps = set()
        for (tr, lo, hi) in reads:
            deps |= tr.access(lo, hi, o.idx, False)
        for (tr, lo, hi) in writes:
            deps |= tr.access(lo, hi, o.idx, True)
        deps.discard(o.idx)
        o.deps = deps
        self.ops.append(o)
        return o

    def emit(self, final_wait_engine="sp"):
        nc = self.nc
        ops = self.ops
        stack = self.stack
        for o in ops:
            latest = {}
            keep = set()
            for d in o.deps:
                p = ops[d]
                if p.is_dma:
                    keep.add(d)
                    continue
                if p.eng == o.eng and (not o.is_dma) and not SAME_ENGINE_SYNC[p.eng]:
                    continue
                if latest.get(p.eng, -1) < d:
                    latest[p.eng] = d
            keep.update(latest.values())
            o.deps = keep
            for d in keep:
                if not ops[d].is_dma:
                    ops[d].signal = True
        eng_sem = {e: stack.enter_context(nc.semaphore("S_" + e)) for e in ENGS}
        dma_sems = {e: [stack.enter_context(nc.semaphore("D_%s_%d" % (e, i))) for i in range(n)]
                    for e, n in self.n_dma_sems.items()}
        dma_use = {e: [0] * n for e, n in self.n_dma_sems.items()}
        dma_rr = {e: 0 for e in self.n_dma_sems}
        counts = {e: 0 for e in ENGS}
        pre_wait = {}
        for o in ops:
            if o.is_dma:
                e = o.eng
                k = dma_rr[e]
                dma_rr[e] = (k + 1) % len(dma_sems[e])
                if dma_use[e][k] > 0:
                    pre_wait[o.idx] = (dma_sems[e][k], dma_use[e][k] * 16)
                dma_use[e][k] += 1
                o.sem = dma_sems[e][k]
                o.count = dma_use[e][k] * 16
            elif o.signal:
                counts[o.eng] += 1
                o.sem = eng_sem[o.eng]
                o.count = counts[o.eng]
        by_eng = {e: [o for o in ops if o.eng == e] for e in ENGS}
        all_dma = [o for o in ops if o.is_dma]
        self.stats = {e: len(by_eng[e]) for e in ENGS}
        self.stats["signals"] = dict(counts)

        def run_engine(ename, eng):
            waited = {}
            nwaits = 0

            def wait(sem, val):
                nonlocal nwaits
                key = id(sem)
                if waited.get(key, 0) >= val:
                    return
                waited[key] = val
                eng.wait_ge(sem, val)
                nwaits += 1

            attach = ename in ("act", "dve")
            for o in by_eng[ename]:
                if o.idx in pre_wait:
                    wait(*pre_wait[o.idx])
                need = []
                for d in sorted(o.deps):
                    p = ops[d]
                    if p.sem is None:
                        continue
                    if waited.get(id(p.sem), 0) >= p.count:
                        continue
                    need.append((p.sem, p.count))
                best = {}
                for sem, val in need:
                    if best.get(id(sem), (None, 0))[1] < val:
                        best[id(sem)] = (sem, val)
                need = list(best.values())
                last = None
                if attach and (not o.is_dma) and need:
                    last = need.pop()
                for sem, val in need:
                    wait(sem, val)
                ins = o.fn(eng)
                if last is not None:
                    ins._wait_ge(last[0], last[1])
                    waited[id(last[0])] = max(waited.get(id(last[0]), 0), last[1])
                    nwaits += 1
                if o.sem is not None:
                    ins.then_inc(o.sem, 16 if o.is_dma else 1)
            if ename == final_wait_engine:
                last = {}
                for o in all_dma:
                    last[id(o.sem)] = (o.sem, o.count)
                for sem, val in last.values():
                    wait(sem, val)
                for e in ENGS:
                    if counts[e] > 0:
                        wait(eng_sem[e], counts[e])
            self.stats["waits_" + ename] = nwaits

        with nc.Block() as block:
            @block.tensor
            def _(eng):
                run_engine("pe", eng)

            @block.scalar
            def _(eng):
                run_engine("act", eng)

            @block.vector
            def _(eng):
                run_engine("dve", eng)

            @block.gpsimd
            def _(eng):
                run_engine("pool", eng)

            @block.sync
            def _(eng):
                run_engine("sp", eng)
        self.stack.close()


D = 2048
KC = 16
DEPTH = 4
NH = 4
DQK = 256
DV = 512
A_IN = 6152
DFF = 5632
FC = 44
S5G = 128
S5N = 64
S5P = 16
EPS = 1e-6
TT = 512
TS = 16
WB = 256
BIG = 30000.0

IN_SPECS = [
    ("x_prompt", None), ("x_sample", [TS, D]),
    ("state_mlstm_C", [2, NH, DQK, DV]), ("state_mlstm_n", [2, NH, DQK]), ("state_mlstm_m", [2, NH]),
    ("state_conv", [1, 2, D]), ("state_s5_re", [1, S5G, S5N]), ("state_s5_im", [1, S5G, S5N]),
    ("c_prompt", [1, D]), ("c_sample", [1, D]),
    ("w_mod", [DEPTH, D, 6 * D]), ("b_mod", [DEPTH, 6 * D]), ("g_norm", [DEPTH, 4, D]),
    ("wA_in", [2, D, A_IN]), ("bA_gates", [2, 8]), ("gA_hnorm", [2, D]), ("wA_out", [2, D, D]),
    ("wB_in", [1, D, 3 * D]), ("wB_conv", [1, D, 3]), ("wB_out", [1, D, D]),
    ("s5_A_re", [1, S5G, S5N]), ("s5_A_im", [1, S5G, S5N]),
    ("s5_B_re", [1, S5G, S5N, S5P]), ("s5_B_im", [1, S5G, S5N, S5P]),
    ("s5_C_re", [1, S5G, S5P, S5N]), ("s5_C_im", [1, S5G, S5P, S5N]),
    ("s5_D", [1, S5G, S5P]), ("s5_log_dt", [1, S5G]), ("wC_out", [1, D, 2 * D]),
    ("w_ffn_gate", [DEPTH, D, DFF]), ("w_ffn_up", [DEPTH, D, DFF]), ("w_ffn_down", [DEPTH, DFF, D]),
]


def build_program(n_tiles=8, n_layers=DEPTH, with_sample=True, mix="rrrr"):
    SEQ = n_tiles * TT
    nc = bass.Bass("TRN2", target_bir_lowering=False)
    P = Prog(nc)
    di = {}
    for name, shp in IN_SPECS:
        if name == "x_prompt":
            shp = [SEQ, D]
        elif shp[0] == DEPTH:
            shp = [max(n_layers, 1)] + list(shp[1:])
        di[name] = nc.dram_tensor(name, shp, F32, kind="ExternalInput").ap()
    do = {}

    def OUT(name, shp):
        do[name] = nc.dram_tensor(name, shp, F32, kind="ExternalOutput").ap()

    OUT("y_prompt", [SEQ, D]); OUT("y_sample", [TS, D])
    for pre in ("p", "s"):
        OUT(pre + "_C", [2, NH, DQK, DV]); OUT(pre + "_n", [2, NH, DQK]); OUT(pre + "_m", [2, NH])
        OUT(pre + "_conv", [1, 2, D]); OUT(pre + "_re", [1, S5G, S5N]); OUT(pre + "_im", [1, S5G, S5N])

    def dma(eng, out, in_, reads=(), writes=(), nonc=False):
        if nonc:
            def fn(e):
                with nc.allow_non_contiguous_dma(reason="small strided parameter/state relayout"):
                    return e.dma_start(out=out, in_=in_)
        else:
            def fn(e):
                return e.dma_start(out=out, in_=in_)
        return P.op(eng, fn, reads=reads, writes=writes, dma=True)

    MAIN = P.sb("main", [128, (32768 + 16384 + 32768 + 55296) // 2], BF16)
    X = MAIN.sub("x", 0, [128, KC, TT], F32)
    H = MAIN.sub("h", 32768, [128, KC, TT], BF16)
    O = MAIN.sub("o", 49152, [128, KC, TT], F32)
    SCR0 = 81920
    YS4 = MAIN.sub("ys4", 49152, [128, 4, D], F32)
    NSLOT = 4
    RING = [P.sb("ring%d" % i, [128, 4096], BF16) for i in range(NSLOT)]
    CST = P.sb("cst", [128, 8192 + 4096], BF16)
    C32 = CST.sub("C32", 0, [128, 8, DV], F32)
    CBF = CST.sub("Cbf", 16384, [128, 8, DV], BF16)
    PS = [P.ps("bank%d" % i, [128, 512], F32) for i in range(8)]

    ident = P.sb("ident", [128, 128], BF16)
    ones_bf = P.sb("ones_bf", [128, 128], BF16)
    maskbig = P.sb("maskbig", [128, 128], BF16)
    ones_f = P.sb("ones_f", [128, 512], F32)
    epsb = P.sb("epsb", [128, 1], F32)
    coef = P.sb("coef", [128, DEPTH, 6, KC, 2], F32)
    TMPA_RAW = P.sb("tmpA_raw", [128, 2 * TT * 2], BF16)
    tmpA = TMPA_RAW.sub("tmpA", 0, [128, 2, TT], F32)
    rstd = P.sb("rstd", [128, TT], F32)
    tmpB = P.sb("tmpB", [128, 2, TT], BF16)

    P_nrep = P.sb("nrep", [128, 8, 128], BF16)
    P_gelu = tmpA
    ones8 = ones_f[:, 0:128].unsqueeze(1).to_broadcast([128, 8, 128])

    def pe(fn, reads, writes):
        return P.op("pe", fn, reads, writes)

    def act(fn, reads, writes):
        return P.op("act", fn, reads, writes)

    def dve(fn, reads, writes):
        return P.op("dve", fn, reads, writes)

    def pool(fn, reads, writes):
        return P.op("pool", fn, reads, writes)

    pool(lambda e: e.memset(ident[:, :], 1.0), [], [ident.r()])
    pool(lambda e: e.affine_select(out=ident[:, :], in_=ident[:, :], pattern=[[-1, 128]], compare_op=ALU.is_equal,
                                   fill=0.0, base=0, channel_multiplier=1), [ident.r()], [ident.r()])
    pool(lambda e: e.memset(ones_bf[:, :], 1.0), [], [ones_bf.r()])
    pool(lambda e: e.memset(ones_f[:, :], 1.0), [], [ones_f.r()])
    pool(lambda e: e.memset(epsb[:, :], EPS), [], [epsb.r()])
    pool(lambda e: e.memset(maskbig[:, :], BIG), [], [maskbig.r()])
    pool(lambda e: e.affine_select(out=maskbig[:, :], in_=maskbig[:, :], pattern=[[-1, 128]], compare_op=ALU.is_gt,
                                   fill=0.0, base=0, channel_multiplier=1), [maskbig.r()], [maskbig.r()])

    wcache = {}
    scratch = {}
    ring_i = [0]

    def scratch_for(name, nblk):
        if name not in scratch:
            t = nc.dram_tensor("scr_" + name, [nblk, 128, 4096], BF16, kind="Internal").ap()
            scratch[name] = (t, P.dram("scrT_" + name, nblk))
        return scratch[name]

    def wblock(name, nblk, blk, src_ap, nel, view_shape, cache=True):
        slot = RING[ring_i[0] % NSLOT]
        ring_i[0] += 1
        names = " ".join("d%d" % i for i in range(len(view_shape)))
        kw = {"d%d" % i: view_shape[i] for i in range(len(view_shape))}
        view = slot.t[:, 0:nel].rearrange("p (%s) -> p %s" % (names, names), **kw)
        if cache and (name, blk) in wcache:
            sc, tr = scratch[name]
            dma("sp", slot.t[:, 0:nel], sc[blk, :, 0:nel], reads=[tr.r(blk, blk + 1)], writes=[slot.r(0, nel)])
        else:
            def fn(e):
                with nc.allow_non_contiguous_dma(reason="weight block"):
                    return e.dma_start(out=view, in_=src_ap)
            P.op("pool", fn, reads=[], writes=[slot.r(0, nel)], dma=True)
            if cache:
                sc, tr = scratch_for(name, nblk)
                dma("sp", sc[blk, :, 0:nel], slot.t[:, 0:nel], reads=[slot.r(0, nel)], writes=[tr.r(blk, blk + 1)])
                wcache[(name, blk)] = True
        return slot, view

    def colblock(name, w2d, ncols_total, c0, width=WB, cache=True):
        src = w2d[:, c0:c0 + width].rearrange("(kc p) c -> p kc c", p=128)
        nblk = (ncols_total + width - 1) // width
        return wblock(name, nblk, c0 // width, src, KC * width, [KC, width], cache=cache)

    class Ctx:
        pass

    def mk_ctx(T, seq, xin, yout):
        c = Ctx()
        c.T = T
        c.seq = seq
        c.xin = xin
        c.yout = yout
        c.L = min(128, T)
        c.NB = T // c.L
        return c

    def xr(b, kc, T, n=1):
        return b.r(kc * TT, (kc + n - 1) * TT + T)

    IOB = SCR0
    NPART = 2

    def load_x(c):
        T, L = c.T, c.L
        xs = MAIN.sub("xs", IOB + 16384, [128, D], F32)
        xh = MAIN.sub("xh", IOB + 16384 + 8192, [128, NPART, D], BF16)
        for tb in range(c.NB):
            t0 = tb * L
            dma("sp", xs[0:L, :], c.xin[t0:t0 + L, :], writes=[xs.r()])
            act(lambda e: e.copy(out=xh[0:L, 0, :], in_=xs[0:L, :]), [xs.r()], [xh.r(0, D)])
            dve(lambda e: e.tensor_sub(out=xs[0:L, :], in0=xs[0:L, :], in1=xh[0:L, 0, :]), [xs.r(), xh.r(0, D)], [xs.r()])
            act(lambda e: e.copy(out=xh[0:L, 1, :], in_=xs[0:L, :]), [xs.r()], [xh.r(D, 2 * D)])
            for k4 in range(4):
                bank = PS[4 + (k4 % 2)]
                for q in range(4):
                    kc = k4 * 4 + q
                    for part in range(NPART):
                        pe(lambda e, kc=kc, q=q, part=part, bank=bank: e.matmul(
                            bank[:, q * 128:q * 128 + L], lhsT=xh[0:L, part, kc * 128:(kc + 1) * 128], rhs=ident[0:L, 0:L],
                            start=(part == 0), stop=(part == NPART - 1)),
                           [xh.r(), ident.r()], [bank.r(q * 128, q * 128 + L)])
                src = bank.t[:, :].rearrange("p (q t) -> p q t", q=4)[:, :, 0:L]
                dst = X[:, k4 * 4:k4 * 4 + 4, t0:t0 + L]
                dve(lambda e, src=src, dst=dst: e.tensor_copy(out=dst, in_=src), [bank.r()], [xr(X, k4 * 4, T, 4)])

    def store_y(c):
        T, L = c.T, c.L
        ys = MAIN.sub("ys", IOB + 40960, [128, D], F32)
        xh = MAIN.sub("yh", IOB + 49152, [128, NPART, 4, 128], BF16)
        r1 = MAIN.sub("yr", IOB + 49152 + 2048, [128, 4, 128], F32)
        YSRC = O if n_layers > 0 else X
        for tb in range(c.NB):
            t0 = tb * L
            for k4 in range(4):
                xa = YSRC[:, k4 * 4:(k4 + 1) * 4, t0:t0 + L]
                rr = xr(YSRC, k4 * 4, T, 4)
                act(lambda e, xa=xa: e.copy(out=xh[:, 0, :, 0:L], in_=xa), [rr], [xh.r(0, 512)])
                dve(lambda e, xa=xa: e.tensor_sub(out=r1[:, :, 0:L], in0=xa, in1=xh[:, 0, :, 0:L]), [rr, xh.r(0, 512)], [r1.r()])
                act(lambda e: e.copy(out=xh[:, 1, :, 0:L], in_=r1[:, :, 0:L]), [r1.r()], [xh.r(512, 1024)])
                bank = PS[6 + (k4 % 2)]
                for q in range(4):
                    for part in range(NPART):
                        pe(lambda e, q=q, part=part, bank=bank: e.matmul(
                            bank[0:L, q * 128:(q + 1) * 128], lhsT=xh[:, part, q, 0:L], rhs=ident[:, :],
                            start=(part == 0), stop=(part == NPART - 1)),
                           [xh.r(), ident.r()], [bank.r(q * 128, (q + 1) * 128)])
                dve(lambda e, bank=bank, k4=k4: e.tensor_copy(out=ys[0:L, k4 * 512:(k4 + 1) * 512], in_=bank[0:L, :]),
                    [bank.r()], [ys.r(k4 * 512, (k4 + 1) * 512)])
            dma("sp", c.yout[t0:t0 + L, :], ys[0:L, :], reads=[ys.r()])

    def mod_setup():
        cf = TMPA_RAW.sub("cf", 1408, [128, KC, 2], F32)
        scb = P.sb("scb", [128, KC, 2], BF16)
        gn = TMPA_RAW.sub("gn_l", 1152, [128, 4, KC], F32)
        bm = TMPA_RAW.sub("bm_l", 768, [128, 96], F32)
        modv = TMPA_RAW.sub("modv", 0, [128, 96, 2], F32)
        for s, nm in enumerate(("c_prompt", "c_sample")):
            dma("sp", cf[:, :, s], di[nm][0, :].rearrange("(kc p) -> p kc", p=128), writes=[cf.r()], nonc=True)
        act(lambda e: e.activation(out=scb[:, :, :], in_=cf[:, :, :], func=AF.Silu), [cf.r()], [scb.r()])
        return cf, scb, gn, bm, modv

    MODB = mod_setup() if n_layers > 0 else None

    def mod_layer(l):
        cf, scb, gn, bm, modv = MODB
        if True:
            bank = PS[6]
            for b in range(6 * D // WB):
                slot, wv = colblock("w_mod%d" % l, di["w_mod"][l], 6 * D, b * WB, cache=False)
                for half in range(2):
                    m = 2 * b + half
                    for kc in range(KC):
                        pe(lambda e, wv=wv, m=m, kc=kc, half=half, bank=bank: e.matmul(
                            bank[:, 2 * m:2 * m + 2], lhsT=wv[:, kc, half * 128:(half + 1) * 128], rhs=scb[:, kc, :],
                            start=(kc == 0), stop=(kc == KC - 1)),
                           [slot.r(), scb.r()], [bank.r(2 * m, 2 * m + 2)])
                yield
            dma("sp", gn[:, :, :], di["g_norm"][l].rearrange("j (kc p) -> p j kc", p=128), writes=[gn.r()], nonc=True)
            dma("sp", bm[:, :], di["b_mod"][l].rearrange("(m p) -> p m", p=128), writes=[bm.r()], nonc=True)
            bmb = bm[:, :].unsqueeze(2).to_broadcast([128, 96, 2])
            dve(lambda e, bank=bank, bmb=bmb: e.tensor_tensor(
                out=modv[:, :, :], in0=bank[:, 0:192].rearrange("p (m s) -> p m s", s=2), in1=bmb, op=ALU.add),
                [bank.r(0, 192), bm.r()], [modv.r()])
            def mv(j):
                return modv[:, j * 16:(j + 1) * 16, :]

            def gnb(j):
                return gn[:, j, :].unsqueeze(2).to_broadcast([128, KC, 2])
            g0, g1_, g2_, g3_ = gnb(0), gnb(1), gnb(2), gnb(3)
            m0_, m1_, m2_, m3_, m4_, m5_ = (mv(j) for j in range(6))
            cf_ = [coef[:, l, q, :, :] for q in range(6)]
            dve(lambda e, o=cf_[0], a=m1_, g=g0: e.scalar_tensor_tensor(out=o, in0=a, scalar=1.0, in1=g, op0=ALU.add, op1=ALU.mult),
                [modv.r(), gn.r()], [coef.r()])
            dve(lambda e, o=cf_[1], a=m0_: e.tensor_copy(out=o, in_=a), [modv.r()], [coef.r()])
            dve(lambda e, o=cf_[2], a=m2_, g=g1_: e.tensor_tensor(out=o, in0=a, in1=g, op=ALU.mult), [modv.r(), gn.r()], [coef.r()])
            dve(lambda e, o=cf_[3], a=m4_, g=g2_: e.scalar_tensor_tensor(out=o, in0=a, scalar=1.0, in1=g, op0=ALU.add, op1=ALU.mult),
                [modv.r(), gn.r()], [coef.r()])
            dve(lambda e, o=cf_[4], a=m3_: e.tensor_copy(out=o, in_=a), [modv.r()], [coef.r()])
            dve(lambda e, o=cf_[5], a=m5_, g=g3_: e.tensor_tensor(out=o, in0=a, in1=g, op=ALU.mult), [modv.r(), gn.r()], [coef.r()])

    def stats(c, src, sqbuf):
        T = c.T
        bank = PS[5]
        for kc in range(KC):
            act(lambda e, kc=kc: e.activation(out=sqbuf[:, kc, 0:T], in_=src[:, kc, 0:T], func=AF.Square),
                [xr(src, kc, T)], [xr(sqbuf, kc, T)])
            pe(lambda e, kc=kc: e.matmul(bank[:, 0:T], lhsT=ones_bf[:, :], rhs=sqbuf[:, kc, 0:T], start=(kc == 0), stop=(kc == KC - 1)),
               [ones_bf.r(), xr(sqbuf, kc, T)], [bank.r(0, T)])
        act(lambda e: e.activation(out=rstd[:, 0:T], in_=bank[:, 0:T], func=AF.Sqrt, bias=epsb[:, 0:1], scale=1.0 / D),
            [bank.r(0, T), epsb.r()], [rstd.r(0, T)])
        dve(lambda e: e.reciprocal(out=rstd[:, 0:T], in_=rstd[:, 0:T]), [rstd.r(0, T)], [rstd.r(0, T)])

    def prenorm(c, l, which):
        T = c.T
        qa, qb = (0, 1) if which == 0 else (3, 4)
        stats(c, X, H)
        for kc in range(KC):
            tb = kc % 2
            dve(lambda e, kc=kc, tb=tb: e.scalar_tensor_tensor(
                out=tmpA[:, tb, 0:T], in0=X[:, kc, 0:T], scalar=coef[:, l, qa, kc, c.seq:c.seq + 1], in1=rstd[:, 0:T],
                op0=ALU.mult, op1=ALU.mult), [xr(X, kc, T), coef.r(), rstd.r(0, T)], [tmpA.r(tb * TT, tb * TT + T)])
            act(lambda e, kc=kc, tb=tb: e.activation(out=H[:, kc, 0:T], in_=tmpA[:, tb, 0:T], func=AF.Identity,
                                                     bias=coef[:, l, qb, kc, c.seq:c.seq + 1], scale=1.0),
                [tmpA.r(tb * TT, tb * TT + T), coef.r()], [xr(H, kc, T)])

    def postnorm(c, l, which, final=False):
        T = c.T
        qg = 2 if which == 0 else 5
        stats(c, O, H)
        for kc in range(KC):
            tb = kc % 2
            dve(lambda e, kc=kc, tb=tb: e.scalar_tensor_tensor(
                out=tmpA[:, tb, 0:T], in0=O[:, kc, 0:T], scalar=coef[:, l, qg, kc, c.seq:c.seq + 1], in1=rstd[:, 0:T],
                op0=ALU.mult, op1=ALU.mult), [xr(O, kc, T), coef.r(), rstd.r(0, T)], [tmpA.r(tb * TT, tb * TT + T)])
            if final:
                pool(lambda e, kc=kc, tb=tb: e.tensor_tensor(out=O[:, kc, 0:T], in0=X[:, kc, 0:T], in1=tmpA[:, tb, 0:T], op=ALU.add),
                     [xr(X, kc, T), tmpA.r(tb * TT, tb * TT + T)], [xr(O, kc, T)])
            else:
                pool(lambda e, kc=kc, tb=tb: e.tensor_tensor(out=X[:, kc, 0:T], in0=X[:, kc, 0:T], in1=tmpA[:, tb, 0:T], op=ALU.add),
                     [xr(X, kc, T), tmpA.r(tb * TT, tb * TT + T)], [xr(X, kc, T)])

    ACTB = MAIN.sub("ffn_act", SCR0, [128, FC, TT], BF16)

    def ffn(cs, l, side=None):
        def pump():
            if side is not None:
                next(side, None)
        for b in range(DFF // WB):
            pump()
            sg, wg = colblock("ffg%d" % l, di["w_ffn_gate"][l], DFF, b * WB)
            su, wu = colblock("ffu%d" % l, di["w_ffn_up"][l], DFF, b * WB)
            for half in range(2):
                f = 2 * b + half
                for c in cs:
                    T = c.T
                    pg = PS[f % 2]
                    pu = PS[2 + f % 2]
                    for kc in range(KC):
                        pe(lambda e, kc=kc, half=half, pg=pg, wg=wg, T=T: e.matmul(
                            pg[:, 0:T], lhsT=wg[:, kc, half * 128:(half + 1) * 128], rhs=H[:, kc, 0:T],
                            start=(kc == 0), stop=(kc == KC - 1)), [sg.r(), xr(H, kc, T)], [pg.r(0, T)])
                    for kc in range(KC):
                        pe(lambda e, kc=kc, half=half, pu=pu, wu=wu, T=T: e.matmul(
                            pu[:, 0:T], lhsT=wu[:, kc, half * 128:(half + 1) * 128], rhs=H[:, kc, 0:T],
                            start=(kc == 0), stop=(kc == KC - 1)), [su.r(), xr(H, kc, T)], [pu.r(0, T)])
                    tb = f % 2
                    act(lambda e, pg=pg, tb=tb, T=T: e.activation(out=tmpB[:, tb, 0:T], in_=pg[:, 0:T], func=AF.Silu),
                        [pg.r(0, T)], [tmpB.r(tb * TT, tb * TT + T)])
                    dve(lambda e, pu=pu, tb=tb, f=f, T=T: e.tensor_tensor(out=ACTB[:, f, 0:T], in0=pu[:, 0:T], in1=tmpB[:, tb, 0:T], op=ALU.mult),
                        [pu.r(0, T), tmpB.r(tb * TT, tb * TT + T)], [xr(ACTB, f, T)])
        wd = di["w_ffn_down"][l]
        HF = FC // 2
        for m in range(KC):
            pump()
            pump()
            for c in cs:
                T = c.T
                pd = PS[4] if m % 2 == 0 else PS[7]
                slots = []
                for hf in range(2):
                    src = wd[hf * HF * 128:(hf + 1) * HF * 128, m * 128:(m + 1) * 128].rearrange("(fc p) c -> p fc c", p=128)
                    sd, wv = wblock("ffd%d" % l, 2 * KC, 2 * m + hf, src, HF * 128, [HF, 128])
                    for fc in range(HF):
                        f = hf * HF + fc
                        pe(lambda e, wv=wv, fc=fc, f=f, pd=pd, T=T: e.matmul(
                            pd[:, 0:T], lhsT=wv[:, fc, :], rhs=ACTB[:, f, 0:T], start=(f == 0), stop=(f == FC - 1)),
                           [sd.r(), xr(ACTB, f, T)], [pd.r(0, T)])
                act(lambda e, pd=pd, m=m, T=T: e.copy(out=O[:, m, 0:T], in_=pd[:, 0:T]), [pd.r(0, T)], [xr(O, m, T)])

    ZP = P.sb("zp", [128, KC, 4], F32)
    wconv = P.sb("wconv", [128, KC, 3], F32)

    def conv_mixer(cs, l):
        w2 = di["wB_in"][0]
        CW = MAIN.sub("conv_w", SCR0, [128, 8, TT + 8], F32)
        for mb in range(D // WB):
            s0, w0 = colblock("wBin", w2, 3 * D, mb * WB)
            s1, w1 = colblock("wBin", w2, 3 * D, D + mb * WB)
            s2, w2v = colblock("wBin", w2, 3 * D, 2 * D + mb * WB)
            for half in range(2):
                m = 2 * mb + half
                for c in cs:
                    T = c.T
                    pgb, pgc, pu = PS[0 + (m % 2) * 3], PS[1 + (m % 2) * 3], PS[2 + (m % 2) * 3]
                    for (pp, wv, sl) in ((pgb, w0, s0), (pgc, w1, s1), (pu, w2v, s2)):
                        for kc in range(KC):
                            pe(lambda e, pp=pp, wv=wv, kc=kc, half=half, T=T: e.matmul(
                                pp[:, 0:T], lhsT=wv[:, kc, half * 128:(half + 1) * 128], rhs=H[:, kc, 0:T],
                                start=(kc == 0), stop=(kc == KC - 1)), [sl.r(), xr(H, kc, T)], [pp.r(0, T)])
                    w = m % 2
                    zt = CW[:, w * 4 + 0, :]
                    gct = CW[:, w * 4 + 1, :]
                    cv = CW[:, w * 4 + 2, :]
                    base = (w * 4) * (TT + 8)
                    zr = CW.r(base, base + TT + 8)
                    gr = CW.r(base + (TT + 8), base + 2 * (TT + 8))
                    cr = CW.r(base + 2 * (TT + 8), base + 3 * (TT + 8))
                    dve(lambda e, zt=zt, m=m: e.tensor_copy(out=zt[:, 0:2], in_=ZP[:, m, 0:2]), [ZP.r(m * 4, m * 4 + 2)], [zr])
                    act(lambda e, gct=gct, pgc=pgc, T=T: e.copy(out=gct[:, 0:T], in_=pgc[:, 0:T]), [pgc.r(0, T)], [gr])
                    dve(lambda e, zt=zt, gct=gct, pu=pu, T=T: e.tensor_tensor(out=zt[:, 2:2 + T], in0=pu[:, 0:T], in1=gct[:, 0:T], op=ALU.mult),
                        [pu.r(0, T), gr], [zr])
                    dve(lambda e, zt=zt, m=m, T=T: e.tensor_copy(out=ZP[:, m, 0:2], in_=zt[:, T:T + 2]), [zr], [ZP.r(m * 4, m * 4 + 2)])
                    act(lambda e, zt=zt, cv=cv, m=m, T=T: e.activation(out=cv[:, 0:T], in_=zt[:, 0:T], func=AF.Identity, scale=wconv[:, m, 0:1]),
                        [zr, wconv.r()], [cr])
                    dve(lambda e, zt=zt, cv=cv, m=m, T=T: e.scalar_tensor_tensor(out=cv[:, 0:T], in0=zt[:, 1:1 + T], scalar=wconv[:, m, 1:2], in1=cv[:, 0:T], op0=ALU.mult, op1=ALU.add),
                        [zr, wconv.r(), cr], [cr])
                    dve(lambda e, zt=zt, cv=cv, m=m, T=T: e.scalar_tensor_tensor(out=cv[:, 0:T], in0=zt[:, 2:2 + T], scalar=wconv[:, m, 2:3], in1=cv[:, 0:T], op0=ALU.mult, op1=ALU.add),
                        [zr, wconv.r(), cr], [cr])
                    dve(lambda e, cv=cv, pgb=pgb, m=m, T=T: e.tensor_tensor(out=CG[:, m, 0:T], in0=pgb[:, 0:T], in1=cv[:, 0:T], op=ALU.mult),
                        [pgb.r(0, T), cr], [xr(CG, m, T)])
        outproj(cs, "wBout", di["wB_out"][0], CG)

    CG = MAIN.sub("conv_g", SCR0 + 8 * (TT + 8) * 4, [128, KC, TT], BF16)

    def outproj(cs, name, w2d, src):
        for mb in range(D // WB):
            sl, wv = colblock(name, w2d, D, mb * WB)
            for half in range(2):
                m = 2 * mb + half
                for c in cs:
                    T = c.T
                    pd = PS[4] if m % 2 == 0 else PS[7]
                    for kc in range(KC):
                        pe(lambda e, wv=wv, kc=kc, half=half, pd=pd, T=T: e.matmul(
                            pd[:, 0:T], lhsT=wv[:, kc, half * 128:(half + 1) * 128], rhs=src[:, kc, 0:T],
                            start=(kc == 0), stop=(kc == KC - 1)), [sl.r(), xr(src, kc, T)], [pd.r(0, T)])
                    act(lambda e, pd=pd, m=m, T=T: e.copy(out=O[:, m, 0:T], in_=pd[:, 0:T]), [pd.r(0, T)], [xr(O, m, T)])


    stC = [nc.dram_tensor("stC%d" % j, [128, 8 * DV], F32, kind="Internal").ap() for j in range(2)]
    stCt = [P.dram("stCt%d" % j, 1) for j in range(2)]
    NCOL = [P.sb("ncol%d" % j, [128, 8], F32) for j in range(2)]
    M0 = [P.sb("m0_%d" % j, [128, 1], F32) for j in range(2)]
    BG = P.sb("bg", [128, 4], F32)
    rowsD = nc.dram_tensor("rowsD", [3, 4, TT], F32, kind="Internal").ap()
    rowsT = P.dram("rowsT", 3)
    m0D = nc.dram_tensor("m0D", [1, 4], F32, kind="Internal").ap()
    m0T = P.dram("m0T", 1)
    GHN = P.sb("ghn", [128, 2, KC], F32)
    SM = P.sb("sm", [128, 48], F32)

    def mlstm_mixer(c, l, j, first, last):
        T, L, NB = c.T, c.L, c.NB
        w = di["wA_in"][j]
        A0 = SCR0
        QFM = MAIN.sub("qfm", A0, [128, 8, TT], BF16)
        KTM = MAIN.sub("ktm", A0 + 8192, [128, 4, 1024], BF16)
        VTM = MAIN.sub("vtm", A0 + 16384, [128, 4, 2048], BF16)
        LI = MAIN.sub("li", A0 + 32768, [128, TT], F32)
        BP = MAIN.sub("bp", A0 + 34816, [128, TT], F32)
        GG = MAIN.sub("gg", A0 + 36864, [128, TT], F32)
        NM = MAIN.sub("nm", A0 + 38912, [128, TT], F32)
        GB = MAIN.sub("gb", A0 + 40960, [128, 4, 128], F32)
        NML = MAIN.sub("nml", A0 + 43008, [128, 4, 128], F32)
        WT = MAIN.sub("wt", A0 + 45056, [128, 4, 128], F32)
        PT = MAIN.sub("pt", A0 + 47104, [128, 4, 128], BF16)
        IWB = MAIN.sub("iwb", A0 + 48128, [128, 4, 128], F32)
        QP = MAIN.sub("qp", A0 + 50176, [128, 8, 128], BF16)
        KFC = MAIN.sub("kfc", A0 + 52224, [128, 8, 128], BF16)
        KW = MAIN.sub("kw", A0 + 52224, [128, 1024], BF16)
        NREP = MAIN.sub("nrep", A0 + 54272 - 2048 + 2048, [128, 8, 64], BF16) if False else None
        if first:
            if c.seq == 0:
                pool(lambda e: e.memset(C32[:, :, :], 0.0), [], [C32.r()])
                pool(lambda e: e.memset(NCOL[j][:, :], 0.0), [], [NCOL[j].r()])
                pool(lambda e: e.memset(M0[j][:, :], 0.0), [], [M0[j].r()])
            else:
                dma("sp", C32[:, :, :], di["state_mlstm_C"][j].rearrange("h (dc p) v -> p (h dc) v", p=128), writes=[C32.r()])
                for blk in range(8):
                    dma("sp", NCOL[j][:, blk:blk + 1], di["state_mlstm_n"][j, blk // 2, (blk % 2) * 128:(blk % 2 + 1) * 128].rearrange("(p o) -> p o", o=1),
                        writes=[NCOL[j].r()], nonc=True)
                dma("sp", M0[j][0:4, 0:1], di["state_mlstm_m"][j, :].rearrange("(p o) -> p o", o=1), writes=[M0[j].r()], nonc=True)
        else:
            dma("sp", C32[:, :, :].rearrange("p a b -> p (a b)"), stC[j], reads=[stCt[j].r()], writes=[C32.r()])
        act(lambda e: e.copy(out=CBF[:, :, :], in_=C32[:, :, :]), [C32.r()], [CBF.r()])
        for r_ in range(2):
            dma("sp", BG[0:4, r_:r_ + 1], di["bA_gates"][j, r_ * 4:(r_ + 1) * 4].rearrange("(p o) -> p o", o=1), writes=[BG.r()], nonc=True)
        dve(lambda e: e.tensor_scalar(out=BG[0:4, 2:3], in0=BG[0:4, 1:2], scalar1=-1.0, scalar2=0.0, op0=ALU.mult, op1=ALU.add), [BG.r()], [BG.r()])
        dma("sp", GHN[:, j, :], di["gA_hnorm"][j, :].rearrange("(kc p) -> p kc", p=128), writes=[GHN.r()], nonc=True)
        for hb in range(4):
            sl, wv = colblock("wAin%d" % j, w, A_IN, hb * WB)
            for dc in range(2):
                bank = PS[dc]
                for kc in range(KC):
                    pe(lambda e, wv=wv, kc=kc, dc=dc, bank=bank: e.matmul(bank[:, 0:T], lhsT=wv[:, kc, dc * 128:(dc + 1) * 128], rhs=H[:, kc, 0:T],
                                                                          start=(kc == 0), stop=(kc == KC - 1)), [sl.r(), xr(H, kc, T)], [bank.r(0, T)])
                blk = hb * 2 + dc
                act(lambda e, bank=bank, blk=blk: e.activation(out=QFM[:, blk, 0:T], in_=bank[:, 0:T], func=AF.Identity, scale=DQK ** -0.5),
                    [bank.r(0, T)], [xr(QFM, blk, T)])
        for (dst, c0, nb_, wid) in ((KTM, 1024, 4, 1024), (VTM, 2048, 8, 2048)):
            for bb in range(nb_):
                sl, wv = colblock("wAin%d" % j, w, A_IN, c0 + bb * WB)
                for tb in range(NB):
                    bank = PS[(bb * NB + tb) % 4]
                    for kc in range(KC):
                        pe(lambda e, wv=wv, kc=kc, tb=tb, bank=bank: e.matmul(bank[0:L, 0:WB], lhsT=H[:, kc, tb * L:(tb + 1) * L], rhs=wv[:, kc, :],
                                                                              start=(kc == 0), stop=(kc == KC - 1)), [sl.r(), xr(H, kc, T)], [bank.r(0, WB)])
                    o_ = dst[0:L, tb, bb * WB:(bb + 1) * WB]
                    act(lambda e, bank=bank, o_=o_: e.copy(out=o_, in_=bank[0:L, 0:WB]), [bank.r(0, WB)],
                        [dst.r(tb * wid + bb * WB, tb * wid + (bb + 1) * WB)])
        srcg = w[:, 6144:6152].rearrange("(kc p) c -> p kc c", p=128)
        slg, wg_ = wblock("wAg%d" % j, 1, 0, srcg, KC * 8, [KC, 8])
        for gi in range(2):
            bank = PS[4 + gi * 3]
            for kc in range(KC):
                pe(lambda e, kc=kc, gi=gi, bank=bank: e.matmul(bank[0:4, 0:T], lhsT=wg_[:, kc, gi * 4:(gi + 1) * 4], rhs=H[:, kc, 0:T],
                                                                start=(kc == 0), stop=(kc == KC - 1)), [slg.r(), xr(H, kc, T)], [bank.r(0, T)])
        act(lambda e: e.activation(out=LI[0:4, 0:T], in_=PS[4][0:4, 0:T], func=AF.Identity, bias=BG[0:4, 0:1], scale=1.0), [PS[4].r(0, T), BG.r()], [LI.r()])
        act(lambda e: e.activation(out=NM[0:4, 0:T], in_=PS[7][0:4, 0:T], func=AF.Exp, bias=BG[0:4, 2:3], scale=-1.0), [PS[7].r(0, T), BG.r()], [NM.r()])
        act(lambda e: e.activation(out=GG[0:4, 0:T], in_=NM[0:4, 0:T], func=AF.Ln, bias=1.0, scale=1.0), [NM.r()], [GG.r()])
        dve(lambda e: e.tensor_tensor_scan(out=BP[0:4, 0:T], data0=ones_f[0:4, 0:T], data1=GG[0:4, 0:T], initial=0.0, op0=ALU.mult, op1=ALU.add),
            [ones_f.r(), GG.r()], [BP.r()])
        dve(lambda e: e.tensor_tensor(out=LI[0:4, 0:T], in0=LI[0:4, 0:T], in1=BP[0:4, 0:T], op=ALU.add), [LI.r(), BP.r()], [LI.r()])
        dve(lambda e: e.tensor_tensor_scan(out=GG[0:4, 0:T], data0=LI[0:4, 0:T], data1=LI[0:4, 0:T], initial=M0[j][0:4, 0:1], op0=ALU.max, op1=ALU.max),
            [LI.r(), M0[j].r()], [GG.r()])
        dve(lambda e: e.tensor_tensor(out=NM[0:4, 0:T], in0=BP[0:4, 0:T], in1=GG[0:4, 0:T], op=ALU.subtract), [BP.r(), GG.r()], [NM.r()])
        dma("sp", m0D[0, :].rearrange("(p o) -> p o", o=1), M0[j][0:4, 0:1], reads=[M0[j].r()], writes=[m0T.r()], nonc=True)
        dma("sp", SM[:, 12:16], m0D[0:1, :].partition_broadcast(128), reads=[m0T.r()], writes=[SM.r(12, 16)], nonc=True)
        for qi, rb in enumerate((GG, NM, LI)):
            dma("sp", rowsD[qi, :, 0:T], rb[0:4, 0:T], reads=[rb.r()], writes=[rowsT.r(qi, qi + 1)])
        dve(lambda e: e.tensor_tensor(out=M0[j][0:4, 0:1], in0=GG[0:4, T - 1:T], in1=BP[0:4, T - 1:T], op=ALU.subtract), [GG.r(), BP.r()], [M0[j].r()])
        NREPt = P_nrep
        dve(lambda e: e.tensor_tensor(out=NREPt[:, :, :], in0=ones_f[:, 0:1024].rearrange("p (a b) -> p a b", a=8) if False else ones8,
                                      in1=NCOL[j][:, :].unsqueeze(2).to_broadcast([128, 8, 128]), op=ALU.mult), [ones_f.r(), NCOL[j].r()], [NREPt.r()])
        def do_chunk(cc):
            t0 = cc * L
            dma("sp", GB[:, :, 0:L], rowsD[0:1, :, t0:t0 + L].partition_broadcast(128), reads=[rowsT.r(0, 1)], writes=[GB.r()], nonc=True)
            dma("sp", NML[:, :, 0:L], rowsD[1:2, :, t0:t0 + L].partition_broadcast(128), reads=[rowsT.r(1, 2)], writes=[NML.r()], nonc=True)
            dma("sp", SM[0:L, 0:4], rowsD[2, :, t0:t0 + L].rearrange("h s -> s h"), reads=[rowsT.r(2, 3)], writes=[SM.r(0, 4)], nonc=True)
            for half in range(2):
                bank = PS[half]
                for q in range(4):
                    blk = half * 4 + q
                    pe(lambda e, blk=blk, q=q, bank=bank: e.matmul(bank[:, q * 128:q * 128 + L], lhsT=KTM[0:L, cc, blk * 128:(blk + 1) * 128], rhs=ident[0:L, 0:L],
                                                                    start=True, stop=True), [KTM.r(cc * 1024, (cc + 1) * 1024), ident.r()], [bank.r(q * 128, q * 128 + L)])
                src = bank.t[:, :].rearrange("p (q t) -> p q t", q=4)[:, :, 0:L]
                act(lambda e, src=src, half=half: e.copy(out=KFC[:, half * 4:(half + 1) * 4, 0:L], in_=src), [bank.r()], [KFC.r(half * 512, (half + 1) * 512)])
            for h in range(4):
                for dc in range(2):
                    blk = 2 * h + dc
                    pe(lambda e, h=h, dc=dc, blk=blk: e.matmul(PS[2][0:L, h * 128:h * 128 + L], lhsT=KFC[:, blk, 0:L], rhs=QFM[:, blk, t0:t0 + L],
                                                                start=(dc == 0), stop=(dc == 1)), [KFC.r(), xr(QFM, blk, T)], [PS[2].r(h * 128, h * 128 + L)])
            mb_ = maskbig[0:L, 0:L].unsqueeze(1).to_broadcast([L, 4, L])
            pool(lambda e, mb_=mb_: e.tensor_tensor(out=WT[0:L, :, 0:L], in0=GB[0:L, :, 0:L], in1=mb_, op=ALU.add), [GB.r(), maskbig.r()], [WT.r()])
            for h in range(4):
                act(lambda e, h=h: e.activation(out=WT[0:L, h, 0:L], in_=WT[0:L, h, 0:L], func=AF.Exp, bias=SM[0:L, h:h + 1], scale=-1.0),
                    [WT.r(), SM.r(0, 4)], [WT.r()])
            ps_s = PS[2].t[:, :].rearrange("p (h t) -> p h t", h=4)[0:L, :, 0:L]
            dve(lambda e, ps_s=ps_s: e.tensor_tensor(out=PT[0:L, :, 0:L], in0=ps_s, in1=WT[0:L, :, 0:L], op=ALU.mult), [PS[2].r(), WT.r()], [PT.r()])
            for h in range(4):
                act(lambda e, h=h: e.activation(out=IWB[:, h, 0:L], in_=GB[:, h, 0:L], func=AF.Exp, bias=SM[:, 12 + h:13 + h], scale=-1.0),
                    [GB.r(), SM.r(12, 16)], [IWB.r()])
            act(lambda e: e.activation(out=NML[:, :, 0:L], in_=NML[:, :, 0:L], func=AF.Exp), [NML.r()], [NML.r()])
            qv = QFM.t.rearrange("p (h d) t -> p h d t", d=2)
            qpv = QP.t.rearrange("p (h d) t -> p h d t", d=2)
            for dc in range(2):
                dve(lambda e, dc=dc: e.tensor_tensor(out=qpv[:, :, dc, 0:L], in0=qv[:, :, dc, t0:t0 + L], in1=IWB[:, :, 0:L], op=ALU.mult),
                    [QFM.r(), IWB.r()], [QP.r()])
            for h in range(4):
                pe(lambda e, h=h: e.matmul(PS[3][:, h * 128:h * 128 + L], lhsT=ones_bf[0:L, :], rhs=PT[0:L, h, 0:L], start=True, stop=False),
                   [ones_bf.r(), PT.r()], [PS[3].r(h * 128, h * 128 + L)])
                for dc in range(2):
                    blk = 2 * h + dc
                    pe(lambda e, h=h, dc=dc, blk=blk: e.matmul(PS[3][:, h * 128:h * 128 + L], lhsT=NREPt[:, blk, :], rhs=QP[:, blk, 0:L], start=False, stop=(dc == 1)),
                       [NREPt.r(), QP.r()], [PS[3].r(h * 128, h * 128 + L)])
            FL = WT
            pd_ = PS[3].t[:, :].rearrange("p (h t) -> p h t", h=4)[:, :, 0:L]
            act(lambda e, pd_=pd_: e.copy(out=FL[:, :, 0:L], in_=pd_), [PS[3].r(), PT.r()], [FL.r()])
            dve(lambda e: e.scalar_tensor_tensor(out=FL[:, :, 0:L], in0=FL[:, :, 0:L], scalar=-1.0, in1=FL[:, :, 0:L], op0=ALU.mult, op1=ALU.max), [FL.r()], [FL.r()])
            dve(lambda e: e.tensor_tensor(out=FL[:, :, 0:L], in0=FL[:, :, 0:L], in1=NML[:, :, 0:L], op=ALU.max), [FL.r(), NML.r()], [FL.r()])
            dve(lambda e: e.reciprocal(out=FL[:, :, 0:L], in_=FL[:, :, 0:L]), [FL.r()], [FL.r()])
            for h in range(4):
                bank = PS[4] if h % 2 == 0 else PS[7]
                for vc in range(4):
                    pe(lambda e, h=h, vc=vc, bank=bank: e.matmul(bank[:, vc * 128:vc * 128 + L], lhsT=VTM[0:L, cc, h * 512 + vc * 128:h * 512 + (vc + 1) * 128],
                                                                  rhs=PT[0:L, h, 0:L], start=True, stop=False),
                       [VTM.r(cc * 2048, (cc + 1) * 2048), PT.r()], [bank.r(vc * 128, vc * 128 + L)])
                    for dc in range(2):
                        blk = 2 * h + dc
                        pe(lambda e, h=h, vc=vc, dc=dc, blk=blk, bank=bank: e.matmul(bank[:, vc * 128:vc * 128 + L], lhsT=CBF[:, blk, vc * 128:(vc + 1) * 128],
                                                                                      rhs=QP[:, blk, 0:L], start=False, stop=(dc == 1)),
                           [CBF.r(), QP.r()], [bank.r(vc * 128, vc * 128 + L)])
                pn_ = bank.t[:, :].rearrange("p (v t) -> p v t", v=4)[:, :, 0:L]
                rf_ = FL[:, h, 0:L].unsqueeze(1).to_broadcast([128, 4, L])
                dve(lambda e, pn_=pn_, rf_=rf_, h=h: e.tensor_tensor(out=O[:, 4 * h:4 * h + 4, t0:t0 + L], in0=pn_, in1=rf_, op=ALU.mult),
                    [bank.r(), FL.r()], [xr(O, 4 * h, T, 4)])
            dve(lambda e: e.tensor_scalar(out=SM[:, 8:12], in0=GB[:, :, L - 1], scalar1=-1.0, scalar2=0.0, op0=ALU.mult, op1=ALU.add), [GB.r()], [SM.r(8, 12)])
            for h in range(4):
                act(lambda e, h=h: e.activation(out=SM[0:L, 4 + h:5 + h], in_=SM[0:L, h:h + 1], func=AF.Exp, bias=SM[0:L, 8 + h:9 + h], scale=1.0),
                    [SM.r(0, 4), SM.r(8, 12)], [SM.r(4, 8)])
            for h in range(4):
                act(lambda e, h=h: e.activation(out=KW[0:L, h * 256:(h + 1) * 256], in_=KTM[0:L, cc, h * 256:(h + 1) * 256], func=AF.Identity, scale=SM[0:L, 4 + h:5 + h]),
                    [KTM.r(cc * 1024, (cc + 1) * 1024), SM.r(4, 8), KFC.r()], [KW.r()])
            d8 = SM[:, 16:24].rearrange("p (h d) -> p h d", d=2)
            for dc in range(2):
                dve(lambda e, dc=dc, d8=d8: e.tensor_copy(out=d8[:, :, dc], in_=IWB[:, :, L - 1]), [IWB.r()], [SM.r(16, 24)])
            dve(lambda e: e.tensor_copy(out=SM[:, 12:16], in_=GB[:, :, L - 1]), [GB.r()], [SM.r(12, 16)])
            for blk in range(8):
                pe(lambda e, blk=blk: e.matmul(PS[6][:, blk:blk + 1], lhsT=KW[0:L, blk * 128:(blk + 1) * 128], rhs=ones_bf[0:L, 0:1], start=True, stop=True),
                   [KW.r(), ones_bf.r()], [PS[6].r(blk, blk + 1)])
            dve(lambda e: e.tensor_tensor(out=NCOL[j][:, :], in0=NCOL[j][:, :], in1=SM[:, 16:24], op=ALU.mult), [NCOL[j].r(), SM.r(16, 24)], [NCOL[j].r()])
            dve(lambda e: e.tensor_tensor(out=NCOL[j][:, :], in0=NCOL[j][:, :], in1=PS[6][:, 0:8], op=ALU.add), [NCOL[j].r(), PS[6].r(0, 8)], [NCOL[j].r()])
            dve(lambda e: e.tensor_tensor(out=NREPt[:, :, :], in0=ones8, in1=NCOL[j][:, :].unsqueeze(2).to_broadcast([128, 8, 128]), op=ALU.mult),
                [ones_f.r(), NCOL[j].r()], [NREPt.r()])
            for blk in range(8):
                h = blk // 2
                pc_ = PS[blk % 2]
                pe(lambda e, blk=blk, h=h, pc_=pc_: e.matmul(pc_[:, 0:DV], lhsT=KW[0:L, blk * 128:(blk + 1) * 128], rhs=VTM[0:L, cc, h * 512:(h + 1) * 512], start=True, stop=True),
                   [KW.r(), VTM.r(cc * 2048, (cc + 1) * 2048)], [pc_.r()])
                dve(lambda e, blk=blk, pc_=pc_: e.scalar_tensor_tensor(out=C32[:, blk, :], in0=C32[:, blk, :], scalar=SM[:, 16 + blk:17 + blk], in1=pc_[:, 0:DV], op0=ALU.mult, op1=ALU.add),
                    [C32.r(blk * DV, (blk + 1) * DV), SM.r(16, 24), pc_.r()], [C32.r(blk * DV, (blk + 1) * DV)])
                act(lambda e, blk=blk: e.copy(out=CBF[:, blk, :], in_=C32[:, blk, :]), [C32.r(blk * DV, (blk + 1) * DV)], [CBF.r(blk * DV, (blk + 1) * DV)])
        for cc_ in range(NB):
            do_chunk(cc_)
        if last:
            pre = "p" if c.seq == 0 else "s"
            dma("sp", do[pre + "_C"][j].rearrange("h (dc p) v -> p (h dc) v", p=128), C32[:, :, :], reads=[C32.r()])
            for blk in range(8):
                dma("sp", do[pre + "_n"][j, blk // 2, (blk % 2) * 128:(blk % 2 + 1) * 128].rearrange("(p o) -> p o", o=1), NCOL[j][:, blk:blk + 1],
                    reads=[NCOL[j].r()], nonc=True)
            dma("sp", do[pre + "_m"][j, :].rearrange("(p o) -> p o", o=1), M0[j][0:4, 0:1], reads=[M0[j].r()], nonc=True)
        else:
            dma("sp", stC[j], C32[:, :, :].rearrange("p a b -> p (a b)"), reads=[C32.r()], writes=[stCt[j].r()])
        HG = MAIN.sub("hg", A0, [128, KC, TT], BF16)
        SQ = MAIN.sub("sq", A0 + 16384, [128, 4, TT], BF16)
        RSH = MAIN.sub("rsh", A0 + 20480, [128, TT], F32)
        OS = MAIN.sub("os", A0 + 22528, [128, 2, TT], F32)
        T1 = MAIN.sub("t1", A0 + 26624, [128, 2, TT], F32)
        for h in range(4):
            for vc in range(4):
                m = 4 * h + vc
                act(lambda e, m=m, vc=vc: e.activation(out=SQ[:, vc, 0:T], in_=O[:, m, 0:T], func=AF.Square), [xr(O, m, T)], [xr(SQ, vc, T)])
                pe(lambda e, vc=vc: e.matmul(PS[5][:, 0:T], lhsT=ones_bf[:, :], rhs=SQ[:, vc, 0:T], start=(vc == 0), stop=(vc == 3)),
                   [ones_bf.r(), xr(SQ, vc, T)], [PS[5].r(0, T)])
            act(lambda e: e.activation(out=RSH[:, 0:T], in_=PS[5][:, 0:T], func=AF.Sqrt, bias=epsb[:, 0:1], scale=1.0 / DV), [PS[5].r(0, T), epsb.r()], [RSH.r()])
            dve(lambda e: e.reciprocal(out=RSH[:, 0:T], in_=RSH[:, 0:T]), [RSH.r()], [RSH.r()])
            for ob in range(2):
                sl, wv = colblock("wAin%d" % j, w, A_IN, 4096 + (2 * h + ob) * WB)
                for half in range(2):
                    m = 4 * h + ob * 2 + half
                    bank = PS[m % 2]
                    for kc in range(KC):
                        pe(lambda e, wv=wv, kc=kc, half=half, bank=bank: e.matmul(bank[:, 0:T], lhsT=wv[:, kc, half * 128:(half + 1) * 128], rhs=H[:, kc, 0:T],
                                                                                  start=(kc == 0), stop=(kc == KC - 1)), [sl.r(), xr(H, kc, T)], [bank.r(0, T)])
                    tb = m % 2
                    act(lambda e, bank=bank, tb=tb: e.activation(out=OS[:, tb, 0:T], in_=bank[:, 0:T], func=AF.Sigmoid), [bank.r(0, T)], [xr(OS, tb, T)])
                    dve(lambda e, m=m, tb=tb: e.scalar_tensor_tensor(out=T1[:, tb, 0:T], in0=O[:, m, 0:T], scalar=GHN[:, j, m:m + 1], in1=RSH[:, 0:T], op0=ALU.mult, op1=ALU.mult),
                        [xr(O, m, T), GHN.r(), RSH.r()], [xr(T1, tb, T)])
                    dve(lambda e, m=m, tb=tb: e.tensor_tensor(out=HG[:, m, 0:T], in0=T1[:, tb, 0:T], in1=OS[:, tb, 0:T], op=ALU.mult),
                        [xr(T1, tb, T), xr(OS, tb, T)], [xr(HG, m, T)])
        outproj([c], "wAout%d" % j, di["wA_out"][j], HG)


    NS5 = 8
    s5MB = nc.dram_tensor("s5MB", [S5G, 128, 128], BF16, kind="Internal").ap()
    s5MBN = nc.dram_tensor("s5MBN", [S5G, 128, 128], BF16, kind="Internal").ap()
    s5MD = nc.dram_tensor("s5MD", [S5G, 128, 128], BF16, kind="Internal").ap()
    s5KT = nc.dram_tensor("s5KT", [S5G, 128, 128], BF16, kind="Internal").ap()
    s5A8 = nc.dram_tensor("s5A8", [2, S5G, S5N], F32, kind="Internal").ap()
    s5T = P.dram("s5T", 5)
    A8S = P.sb("a8s", [128, 2, 64], F32)
    XS = P.sb("xs5", [128, 2, 64], F32)
    DFM = P.sb("dfm", [128, KC], F32)
    ZT = CST.sub("zt", 16384, [128, 8, 240], BF16)

    def s5_generate():
        G0 = 0
        LR = MAIN.sub("g_lr", G0, [128, 64], F32)
        LIm = MAIN.sub("g_li", G0 + 256, [128, 64], F32)
        DT = MAIN.sub("g_dt", G0 + 512, [128, 4], F32)
        AR = MAIN.sub("g_ar", G0 + 1024, [128, 64], F32)
        AI = MAIN.sub("g_ai", G0 + 1280, [128, 64], F32)
        T1 = MAIN.sub("g_t1", G0 + 1536, [128, 64], F32)
        T2 = MAIN.sub("g_t2", G0 + 1792, [128, 64], F32)
        T3 = MAIN.sub("g_t3", G0 + 2048, [128, 64], F32)
        FR = MAIN.sub("g_fr", G0 + 2304, [128, 64], F32)
        FI = MAIN.sub("g_fi", G0 + 2560, [128, 64], F32)
        PW = MAIN.sub("g_pw", G0 + 3072, [128, 2, 9, 64], F32)
        NP_ = MAIN.sub("g_np", G0 + 7680, [128, 2, 9, 64], F32)
        BR = MAIN.sub("g_br", G0 + 12288, [128, 64, 16], F32)
        BI = MAIN.sub("g_bi", G0 + 16384, [128, 64, 16], F32)
        BBR = MAIN.sub("g_bbr", G0 + 20480, [128, 64, 16], F32)
        BBI = MAIN.sub("g_bbi", G0 + 24576, [128, 64, 16], F32)
        CR = MAIN.sub("g_cr", G0 + 28672, [128, 16, 64], F32)
        CI = MAIN.sub("g_ci", G0 + 32768, [128, 16, 64], F32)
        W1 = MAIN.sub("g_w1", G0 + 36864, [128, 1024], F32)
        W2 = MAIN.sub("g_w2", G0 + 40960, [128, 1024], F32)
        EO = MAIN.sub("g_eo", G0 + 45056, [128, 128, 128], BF16)
        g2 = lambda ap: ap
        dma("sp", LR[:, :], di["s5_A_re"][0], writes=[LR.r()])
        dma("sp", LIm[:, :], di["s5_A_im"][0], writes=[LIm.r()])
        dma("sp", DT[:, 0:1], di["s5_log_dt"][0, :].rearrange("(p o) -> p o", o=1), writes=[DT.r()], nonc=True)
        dma("sp", BR[:, :, :], di["s5_B_re"][0], writes=[BR.r()])
        dma("sp", BI[:, :, :], di["s5_B_im"][0], writes=[BI.r()])
        dma("sp", CR[:, :, :], di["s5_C_re"][0], writes=[CR.r()])
        dma("sp", CI[:, :, :], di["s5_C_im"][0], writes=[CI.r()])
        dma("sp", DFM[:, :], di["s5_D"][0].rearrange("(j gl) p -> gl p j", gl=8), writes=[DFM.r()], nonc=True) if False else None
        for gl in range(8):
            dma("sp", DFM[gl * 16:(gl + 1) * 16, :], di["s5_D"][0].rearrange("(j gl) p -> gl p j", gl=8)[gl], writes=[DFM.r()], nonc=True)
        act(lambda e: e.activation(out=DT[:, 1:2], in_=DT[:, 0:1], func=AF.Exp), [DT.r()], [DT.r()])
        dve(lambda e: e.tensor_scalar(out=DT[:, 2:3], in0=DT[:, 1:2], scalar1=1.0 / 16, scalar2=0.0, op0=ALU.mult, op1=ALU.add), [DT.r()], [DT.r()])
        act(lambda e: e.activation(out=T1[:, :], in_=LR[:, :], func=AF.Exp, scale=DT[:, 2:3]), [LR.r(), DT.r()], [T1.r()])
        act(lambda e: e.activation(out=T2[:, :], in_=LIm[:, :], func=AF.Sin, scale=DT[:, 2:3]), [LIm.r(), DT.r()], [T2.r()])
        dve(lambda e: e.tensor_scalar(out=T3[:, :], in0=LIm[:, :], scalar1=DT[:, 2:3], scalar2=math.pi / 2, op0=ALU.mult, op1=ALU.add), [LIm.r(), DT.r()], [T3.r()])
        act(lambda e: e.activation(out=T3[:, :], in_=T3[:, :], func=AF.Sin), [T3.r()], [T3.r()])
        dve(lambda e: e.tensor_tensor(out=AR[:, :], in0=T1[:, :], in1=T3[:, :], op=ALU.mult), [T1.r(), T3.r()], [AR.r()])
        dve(lambda e: e.tensor_tensor(out=AI[:, :], in0=T1[:, :], in1=T2[:, :], op=ALU.mult), [T1.r(), T2.r()], [AI.r()])

        def cmul(or_, oi_, ar_, ai_, br_, bi_, rr, ww):
            dve(lambda e: e.tensor_tensor(out=T1[:, :], in0=ar_, in1=br_, op=ALU.mult), rr, [T1.r()])
            dve(lambda e: e.tensor_tensor(out=T2[:, :], in0=ai_, in1=bi_, op=ALU.mult), rr, [T2.r()])
            dve(lambda e: e.tensor_tensor(out=T3[:, :], in0=ar_, in1=bi_, op=ALU.mult), rr, [T3.r()])
            dve(lambda e: e.tensor_tensor(out=oi_, in0=ai_, in1=br_, op=ALU.mult), rr, ww)
            dve(lambda e: e.tensor_tensor(out=oi_, in0=oi_, in1=T3[:, :], op=ALU.add), rr + [T3.r()], ww)
            dve(lambda e: e.tensor_tensor(out=or_, in0=T1[:, :], in1=T2[:, :], op=ALU.subtract), [T1.r(), T2.r()], ww)
        for _ in range(4):
            cmul(FR[:, :], FI[:, :], AR[:, :], AI[:, :], AR[:, :], AI[:, :], [AR.r(), AI.r()], [FR.r(), FI.r()])
            dve(lambda e: e.tensor_copy(out=AR[:, :], in_=FR[:, :]), [FR.r()], [AR.r()])
            dve(lambda e: e.tensor_copy(out=AI[:, :], in_=FI[:, :]), [FI.r()], [AI.r()])
        pool(lambda e: e.memset(PW[:, 0, 0, :], 1.0), [], [PW.r()])
        pool(lambda e: e.memset(PW[:, 1, 0, :], 0.0), [], [PW.r()])
        for k in range(1, 9):
            cmul(PW[:, 0, k, :], PW[:, 1, k, :], PW[:, 0, k - 1, :], PW[:, 1, k - 1, :], AR[:, :], AI[:, :], [PW.r(), AR.r(), AI.r()], [PW.r()])
        dve(lambda e: e.tensor_tensor(out=T1[:, :], in0=AR[:, :], in1=AR[:, :], op=ALU.mult), [AR.r()], [T1.r()])
        dve(lambda e: e.tensor_tensor(out=T2[:, :], in0=AI[:, :], in1=AI[:, :], op=ALU.mult), [AI.r()], [T2.r()])
        dve(lambda e: e.tensor_tensor(out=T1[:, :], in0=T1[:, :], in1=T2[:, :], op=ALU.add), [T1.r(), T2.r()], [T1.r()])
        dve(lambda e: e.reciprocal(out=T1[:, :], in_=T1[:, :]), [T1.r()], [T1.r()])
        dve(lambda e: e.tensor_tensor(out=NP_[:, 0, 1, :], in0=AR[:, :], in1=T1[:, :], op=ALU.mult), [AR.r(), T1.r()], [NP_.r()])
        dve(lambda e: e.scalar_tensor_tensor(out=NP_[:, 1, 1, :], in0=AI[:, :], scalar=-1.0, in1=T1[:, :], op0=ALU.mult, op1=ALU.mult), [AI.r(), T1.r()], [NP_.r()])
        dve(lambda e: e.tensor_copy(out=FR[:, :], in_=NP_[:, 0, 1, :]), [NP_.r()], [FR.r()])
        dve(lambda e: e.tensor_copy(out=FI[:, :], in_=NP_[:, 1, 1, :]), [NP_.r()], [FI.r()])
        for k in range(2, 9):
            cmul(NP_[:, 0, k, :], NP_[:, 1, k, :], NP_[:, 0, k - 1, :], NP_[:, 1, k - 1, :], FR[:, :], FI[:, :], [NP_.r(), FR.r(), FI.r()], [NP_.r()])
        dma("sp", s5A8[0], PW[:, 0, 8, :], reads=[PW.r()], writes=[s5T.r(4, 5)])
        dma("sp", s5A8[1], PW[:, 1, 8, :], reads=[PW.r()], writes=[s5T.r(4, 5)])
        for ri in range(2):
            for par in range(2):
                dma("sp", A8S[par * 64:(par + 1) * 64, ri, :], s5A8[ri].rearrange("(gp par) n -> par n gp", par=2)[par],
                    reads=[s5T.r(4, 5)], writes=[A8S.r()], nonc=True)
        dve(lambda e: e.tensor_tensor(out=T1[:, :], in0=LR[:, :], in1=LR[:, :], op=ALU.mult), [LR.r()], [T1.r()])
        dve(lambda e: e.tensor_tensor(out=T2[:, :], in0=LIm[:, :], in1=LIm[:, :], op=ALU.mult), [LIm.r()], [T2.r()])
        dve(lambda e: e.tensor_tensor(out=T1[:, :], in0=T1[:, :], in1=T2[:, :], op=ALU.add), [T1.r(), T2.r()], [T1.r()])
        dve(lambda e: e.reciprocal(out=T1[:, :], in_=T1[:, :]), [T1.r()], [T1.r()])
        dve(lambda e: e.tensor_scalar(out=T2[:, :], in0=AR[:, :], scalar1=1.0, scalar2=-1.0, op0=ALU.mult, op1=ALU.add), [AR.r()], [T2.r()])
        dve(lambda e: e.tensor_tensor(out=FR[:, :], in0=T2[:, :], in1=LR[:, :], op=ALU.mult), [T2.r(), LR.r()], [FR.r()])
        dve(lambda e: e.tensor_tensor(out=T3[:, :], in0=AI[:, :], in1=LIm[:, :], op=ALU.mult), [AI.r(), LIm.r()], [T3.r()])
        dve(lambda e: e.tensor_tensor(out=FR[:, :], in0=FR[:, :], in1=T3[:, :], op=ALU.add), [FR.r(), T3.r()], [FR.r()])
        dve(lambda e: e.tensor_tensor(out=FR[:, :], in0=FR[:, :], in1=T1[:, :], op=ALU.mult), [FR.r(), T1.r()], [FR.r()])
        dve(lambda e: e.tensor_tensor(out=FI[:, :], in0=AI[:, :], in1=LR[:, :], op=ALU.mult), [AI.r(), LR.r()], [FI.r()])
        dve(lambda e: e.tensor_tensor(out=T3[:, :], in0=T2[:, :], in1=LIm[:, :], op=ALU.mult), [T2.r(), LIm.r()], [T3.r()])
        dve(lambda e: e.tensor_tensor(out=FI[:, :], in0=FI[:, :], in1=T3[:, :], op=ALU.subtract), [FI.r(), T3.r()], [FI.r()])
        dve(lambda e: e.tensor_tensor(out=FI[:, :], in0=FI[:, :], in1=T1[:, :], op=ALU.mult), [FI.r(), T1.r()], [FI.r()])

        def bc_np(ap):
            return ap.unsqueeze(2).to_broadcast([128, 64, 16])

        def bc_pn(ap):
            return ap.unsqueeze(1).to_broadcast([128, 16, 64])
        w1n = W1.t.rearrange("p (n q) -> p n q", q=16)
        w2n = W2.t.rearrange("p (n q) -> p n q", q=16)
        w1p = W1.t.rearrange("p (q n) -> p q n", n=64)
        w2p = W2.t.rearrange("p (q n) -> p q n", n=64)

        def cmul_b(or_, oi_, ar_, ai_, br_, bi_, rr, ww, w1, w2, sgn_i=1.0, sgn_r=1.0):
            dve(lambda e: e.tensor_tensor(out=w1, in0=br_, in1=ar_, op=ALU.mult), rr, [W1.r()])
            dve(lambda e: e.tensor_tensor(out=w2, in0=bi_, in1=ai_, op=ALU.mult), rr, [W2.r()])
            if sgn_r > 0:
                dve(lambda e: e.tensor_tensor(out=or_, in0=w1, in1=w2, op=ALU.subtract), [W1.r(), W2.r()], ww)
            else:
                dve(lambda e: e.tensor_tensor(out=or_, in0=w2, in1=w1, op=ALU.subtract), [W1.r(), W2.r()], ww)
            dve(lambda e: e.tensor_tensor(out=w1, in0=bi_, in1=ar_, op=ALU.mult), rr + ww, [W1.r()])
            dve(lambda e: e.tensor_tensor(out=w2, in0=br_, in1=ai_, op=ALU.mult), rr + ww, [W2.r()])
            if sgn_i > 0:
                dve(lambda e: e.tensor_tensor(out=oi_, in0=w1, in1=w2, op=ALU.add), [W1.r(), W2.r()], ww)
            else:
                dve(lambda e: e.scalar_tensor_tensor(out=oi_, in0=w1, scalar=-1.0, in1=w2, op0=ALU.mult, op1=ALU.subtract), [W1.r(), W2.r()], ww)
        cmul_b(BBR[:, :, :], BBI[:, :, :], bc_np(FR[:, :]), bc_np(FI[:, :]), BR[:, :, :], BI[:, :, :], [FR.r(), FI.r(), BR.r(), BI.r()], [BBR.r(), BBI.r()], w1n, w2n)
        eo_sp = EO.t.rearrange("g (s q) (r n) -> g s q r n", q=16, n=64)
        bbr_pn = BBR.t.rearrange("g n q -> g q n")
        bbi_pn = BBI.t.rearrange("g n q -> g q n")
        for s_ in range(8):
            k = 7 - s_
            cmul_b(eo_sp[:, s_, :, 0, :], eo_sp[:, s_, :, 1, :], bc_pn(PW[:, 0, k, :]), bc_pn(PW[:, 1, k, :]), bbr_pn, bbi_pn,
                   [PW.r(), BBR.r(), BBI.r()], [EO.r()], w1p, w2p)
        dma("sp", s5MB.rearrange("g a b -> g (a b)"), EO.t.rearrange("g a b -> g (a b)"), reads=[EO.r()], writes=[s5T.r(0, 1)])
        eo_rn = EO.t.rearrange("g (r n) (s q) -> g r n s q", n=64, q=16)
        for s_ in range(8):
            k = s_ + 1
            cmul_b(eo_rn[:, 0, :, s_, :], eo_rn[:, 1, :, s_, :], bc_np(NP_[:, 0, k, :]), bc_np(NP_[:, 1, k, :]), BBR[:, :, :], BBI[:, :, :],
                   [NP_.r(), BBR.r(), BBI.r()], [EO.r()], w1n, w2n)
        dma("sp", s5MBN.rearrange("g a b -> g (a b)"), EO.t.rearrange("g a b -> g (a b)"), reads=[EO.r()], writes=[s5T.r(1, 2)])
        cr_np = CR.t.rearrange("g q n -> g n q")
        ci_np = CI.t.rearrange("g q n -> g n q")
        for i_ in range(8):
            k = i_ + 1
            cmul_b(eo_rn[:, 0, :, i_, :], eo_rn[:, 1, :, i_, :], bc_np(PW[:, 0, k, :]), bc_np(PW[:, 1, k, :]), cr_np, ci_np,
                   [PW.r(), CR.r(), CI.r()], [EO.r()], w1n, w2n, sgn_i=-1.0)
        dma("sp", s5MD.rearrange("g a b -> g (a b)"), EO.t.rearrange("g a b -> g (a b)"), reads=[EO.r()], writes=[s5T.r(2, 3)])
        KMASK = MAIN.sub("g_kmask", G0 + 512, [128, 128], BF16)
        ES = MAIN.sub("g_es", G0, [128, 8, 16], BF16)
        EI = MAIN.sub("g_ei", G0 + 256, [128, 8, 16], BF16)
        pool(lambda e: e.memset(ES[0:8, :, :], 1.0), [LR.r(), LIm.r(), DT.r(), AR.r()], [ES.r()])
        pool(lambda e: e.memset(EI[0:8, :, :], 1.0), [LR.r(), LIm.r(), DT.r(), AR.r()], [EI.r()])
        pool(lambda e: e.affine_select(out=ES[0:8, :, :], in_=ES[0:8, :, :], pattern=[[1, 8], [0, 16]], compare_op=ALU.is_equal, fill=0.0, base=0, channel_multiplier=-1),
             [ES.r()], [ES.r()])
        pool(lambda e: e.affine_select(out=EI[0:8, :, :], in_=EI[0:8, :, :], pattern=[[1, 8], [0, 16]], compare_op=ALU.is_ge, fill=0.0, base=0, channel_multiplier=-1),
             [EI.r()], [EI.r()])
        pe(lambda e: e.matmul(PS[0][:, 0:128], lhsT=ES[0:8, :, :].rearrange("k a b -> k (a b)"), rhs=EI[0:8, :, :].rearrange("k a b -> k (a b)"), start=True, stop=True),
           [ES.r(), EI.r()], [PS[0].r(0, 128)])
        act(lambda e: e.copy(out=KMASK[:, :], in_=PS[0][:, 0:128]), [PS[0].r(0, 128)], [KMASK.r()])
        LN = MAIN.sub("g_ln", G0 + 1024, [128, 8, 128], BF16)
        LD = MAIN.sub("g_ld", G0 + 3072, [128, 8, 128], BF16)
        KO = MAIN.sub("g_ko", G0 + 5120, [128, 8, 128], BF16)
        for jj in range(16):
            dma("sp", LN[:, :, :], s5MBN[8 * jj:8 * jj + 8].rearrange("g x y -> x g y"), reads=[s5T.r(1, 2)], writes=[LN.r()], nonc=True)
            dma("sp", LD[:, :, :], s5MD[8 * jj:8 * jj + 8].rearrange("g x y -> x g y"), reads=[s5T.r(2, 3)], writes=[LD.r()], nonc=True)
            for hb in range(2):
                bank = PS[1 + hb]
                for q in range(4):
                    gl = hb * 4 + q
                    pe(lambda e, gl=gl, q=q, bank=bank: e.matmul(bank[:, q * 128:(q + 1) * 128], lhsT=LN[:, gl, :], rhs=LD[:, gl, :], start=True, stop=True),
                       [LN.r(), LD.r()], [bank.r(q * 128, (q + 1) * 128)])
                km = KMASK[:, :].unsqueeze(1).to_broadcast([128, 4, 128])
                dve(lambda e, bank=bank, hb=hb, km=km: e.tensor_tensor(out=KO[:, hb * 4:(hb + 1) * 4, :], in0=bank.t[:, :].rearrange("p (q t) -> p q t", q=4), in1=km, op=ALU.mult),
                    [bank.r(), KMASK.r()], [KO.r(hb * 512, (hb + 1) * 512)])
            dma("sp", s5KT[8 * jj:8 * jj + 8].rearrange("g x y -> x g y"), KO[:, :, :], reads=[KO.r()], writes=[s5T.r(3, 4)], nonc=True)

    def s5_mixer(c, first, last):
        T = c.T
        NCK = T // NS5
        A0 = SCR0
        UP = MAIN.sub("s5_up", A0, [128, KC, 8, 64], BF16)
        SR = MAIN.sub("s5_sr", A0 + 16384, [128, 2, 65, 64], F32)
        YP = MAIN.sub("s5_yp", A0 + 49664, [128, 8, 64], BF16)
        XB = MAIN.sub("s5_xb", A0 + 50688, [128, 2, 64, 4], BF16)
        TT1 = MAIN.sub("s5_t1", A0 + 51712, [128, 4, 64], F32)
        YG = CST.sub("s5_yg", 0, [128, KC, TT], BF16)
        GT = P_gelu
        pool(lambda e: e.memset(ZT[:, :, :], 0.0), [], [ZT.r()])
        for a_ in range(8):
            pool(lambda e, a_=a_: e.tensor_copy(out=ZT[:, a_, 112:128], in_=ident[:, a_ * 16:(a_ + 1) * 16]), [ident.r(), ZT.r()], [ZT.r()])
        if first:
            if c.seq == 0:
                pool(lambda e: e.memset(XS[:, :, :], 0.0), [], [XS.r()])
            else:
                for ri, nm in enumerate(("state_s5_re", "state_s5_im")):
                    for par in range(2):
                        dma("sp", XS[par * 64:(par + 1) * 64, ri, :], di[nm][0].rearrange("(gp par) n -> par n gp", par=2)[par], writes=[XS.r()], nonc=True)
        dve(lambda e: e.tensor_copy(out=SR[:, :, 0, :], in_=XS[:, :, :]), [XS.r()], [SR.r()])
        for j in range(KC):
            slot = RING[ring_i[0] % NSLOT]
            ring_i[0] += 1
            mbv = slot.t[:, 0:1024].rearrange("p (g x) -> p g x", g=8)
            dma("sp", mbv, s5MB[8 * j:8 * j + 8].rearrange("g x y -> x g y"), reads=[s5T.r(0, 1)], writes=[slot.r(0, 1024)], nonc=True)
            for gl in range(8):
                for s_ in range(8):
                    pe(lambda e, j=j, gl=gl, s_=s_: e.matmul(PS[0][:, gl * 64:gl * 64 + NCK], lhsT=ZT[:, gl, (7 - s_) * 16:(7 - s_) * 16 + 128], rhs=H[:, j, s_:T:8],
                                                             start=(s_ == 0), stop=(s_ == 7)), [ZT.r(), xr(H, j, T)], [PS[0].r(gl * 64, gl * 64 + NCK)])
            pu = PS[0].t[:, :].rearrange("p (g c) -> p g c", g=8)[:, :, 0:NCK]
            act(lambda e, j=j, pu=pu: e.copy(out=UP[:, j, :, 0:NCK], in_=pu), [PS[0].r()], [UP.r(j * 512, (j + 1) * 512)])
            for gl in range(8):
                par, gpl = gl % 2, gl // 2
                for ri in range(2):
                    pe(lambda e, j=j, gl=gl, par=par, gpl=gpl, ri=ri, mbv=mbv: e.matmul(
                        PS[1 + ri][par * 64:(par + 1) * 64, gpl * 64:gpl * 64 + NCK], lhsT=mbv[:, gl, ri * 64:(ri + 1) * 64], rhs=UP[:, j, gl, 0:NCK], start=True, stop=True),
                       [slot.r(0, 1024), UP.r(j * 512, (j + 1) * 512)], [PS[1 + ri].r(gpl * 64, gpl * 64 + NCK)])
            for ri in range(2):
                src = PS[1 + ri].t[:, 0:256].rearrange("p (g c) -> p c g", g=4)[:, 0:NCK, :]
                dve(lambda e, j=j, ri=ri, src=src: e.tensor_copy(out=SR[:, ri, 1:1 + NCK, 4 * j:4 * j + 4], in_=src), [PS[1 + ri].r(0, 256)], [SR.r()])
        def srr(ri, slot):
            o_ = (ri * 65 + slot) * 64
            return SR.r(o_, o_ + 64)
        for cc in range(NCK):
            xr_, xi_ = SR[:, 0, cc, :], SR[:, 1, cc, :]
            nr_, ni_ = SR[:, 0, cc + 1, :], SR[:, 1, cc + 1, :]
            dve(lambda e, xr_=xr_: e.tensor_tensor(out=TT1[:, 0, :], in0=xr_, in1=A8S[:, 0, :], op=ALU.mult), [srr(0, cc), A8S.r()], [TT1.r(0, 64)])
            dve(lambda e, xi_=xi_: e.tensor_tensor(out=TT1[:, 1, :], in0=xi_, in1=A8S[:, 1, :], op=ALU.mult), [srr(1, cc), A8S.r()], [TT1.r(64, 128)])
            dve(lambda e, xi_=xi_: e.tensor_tensor(out=TT1[:, 2, :], in0=xi_, in1=A8S[:, 0, :], op=ALU.mult), [srr(1, cc), A8S.r()], [TT1.r(128, 192)])
            dve(lambda e, xr_=xr_: e.tensor_tensor(out=TT1[:, 3, :], in0=xr_, in1=A8S[:, 1, :], op=ALU.mult), [srr(0, cc), A8S.r()], [TT1.r(192, 256)])
            dve(lambda e, nr_=nr_: e.tensor_tensor(out=nr_, in0=nr_, in1=TT1[:, 0, :], op=ALU.add), [srr(0, cc + 1), TT1.r(0, 64)], [srr(0, cc + 1)])
            dve(lambda e, nr_=nr_: e.tensor_tensor(out=nr_, in0=nr_, in1=TT1[:, 1, :], op=ALU.subtract), [srr(0, cc + 1), TT1.r(64, 128)], [srr(0, cc + 1)])
            dve(lambda e, ni_=ni_: e.tensor_tensor(out=ni_, in0=ni_, in1=TT1[:, 2, :], op=ALU.add), [srr(1, cc + 1), TT1.r(128, 192)], [srr(1, cc + 1)])
            dve(lambda e, ni_=ni_: e.tensor_tensor(out=ni_, in0=ni_, in1=TT1[:, 3, :], op=ALU.add), [srr(1, cc + 1), TT1.r(192, 256)], [srr(1, cc + 1)])
        dve(lambda e: e.tensor_copy(out=XS[:, :, :], in_=SR[:, :, NCK, :]), [SR.r()], [XS.r()])
        if last:
            pre = "p" if c.seq == 0 else "s"
            for ri, nm in enumerate(("_re", "_im")):
                for par in range(2):
                    dma("sp", do[pre + nm][0].rearrange("(gp par) n -> par n gp", par=2)[par], XS[par * 64:(par + 1) * 64, ri, :], reads=[XS.r()], nonc=True)
        for j in range(KC):
            slot = RING[ring_i[0] % NSLOT]
            ring_i[0] += 1
            ktv = slot.t[:, 0:1024].rearrange("p (g x) -> p g x", g=8)
            mdv = slot.t[:, 1024:2048].rearrange("p (g x) -> p g x", g=8)
            dma("sp", ktv, s5KT[8 * j:8 * j + 8].rearrange("g x y -> x g y"), reads=[s5T.r(3, 4)], writes=[slot.r(0, 1024)], nonc=True)
            mdi = slot.t[:, 2048:3072].rearrange("p (g x) -> p g x", g=8)
            for par in range(2):
                srcg = s5MD[8 * j:8 * j + 8].rearrange("(gp par) x y -> par x gp y", par=2)[par]
                dma("sp", mdv[par * 64:(par + 1) * 64, 0:4, :], srcg[0:64], reads=[s5T.r(2, 3)], writes=[slot.r(1024, 2048)], nonc=True)
                dma("sp", mdi[par * 64:(par + 1) * 64, 0:4, :], srcg[64:128], reads=[s5T.r(2, 3)], writes=[slot.r(2048, 3072)], nonc=True)
            for ri in range(2):
                dve(lambda e, j=j, ri=ri: e.tensor_copy(out=XB[:, ri, 0:NCK, :], in_=SR[:, ri, 0:NCK, 4 * j:4 * j + 4]), [SR.r()], [XB.r(ri * 256, (ri + 1) * 256)])
            for gl in range(8):
                par, gpl = gl % 2, gl // 2
                pe(lambda e, j=j, gl=gl, ktv=ktv: e.matmul(PS[3][:, gl * 64:gl * 64 + NCK], lhsT=ktv[:, gl, :], rhs=UP[:, j, gl, 0:NCK], start=True, stop=False),
                   [slot.r(0, 1024), UP.r(j * 512, (j + 1) * 512)], [PS[3].r(gl * 64, gl * 64 + NCK)])
                pe(lambda e, gl=gl, par=par, gpl=gpl, mdv=mdv: e.matmul(PS[3][:, gl * 64:gl * 64 + NCK], lhsT=mdv[par * 64:(par + 1) * 64, gpl, :],
                                                                       rhs=XB[par * 64:(par + 1) * 64, 0, 0:NCK, gpl], start=False, stop=False),
                   [slot.r(1024, 2048), XB.r()], [PS[3].r(gl * 64, gl * 64 + NCK)])
                pe(lambda e, gl=gl, par=par, gpl=gpl, mdi=mdi: e.matmul(PS[3][:, gl * 64:gl * 64 + NCK], lhsT=mdi[par * 64:(par + 1) * 64, gpl, :],
                                                                       rhs=XB[par * 64:(par + 1) * 64, 1, 0:NCK, gpl], start=False, stop=True),
                   [slot.r(2048, 3072), XB.r()], [PS[3].r(gl * 64, gl * 64 + NCK)])
            py = PS[3].t[:, :].rearrange("p (g c) -> p g c", g=8)[:, :, 0:NCK]
            act(lambda e, py=py: e.copy(out=YP[:, :, 0:NCK], in_=py), [PS[3].r()], [YP.r()])
            bank = PS[4] if j % 2 == 0 else PS[7]
            for i_ in range(8):
                for gl in range(8):
                    pe(lambda e, i_=i_, gl=gl, bank=bank: e.matmul(bank[:, i_ * 64:i_ * 64 + NCK], lhsT=ZT[:, i_, (7 - gl) * 16:(7 - gl) * 16 + 128], rhs=YP[:, gl, 0:NCK],
                                                                    start=(gl == 0), stop=(gl == 7)), [ZT.r(), YP.r()], [bank.r(i_ * 64, i_ * 64 + NCK)])
            po = bank.t[:, :].rearrange("p (i c) -> p c i", i=8)[:, 0:NCK, :]
            hv = H[:, j, 0:T].rearrange("p (c i) -> p c i", i=8)
            g0 = GT[:, 0, 0:T].rearrange("p (c i) -> p c i", i=8)
            dve(lambda e, j=j, po=po, hv=hv, g0=g0: e.scalar_tensor_tensor(out=g0, in0=hv, scalar=DFM[:, j:j + 1], in1=po, op0=ALU.mult, op1=ALU.add),
                [xr(H, j, T), DFM.r(), bank.r()], [GT.r(0, TT)])
            act(lambda e: e.activation(out=GT[:, 1, 0:T], in_=GT[:, 0, 0:T], func=AF.Square), [GT.r(0, TT)], [GT.r(TT, 2 * TT)])
            dve(lambda e: e.tensor_scalar(out=GT[:, 1, 0:T], in0=GT[:, 1, 0:T], scalar1=0.044715, scalar2=1.0, op0=ALU.mult, op1=ALU.add), [GT.r(TT, 2 * TT)], [GT.r(TT, 2 * TT)])
            dve(lambda e: e.tensor_tensor(out=GT[:, 1, 0:T], in0=GT[:, 1, 0:T], in1=GT[:, 0, 0:T], op=ALU.mult), [GT.r(0, 2 * TT)], [GT.r(TT, 2 * TT)])
            act(lambda e: e.activation(out=GT[:, 1, 0:T], in_=GT[:, 1, 0:T], func=AF.Sigmoid, scale=2.0 * math.sqrt(2.0 / math.pi)), [GT.r(TT, 2 * TT)], [GT.r(TT, 2 * TT)])
            dve(lambda e, j=j: e.tensor_tensor(out=YG[:, j, 0:T], in0=GT[:, 1, 0:T], in1=GT[:, 0, 0:T], op=ALU.mult), [GT.r(0, 2 * TT)], [xr(YG, j, T)])
        wc = di["wC_out"][0]
        for mb in range(D // WB):
            sa, wa = colblock("wCout", wc, 2 * D, mb * WB)
            sb_, wb_ = colblock("wCout", wc, 2 * D, D + mb * WB)
            for half in range(2):
                m = 2 * mb + half
                pa = PS[0 + (m % 2) * 2]
                pb = PS[1 + (m % 2) * 2]
                for (pp, wv, sl) in ((pa, wa, sa), (pb, wb_, sb_)):
                    for kc in range(KC):
                        pe(lambda e, pp=pp, wv=wv, kc=kc, half=half: e.matmul(pp[:, 0:T], lhsT=wv[:, kc, half * 128:(half + 1) * 128], rhs=YG[:, kc, 0:T],
                                                                              start=(kc == 0), stop=(kc == KC - 1)), [sl.r(), xr(YG, kc, T)], [pp.r(0, T)])
                tb = m % 2
                act(lambda e, pb=pb, tb=tb: e.activation(out=tmpA[:, tb, 0:T], in_=pb[:, 0:T], func=AF.Sigmoid), [pb.r(0, T)], [tmpA.r(tb * TT, tb * TT + T)])
                dve(lambda e, pa=pa, tb=tb, m=m: e.tensor_tensor(out=O[:, m, 0:T], in0=pa[:, 0:T], in1=tmpA[:, tb, 0:T], op=ALU.mult),
                    [pa.r(0, T), tmpA.r(tb * TT, tb * TT + T)], [xr(O, m, T)])

    for l_ in range(n_layers):
        for _ in mod_layer(l_):
            pass
    if n_layers > 2 and mix[2] == "r":
        s5_generate()
    dma("sp", wconv[:, :, :], di["wB_conv"][0].rearrange("(kc p) t -> p kc t", p=128), writes=[wconv.r()], nonc=True)

    passes = []
    for ti in range(n_tiles):
        passes.append(mk_ctx(TT, 0, di["x_prompt"][ti * TT:(ti + 1) * TT, :], do["y_prompt"][ti * TT:(ti + 1) * TT, :]))
    if with_sample:
        passes.append(mk_ctx(TS, 1, di["x_sample"], do["y_sample"]))

    prev_seq = None
    for c in passes:
        if c.seq != prev_seq:
            if c.seq == 0:
                pool(lambda e: e.memset(ZP[:, :, :], 0.0), [], [ZP.r()])
            else:
                for r in range(2):
                    dma("sp", ZP[:, :, r], di["state_conv"][0, r].rearrange("(kc p) -> p kc", p=128), writes=[ZP.r()], nonc=True)
            prev_seq = c.seq
        load_x(c)
        for l in range(n_layers):
            mod_side = None
            prenorm(c, l, 0)
            kind = l % 3
            idx_c = passes.index(c)
            first = idx_c == 0 or passes[idx_c - 1].seq != c.seq
            lastt = idx_c == len(passes) - 1 or passes[idx_c + 1].seq != c.seq
            if mix[l] == "r" and kind == 1:
                conv_mixer([c], l)
            elif mix[l] == "r" and kind == 0:
                mlstm_mixer(c, l, l // 3, first, lastt)
            elif mix[l] == "r" and kind == 2:
                s5_mixer(c, first, lastt)
            else:
                for kc in range(KC):
                    act(lambda e, kc=kc, T=c.T: e.copy(out=O[:, kc, 0:T], in_=H[:, kc, 0:T]), [xr(H, kc, c.T)], [xr(O, kc, c.T)])
            postnorm(c, l, 0)
            prenorm(c, l, 1)
            ffn([c], l, side=(mod_side if c is passes[0] else None))
            if mod_side is not None:
                for _ in mod_side:
                    pass
            postnorm(c, l, 1, final=(l == n_layers - 1))
        store_y(c)
        last_of_seq = (c is passes[-1]) or (passes[passes.index(c) + 1].seq != c.seq)
        if last_of_seq:
            pre = "p" if c.seq == 0 else "s"
            for r in range(2):
                dma("sp", do[pre + "_conv"][0, r].rearrange("(kc p) -> p kc", p=128), ZP[:, :, r], reads=[ZP.r()], nonc=True)

    P.emit()
    return nc, P


STATE_KEYS = ("state_mlstm_C", "state_mlstm_n", "state_mlstm_m", "state_conv", "state_s5_re", "state_s5_im")


def core_inputs(inputs, b, seq, n_layers=DEPTH):
    m = {}
    for name, shp in IN_SPECS:
        a = inputs[name]
        if shp is not None and shp[0] == DEPTH:
            a = a[:max(n_layers, 1)]
        if name == "x_prompt":
            a = a[b, :seq]
        elif name == "x_sample":
            a = a[b]
        elif name in STATE_KEYS:
            a = a[:, b]
        elif name in ("c_prompt", "c_sample"):
            a = a[b:b + 1]
        m[name] = np.ascontiguousarray(a, dtype=np.float32)
    return m


_PROG = {}


def kernel(**inputs):
    n = 8
    seq = inputs["x_prompt"].shape[1]
    n_tiles = seq // TT
    key = n_tiles
    nc, _ = build_program(n_tiles=n_tiles, n_layers=DEPTH, with_sample=True, mix="rrrr")
    in_maps = [core_inputs(inputs, b, seq) for b in range(n)]
    res = run_bass_kernel_spmd(nc, in_maps, core_ids=list(range(n)))
    R = res.results

    def stack(name, axis):
        return np.stack([np.asarray(R[b][name], dtype=np.float32) for b in range(n)], axis=axis)

    outs = [stack("y_prompt", 0), stack("y_sample", 0)]
    for pre in ("p", "s"):
        for nm in ("_C", "_n", "_m", "_conv", "_re", "_im"):
            outs.append(stack(pre + nm, 1))
    return tuple(outs)
```

```python
import math
import numpy as np
import concourse.bass as bass
import concourse.mybir as mybir
from concourse.bass_utils import run_bass_kernel_spmd
from contextlib import ExitStack

F32 = mybir.dt.float32
BF16 = mybir.dt.bfloat16
AF = mybir.ActivationFunctionType
ALU = mybir.AluOpType

ENGS = ("pe", "act", "dve", "pool", "sp")
SAME_ENGINE_SYNC = {"pe": False, "act": True, "dve": True, "pool": True, "sp": False}
DT_SIZE = {F32: 4, BF16: 2}


class Track:
    def __init__(self, size, name=""):
        self.size = size
        self.name = name
        self.segs = [[0, size, None, []]]

    def _split(self, pos):
        segs = self.segs
        lo, hi = 0, len(segs)
        while lo < hi:
            mid = (lo + hi) // 2
            if segs[mid][1] <= pos:
                lo = mid + 1
            else:
                hi = mid
        if lo < len(segs):
            s = segs[lo]
            if s[0] < pos < s[1]:
                segs[lo:lo + 1] = [[s[0], pos, s[2], list(s[3])], [pos, s[1], s[2], list(s[3])]]

    def access(self, lo, hi, op_idx, write):
        assert 0 <= lo < hi <= self.size, (self.name, lo, hi, self.size)
        self._split(lo)
        self._split(hi)
        deps = set()
        segs = self.segs
        a, b = 0, len(segs)
        while a < b:
            mid = (a + b) // 2
            if segs[mid][0] < lo:
                a = mid + 1
            else:
                b = mid
        k = a
        first = k
        while k < len(segs) and segs[k][1] <= hi:
            s = segs[k]
            if s[2] is not None:
                deps.add(s[2])
            if write:
                deps.update(s[3])
            else:
                s[3].append(op_idx)
            k += 1
        if write:
            segs[first:k] = [[lo, hi, op_idx, []]]
        return deps


class Buf:
    def __init__(self, prog, name, shape, dtype, space="sbuf", arena=None, off=0, ap=None):
        self.prog = prog
        self.name = name
        self.shape = list(shape)
        self.dtype = dtype
        self.space = space
        self.esz = DT_SIZE[dtype]
        self.fsize = int(np.prod(self.shape[1:]))
        nc = prog.nc
        self.arena = arena
        self.off = off
        if arena is not None:
            n16 = self.fsize * self.esz // 2
            assert off % 4 == 0 and off + n16 * 2 <= arena.fsize * 2, (name, off, n16, arena.fsize)
            base = arena.t[:, off // 2: off // 2 + n16]
            v = base.bitcast(dtype) if dtype != BF16 else base
            if len(self.shape) > 2:
                names = " ".join("d%d" % i for i in range(1, len(self.shape)))
                kw = {"d%d" % i: self.shape[i] for i in range(1, len(self.shape))}
                v = v.rearrange("p (%s) -> p %s" % (names, names), **kw)
            self.t = v
            self.track = arena.track
        elif space == "sbuf":
            self.t = prog.stack.enter_context(nc.sbuf_tensor(name, self.shape, dtype))
            self.track = Track(self.fsize * self.esz, name)
        elif space == "psum":
            self.t = prog.stack.enter_context(nc.psum_tensor(name, self.shape, dtype))
            self.track = Track(self.fsize * self.esz, name)
        else:
            self.t = ap
            self.track = Track(self.fsize, name)
            self.esz = 1

    def __getitem__(self, idx):
        return self.t[idx]

    def r(self, lo=0, hi=None):
        hi = self.fsize if hi is None else hi
        return (self.track, self.off + lo * self.esz, self.off + hi * self.esz)

    def sub(self, name, off, shape, dtype):
        return Buf(self.prog, name, shape, dtype, arena=self, off=off)


class Op:
    __slots__ = ("eng", "fn", "deps", "is_dma", "signal", "count", "sem", "idx")


class Prog:
    def __init__(self, nc):
        self.nc = nc
        self.stack = ExitStack()
        self.ops = []
        self.n_dma_sems = {"pool": 24, "sp": 40, "act": 4}

    def sb(self, name, shape, dtype=F32):
        return Buf(self, name, shape, dtype, "sbuf")

    def ps(self, name, shape, dtype=F32):
        return Buf(self, name, shape, dtype, "psum")

    def dram(self, name, nblocks):
        return Buf(self, name, [1, nblocks], BF16, "dram")

    def op(self, eng, fn, reads=(), writes=(), dma=False):
        o = Op()
        o.eng = eng
        o.fn = fn
        o.is_dma = dma
        o.idx = len(self.ops)
        o.signal = False
        o.count = None
        o.sem = None
        deps = set()
        for (tr, lo, hi) in reads:
            deps |= tr.access(lo, hi, o.idx, False)
        for (tr, lo, hi) in writes:
            deps |= tr.access(lo, hi, o.idx, True)
        deps.discard(o.idx)
        o.deps = deps
        self.ops.append(o)
        return o

    def emit(self, final_wait_engine="sp"):
        nc = self.nc
        ops = self.ops
        stack = self.stack
        for o in ops:
            latest = {}
            keep = set()
            for d in o.deps:
                p = ops[d]
                if p.is_dma:
                    keep.add(d)
                    continue
                if p.eng == o.eng and (not o.is_dma) and not SAME_ENGINE_SYNC[p.eng]:
                    continue
                if latest.get(p.eng, -1) < d:
                    latest[p.eng] = d
            keep.update(latest.values())
            o.deps = keep
            for d in keep:
                if not ops[d].is_dma:
                    ops[d].signal = True
        eng_sem = {e: stack.enter_context(nc.semaphore("S_" + e)) for e in ENGS}
        dma_sems = {e: [stack.enter_context(nc.semaphore("D_%s_%d" % (e, i))) for i in range(n)]
                    for e, n in self.n_dma_sems.items()}
        dma_use = {e: [0] * n for e, n in self.n_dma_sems.items()}
        dma_rr = {e: 0 for e in self.n_dma_sems}
        counts = {e: 0 for e in ENGS}
        pre_wait = {}
        for o in ops:
            if o.is_dma:
                e = o.eng
                k = dma_rr[e]
                dma_rr[e] = (k + 1) % len(dma_sems[e])
                if dma_use[e][k] > 0:
                    pre_wait[o.idx] = (dma_sems[e][k], dma_use[e][k] * 16)
                dma_use[e][k] += 1
                o.sem = dma_sems[e][k]
                o.count = dma_use[e][k] * 16
            elif o.signal:
                counts[o.eng] += 1
                o.sem = eng_sem[o.eng]
                o.count = counts[o.eng]
        by_eng = {e: [o for o in ops if o.eng == e] for e in ENGS}
        all_dma = [o for o in ops if o.is_dma]
        self.stats = {e: len(by_eng[e]) for e in ENGS}
        self.stats["signals"] = dict(counts)

        def run_engine(ename, eng):
            waited = {}
            nwaits = 0

            def wait(sem, val):
                nonlocal nwaits
                key = id(sem)
                if waited.get(key, 0) >= val:
                    return
                waited[key] = val
                eng.wait_ge(sem, val)
                nwaits += 1

            attach = ename in ("act", "dve")
            for o in by_eng[ename]:
                if o.idx in pre_wait:
                    wait(*pre_wait[o.idx])
                need = []
                for d in sorted(o.deps):
                    p = ops[d]
                    if p.sem is None:
                        continue
                    if waited.get(id(p.sem), 0) >= p.count:
                        continue
                    need.append((p.sem, p.count))
                best = {}
                for sem, val in need:
                    if best.get(id(sem), (None, 0))[1] < val:
                        best[id(sem)] = (sem, val)
                need = list(best.values())
                last = None
                if attach and (not o.is_dma) and need:
                    last = need.pop()
                for sem, val in need:
                    wait(sem, val)
                ins = o.fn(eng)
                if last is not None:
                    ins._wait_ge(last[0], last[1])
                    waited[id(last[0])] = max(waited.get(id(last[0]), 0), last[1])
                    nwaits += 1
                if o.sem is not None:
                    ins.then_inc(o.sem, 16 if o.is_dma else 1)
            if ename == final_wait_engine:
                last = {}
                for o in all_dma:
                    last[id(o.sem)] = (o.sem, o.count)
                for sem, val in last.values():
                    wait(sem, val)
                for e in ENGS:
                    if counts[e] > 0:
                        wait(eng_sem[e], counts[e])
            self.stats["waits_" + ename] = nwaits

        with nc.Block() as block:
            @block.tensor
            def _(eng):
                run_engine("pe", eng)

            @block.scalar
            def _(eng):
                run_engine("act", eng)

            @block.vector
            def _(eng):
                run_engine("dve", eng)

            @block.gpsimd
            def _(eng):
                run_engine("pool", eng)

            @block.sync
            def _(eng):
                run_engine("sp", eng)
        self.stack.close()


D = 2048
KC = 16
DEPTH = 4
NH = 4
DQK = 256
DV = 512
A_IN = 6152
DFF = 5632
FC = 44
S5G = 128
S5N = 64
S5P = 16
EPS = 1e-6
TT = 512
TS = 16
WB = 256
BIG = 30000.0

IN_SPECS = [
    ("x_prompt", None), ("x_sample", [TS, D]),
    ("state_mlstm_C", [2, NH, DQK, DV]), ("state_mlstm_n", [2, NH, DQK]), ("state_mlstm_m", [2, NH]),
    ("state_conv", [1, 2, D]), ("state_s5_re", [1, S5G, S5N]), ("state_s5_im", [1, S5G, S5N]),
    ("c_prompt", [1, D]), ("c_sample", [1, D]),
    ("w_mod", [DEPTH, D, 6 * D]), ("b_mod", [DEPTH, 6 * D]), ("g_norm", [DEPTH, 4, D]),
    ("wA_in", [2, D, A_IN]), ("bA_gates", [2, 8]), ("gA_hnorm", [2, D]), ("wA_out", [2, D, D]),
    ("wB_in", [1, D, 3 * D]), ("wB_conv", [1, D, 3]), ("wB_out", [1, D, D]),
    ("s5_A_re", [1, S5G, S5N]), ("s5_A_im", [1, S5G, S5N]),
    ("s5_B_re", [1, S5G, S5N, S5P]), ("s5_B_im", [1, S5G, S5N, S5P]),
    ("s5_C_re", [1, S5G, S5P, S5N]), ("s5_C_im", [1, S5G, S5P, S5N]),
    ("s5_D", [1, S5G, S5P]), ("s5_log_dt", [1, S5G]), ("wC_out", [1, D, 2 * D]),
    ("w_ffn_gate", [DEPTH, D, DFF]), ("w_ffn_up", [DEPTH, D, DFF]), ("w_ffn_down", [DEPTH, DFF, D]),
]


def build_program(n_tiles=8, n_layers=DEPTH, with_sample=True, mix="rrrr"):
    SEQ = n_tiles * TT
    nc = bass.Bass("TRN2", target_bir_lowering=False)
    P = Prog(nc)
    di = {}
    for name, shp in IN_SPECS:
        if name == "x_prompt":
            shp = [SEQ, D]
        elif shp[0] == DEPTH:
            shp = [max(n_layers, 1)] + list(shp[1:])
        di[name] = nc.dram_tensor(name, shp, F32, kind="ExternalInput").ap()
    do = {}

    def OUT(name, shp):
        do[name] = nc.dram_tensor(name, shp, F32, kind="ExternalOutput").ap()

    OUT("y_prompt", [SEQ, D]); OUT("y_sample", [TS, D])
    for pre in ("p", "s"):
        OUT(pre + "_C", [2, NH, DQK, DV]); OUT(pre + "_n", [2, NH, DQK]); OUT(pre + "_m", [2, NH])
        OUT(pre + "_conv", [1, 2, D]); OUT(pre + "_re", [1, S5G, S5N]); OUT(pre + "_im", [1, S5G, S5N])

    def dma(eng, out, in_, reads=(), writes=(), nonc=False):
        if nonc:
            def fn(e):
                with nc.allow_non_contiguous_dma(reason="small strided parameter/state relayout"):
                    return e.dma_start(out=out, in_=in_)
        else:
            def fn(e):
                return e.dma_start(out=out, in_=in_)
        return P.op(eng, fn, reads=reads, writes=writes, dma=True)

    MAIN = P.sb("main", [128, (32768 + 16384 + 32768 + 55296) // 2], BF16)
    X = MAIN.sub("x", 0, [128, KC, TT], F32)
    H = MAIN.sub("h", 32768, [128, KC, TT], BF16)
    O = MAIN.sub("o", 49152, [128, KC, TT], F32)
    SCR0 = 81920
    YS4 = MAIN.sub("ys4", 49152, [128, 4, D], F32)
    NSLOT = 4
    RING = [P.sb("ring%d" % i, [128, 4096], BF16) for i in range(NSLOT)]
    CST = P.sb("cst", [128, 8192 + 4096], BF16)
    C32 = CST.sub("C32", 0, [128, 8, DV], F32)
    CBF = CST.sub("Cbf", 16384, [128, 8, DV], BF16)
    PS = [P.ps("bank%d" % i, [128, 512], F32) for i in range(8)]

    ident = P.sb("ident", [128, 128], BF16)
    ones_bf = P.sb("ones_bf", [128, 128], BF16)
    maskbig = P.sb("maskbig", [128, 128], BF16)
    ones_f = P.sb("ones_f", [128, 512], F32)
    epsb = P.sb("epsb", [128, 1], F32)
    coef = P.sb("coef", [128, DEPTH, 6, KC, 2], F32)
    TMPA_RAW = P.sb("tmpA_raw", [128, 2 * TT * 2], BF16)
    tmpA = TMPA_RAW.sub("tmpA", 0, [128, 2, TT], F32)
    rstd = P.sb("rstd", [128, TT], F32)
    tmpB = P.sb("tmpB", [128, 2, TT], BF16)

    P_nrep = P.sb("nrep", [128, 8, 128], BF16)
    P_gelu = tmpA
    ones8 = ones_f[:, 0:128].unsqueeze(1).to_broadcast([128, 8, 128])

    def pe(fn, reads, writes):
        return P.op("pe", fn, reads, writes)

    def act(fn, reads, writes):
        return P.op("act", fn, reads, writes)

    def dve(fn, reads, writes):
        return P.op("dve", fn, reads, writes)

    def pool(fn, reads, writes):
        return P.op("pool", fn, reads, writes)

    pool(lambda e: e.memset(ident[:, :], 1.0), [], [ident.r()])
    pool(lambda e: e.affine_select(out=ident[:, :], in_=ident[:, :], pattern=[[-1, 128]], compare_op=ALU.is_equal,
                                   fill=0.0, base=0, channel_multiplier=1), [ident.r()], [ident.r()])
    pool(lambda e: e.memset(ones_bf[:, :], 1.0), [], [ones_bf.r()])
    pool(lambda e: e.memset(ones_f[:, :], 1.0), [], [ones_f.r()])
    pool(lambda e: e.memset(epsb[:, :], EPS), [], [epsb.r()])
    pool(lambda e: e.memset(maskbig[:, :], BIG), [], [maskbig.r()])
    pool(lambda e: e.affine_select(out=maskbig[:, :], in_=maskbig[:, :], pattern=[[-1, 128]], compare_op=ALU.is_gt,
                                   fill=0.0, base=0, channel_multiplier=1), [maskbig.r()], [maskbig.r()])

    wcache = {}
    scratch = {}
    ring_i = [0]

    def scratch_for(name, nblk):
        if name not in scratch:
            t = nc.dram_tensor("scr_" + name, [nblk, 128, 4096], BF16, kind="Internal").ap()
            scratch[name] = (t, P.dram("scrT_" + name, nblk))
        return scratch[name]

    def wblock(name, nblk, blk, src_ap, nel, view_shape, cache=True):
        slot = RING[ring_i[0] % NSLOT]
        ring_i[0] += 1
        names = " ".join("d%d" % i for i in range(len(view_shape)))
        kw = {"d%d" % i: view_shape[i] for i in range(len(view_shape))}
        view = slot.t[:, 0:nel].rearrange("p (%s) -> p %s" % (names, names), **kw)
        if cache and (name, blk) in wcache:
            sc, tr = scratch[name]
            dma("sp", slot.t[:, 0:nel], sc[blk, :, 0:nel], reads=[tr.r(blk, blk + 1)], writes=[slot.r(0, nel)])
        else:
            def fn(e):
                with nc.allow_non_contiguous_dma(reason="weight block"):
                    return e.dma_start(out=view, in_=src_ap)
            P.op("pool", fn, reads=[], writes=[slot.r(0, nel)], dma=True)
            if cache:
                sc, tr = scratch_for(name, nblk)
                dma("sp", sc[blk, :, 0:nel], slot.t[:, 0:nel], reads=[slot.r(0, nel)], writes=[tr.r(blk, blk + 1)])
                wcache[(name, blk)] = True
        return slot, view

    def colblock(name, w2d, ncols_total, c0, width=WB, cache=True):
        src = w2d[:, c0:c0 + width].rearrange("(kc p) c -> p kc c", p=128)
        nblk = (ncols_total + width - 1) // width
        return wblock(name, nblk, c0 // width, src, KC * width, [KC, width], cache=cache)

    class Ctx:
        pass

    def mk_ctx(T, seq, xin, yout):
        c = Ctx()
        c.T = T
        c.seq = seq
        c.xin = xin
        c.yout = yout
        c.L = min(128, T)
        c.NB = T // c.L
        return c

    def xr(b, kc, T, n=1):
        return b.r(kc * TT, (kc + n - 1) * TT + T)

    IOB = SCR0
    NPART = 2

    def load_x(c):
        T, L = c.T, c.L
        xs = MAIN.sub("xs", IOB + 16384, [128, D], F32)
        xh = MAIN.sub("xh", IOB + 16384 + 8192, [128, NPART, D], BF16)
        for tb in range(c.NB):
            t0 = tb * L
            dma("sp", xs[0:L, :], c.xin[t0:t0 + L, :], writes=[xs.r()])
            act(lambda e: e.copy(out=xh[0:L, 0, :], in_=xs[0:L, :]), [xs.r()], [xh.r(0, D)])
            dve(lambda e: e.tensor_sub(out=xs[0:L, :], in0=xs[0:L, :], in1=xh[0:L, 0, :]), [xs.r(), xh.r(0, D)], [xs.r()])
            act(lambda e: e.copy(out=xh[0:L, 1, :], in_=xs[0:L, :]), [xs.r()], [xh.r(D, 2 * D)])
            for k4 in range(4):
                bank = PS[4 + (k4 % 2)]
                for q in range(4):
                    kc = k4 * 4 + q
                    for part in range(NPART):
                        pe(lambda e, kc=kc, q=q, part=part, bank=bank: e.matmul(
                            bank[:, q * 128:q * 128 + L], lhsT=xh[0:L, part, kc * 128:(kc + 1) * 128], rhs=ident[0:L, 0:L],
                            start=(part == 0), stop=(part == NPART - 1)),
                           [xh.r(), ident.r()], [bank.r(q * 128, q * 128 + L)])
                src = bank.t[:, :].rearrange("p (q t) -> p q t", q=4)[:, :, 0:L]
                dst = X[:, k4 * 4:k4 * 4 + 4, t0:t0 + L]
                dve(lambda e, src=src, dst=dst: e.tensor_copy(out=dst, in_=src), [bank.r()], [xr(X, k4 * 4, T, 4)])

    def store_y(c):
        T, L = c.T, c.L
        ys = MAIN.sub("ys", IOB + 40960, [128, D], F32)
        xh = MAIN.sub("yh", IOB + 49152, [128, NPART, 4, 128], BF16)
        r1 = MAIN.sub("yr", IOB + 49152 + 2048, [128, 4, 128], F32)
        YSRC = O if n_layers > 0 else X
        for tb in range(c.NB):
            t0 = tb * L
            for k4 in range(4):
                xa = YSRC[:, k4 * 4:(k4 + 1) * 4, t0:t0 + L]
                rr = xr(YSRC, k4 * 4, T, 4)
                act(lambda e, xa=xa: e.copy(out=xh[:, 0, :, 0:L], in_=xa), [rr], [xh.r(0, 512)])
                dve(lambda e, xa=xa: e.tensor_sub(out=r1[:, :, 0:L], in0=xa, in1=xh[:, 0, :, 0:L]), [rr, xh.r(0, 512)], [r1.r()])
                act(lambda e: e.copy(out=xh[:, 1, :, 0:L], in_=r1[:, :, 0:L]), [r1.r()], [xh.r(512, 1024)])
                bank = PS[6 + (k4 % 2)]
                for q in range(4):
                    for part in range(NPART):
                        pe(lambda e, q=q, part=part, bank=bank: e.matmul(
                            bank[0:L, q * 128:(q + 1) * 128], lhsT=xh[:, part, q, 0:L], rhs=ident[:, :],
                            start=(part == 0), stop=(part == NPART - 1)),
                           [xh.r(), ident.r()], [bank.r(q * 128, (q + 1) * 128)])
                dve(lambda e, bank=bank, k4=k4: e.tensor_copy(out=ys[0:L, k4 * 512:(k4 + 1) * 512], in_=bank[0:L, :]),
                    [bank.r()], [ys.r(k4 * 512, (k4 + 1) * 512)])
            dma("sp", c.yout[t0:t0 + L, :], ys[0:L, :], reads=[ys.r()])

    def mod_setup():
        cf = TMPA_RAW.sub("cf", 1408, [128, KC, 2], F32)
        scb = P.sb("scb", [128, KC, 2], BF16)
        gn = TMPA_RAW.sub("gn_l", 1152, [128, 4, KC], F32)
        bm = TMPA_RAW.sub("bm_l", 768, [128, 96], F32)
        modv = TMPA_RAW.sub("modv", 0, [128, 96, 2], F32)
        for s, nm in enumerate(("c_prompt", "c_sample")):
            dma("sp", cf[:, :, s], di[nm][0, :].rearrange("(kc p) -> p kc", p=128), writes=[cf.r()], nonc=True)
        act(lambda e: e.activation(out=scb[:, :, :], in_=cf[:, :, :], func=AF.Silu), [cf.r()], [scb.r()])
        return cf, scb, gn, bm, modv

    MODB = mod_setup() if n_layers > 0 else None

    def mod_layer(l):
        cf, scb, gn, bm, modv = MODB
        if True:
            bank = PS[6]
            for b in range(6 * D // WB):
                slot, wv = colblock("w_mod%d" % l, di["w_mod"][l], 6 * D, b * WB, cache=False)
                for half in range(2):
                    m = 2 * b + half
                    for kc in range(KC):
                        pe(lambda e, wv=wv, m=m, kc=kc, half=half, bank=bank: e.matmul(
                            bank[:, 2 * m:2 * m + 2], lhsT=wv[:, kc, half * 128:(half + 1) * 128], rhs=scb[:, kc, :],
                            start=(kc == 0), stop=(kc == KC - 1)),
                           [slot.r(), scb.r()], [bank.r(2 * m, 2 * m + 2)])
                yield
            dma("sp", gn[:, :, :], di["g_norm"][l].rearrange("j (kc p) -> p j kc", p=128), writes=[gn.r()], nonc=True)
            dma("sp", bm[:, :], di["b_mod"][l].rearrange("(m p) -> p m", p=128), writes=[bm.r()], nonc=True)
            bmb = bm[:, :].unsqueeze(2).to_broadcast([128, 96, 2])
            dve(lambda e, bank=bank, bmb=bmb: e.tensor_tensor(
                out=modv[:, :, :], in0=bank[:, 0:192].rearrange("p (m s) -> p m s", s=2), in1=bmb, op=ALU.add),
                [bank.r(0, 192), bm.r()], [modv.r()])
            def mv(j):
                return modv[:, j * 16:(j + 1) * 16, :]

            def gnb(j):
                return gn[:, j, :].unsqueeze(2).to_broadcast([128, KC, 2])
            g0, g1_, g2_, g3_ = gnb(0), gnb(1), gnb(2), gnb(3)
            m0_, m1_, m2_, m3_, m4_, m5_ = (mv(j) for j in range(6))
            cf_ = [coef[:, l, q, :, :] for q in range(6)]
            dve(lambda e, o=cf_[0], a=m1_, g=g0: e.scalar_tensor_tensor(out=o, in0=a, scalar=1.0, in1=g, op0=ALU.add, op1=ALU.mult),
                [modv.r(), gn.r()], [coef.r()])
            dve(lambda e, o=cf_[1], a=m0_: e.tensor_copy(out=o, in_=a), [modv.r()], [coef.r()])
            dve(lambda e, o=cf_[2], a=m2_, g=g1_: e.tensor_tensor(out=o, in0=a, in1=g, op=ALU.mult), [modv.r(), gn.r()], [coef.r()])
            dve(lambda e, o=cf_[3], a=m4_, g=g2_: e.scalar_tensor_tensor(out=o, in0=a, scalar=1.0, in1=g, op0=ALU.add, op1=ALU.mult),
                [modv.r(), gn.r()], [coef.r()])
            dve(lambda e, o=cf_[4], a=m3_: e.tensor_copy(out=o, in_=a), [modv.r()], [coef.r()])
            dve(lambda e, o=cf_[5], a=m5_, g=g3_: e.tensor_tensor(out=o, in0=a, in1=g, op=ALU.mult), [modv.r(), gn.r()], [coef.r()])

    def stats(c, src, sqbuf):
        T = c.T
        bank = PS[5]
        for kc in range(KC):
            act(lambda e, kc=kc: e.activation(out=sqbuf[:, kc, 0:T], in_=src[:, kc, 0:T], func=AF.Square),
                [xr(src, kc, T)], [xr(sqbuf, kc, T)])
            pe(lambda e, kc=kc: e.matmul(bank[:, 0:T], lhsT=ones_bf[:, :], rhs=sqbuf[:, kc, 0:T], start=(kc == 0), stop=(kc == KC - 1)),
               [ones_bf.r(), xr(sqbuf, kc, T)], [bank.r(0, T)])
        act(lambda e: e.activation(out=rstd[:, 0:T], in_=bank[:, 0:T], func=AF.Sqrt, bias=epsb[:, 0:1], scale=1.0 / D),
            [bank.r(0, T), epsb.r()], [rstd.r(0, T)])
        dve(lambda e: e.reciprocal(out=rstd[:, 0:T], in_=rstd[:, 0:T]), [rstd.r(0, T)], [rstd.r(0, T)])

    def prenorm(c, l, which):
        T = c.T
        qa, qb = (0, 1) if which == 0 else (3, 4)
        stats(c, X, H)
        for kc in range(KC):
            tb = kc % 2
            dve(lambda e, kc=kc, tb=tb: e.scalar_tensor_tensor(
                out=tmpA[:, tb, 0:T], in0=X[:, kc, 0:T], scalar=coef[:, l, qa, kc, c.seq:c.seq + 1], in1=rstd[:, 0:T],
                op0=ALU.mult, op1=ALU.mult), [xr(X, kc, T), coef.r(), rstd.r(0, T)], [tmpA.r(tb * TT, tb * TT + T)])
            act(lambda e, kc=kc, tb=tb: e.activation(out=H[:, kc, 0:T], in_=tmpA[:, tb, 0:T], func=AF.Identity,
                                                     bias=coef[:, l, qb, kc, c.seq:c.seq + 1], scale=1.0),
                [tmpA.r(tb * TT, tb * TT + T), coef.r()], [xr(H, kc, T)])

    def postnorm(c, l, which, final=False):
        T = c.T
        qg = 2 if which == 0 else 5
        stats(c, O, H)
        for kc in range(KC):
            tb = kc % 2
            dve(lambda e, kc=kc, tb=tb: e.scalar_tensor_tensor(
                out=tmpA[:, tb, 0:T], in0=O[:, kc, 0:T], scalar=coef[:, l, qg, kc, c.seq:c.seq + 1], in1=rstd[:, 0:T],
                op0=ALU.mult, op1=ALU.mult), [xr(O, kc, T), coef.r(), rstd.r(0, T)], [tmpA.r(tb * TT, tb * TT + T)])
            if final:
                pool(lambda e, kc=kc, tb=tb: e.tensor_tensor(out=O[:, kc, 0:T], in0=X[:, kc, 0:T], in1=tmpA[:, tb, 0:T], op=ALU.add),
                     [xr(X, kc, T), tmpA.r(tb * TT, tb * TT + T)], [xr(O, kc, T)])
            else:
                pool(lambda e, kc=kc, tb=tb: e.tensor_tensor(out=X[:, kc, 0:T], in0=X[:, kc, 0:T], in1=tmpA[:, tb, 0:T], op=ALU.add),
                     [xr(X, kc, T), tmpA.r(tb * TT, tb * TT + T)], [xr(X, kc, T)])

    ACTB = MAIN.sub("ffn_act", SCR0, [128, FC, TT], BF16)

    def ffn(cs, l, side=None):
        def pump():
            if side is not None:
                next(side, None)
        for b in range(DFF // WB):
            pump()
            sg, wg = colblock("ffg%d" % l, di["w_ffn_gate"][l], DFF, b * WB)
            su, wu = colblock("ffu%d" % l, di["w_ffn_up"][l], DFF, b * WB)
            for half in range(2):
                f = 2 * b + half
                for c in cs:
                    T = c.T
                    pg = PS[f % 2]
                    pu = PS[2 + f % 2]
                    for kc in range(KC):
                        pe(lambda e, kc=kc, half=half, pg=pg, wg=wg, T=T: e.matmul(
                            pg[:, 0:T], lhsT=wg[:, kc, half * 128:(half + 1) * 128], rhs=H[:, kc, 0:T],
                            start=(kc == 0), stop=(kc == KC - 1)), [sg.r(), xr(H, kc, T)], [pg.r(0, T)])
                    for kc in range(KC):
                        pe(lambda e, kc=kc, half=half, pu=pu, wu=wu, T=T: e.matmul(
                            pu[:, 0:T], lhsT=wu[:, kc, half * 128:(half + 1) * 128], rhs=H[:, kc, 0:T],
                            start=(kc == 0), stop=(kc == KC - 1)), [su.r(), xr(H, kc, T)], [pu.r(0, T)])
                    tb = f % 2
                    act(lambda e, pg=pg, tb=tb, T=T: e.activation(out=tmpB[:, tb, 0:T], in_=pg[:, 0:T], func=AF.Silu),
                        [pg.r(0, T)], [tmpB.r(tb * TT, tb * TT + T)])
                    dve(lambda e, pu=pu, tb=tb, f=f, T=T: e.tensor_tensor(out=ACTB[:, f, 0:T], in0=pu[:, 0:T], in1=tmpB[:, tb, 0:T], op=ALU.mult),
                        [pu.r(0, T), tmpB.r(tb * TT, tb * TT + T)], [xr(ACTB, f, T)])
        wd = di["w_ffn_down"][l]
        HF = FC // 2
        for m in range(KC):
            pump()
            pump()
            for c in cs:
                T = c.T
                pd = PS[4] if m % 2 == 0 else PS[7]
                slots = []
                for hf in range(2):
                    src = wd[hf * HF * 128:(hf + 1) * HF * 128, m * 128:(m + 1) * 128].rearrange("(fc p) c -> p fc c", p=128)
                    sd, wv = wblock("ffd%d" % l, 2 * KC, 2 * m + hf, src, HF * 128, [HF, 128])
                    for fc in range(HF):
                        f = hf * HF + fc
                        pe(lambda e, wv=wv, fc=fc, f=f, pd=pd, T=T: e.matmul(
                            pd[:, 0:T], lhsT=wv[:, fc, :], rhs=ACTB[:, f, 0:T], start=(f == 0), stop=(f == FC - 1)),
                           [sd.r(), xr(ACTB, f, T)], [pd.r(0, T)])
                act(lambda e, pd=pd, m=m, T=T: e.copy(out=O[:, m, 0:T], in_=pd[:, 0:T]), [pd.r(0, T)], [xr(O, m, T)])

    ZP = P.sb("zp", [128, KC, 4], F32)
    wconv = P.sb("wconv", [128, KC, 3], F32)

    def conv_mixer(cs, l):
        w2 = di["wB_in"][0]
        CW = MAIN.sub("conv_w", SCR0, [128, 8, TT + 8], F32)
        for mb in range(D // WB):
            s0, w0 = colblock("wBin", w2, 3 * D, mb * WB)
            s1, w1 = colblock("wBin", w2, 3 * D, D + mb * WB)
            s2, w2v = colblock("wBin", w2, 3 * D, 2 * D + mb * WB)
            for half in range(2):
                m = 2 * mb + half
                for c in cs:
                    T = c.T
                    pgb, pgc, pu = PS[0 + (m % 2) * 3], PS[1 + (m % 2) * 3], PS[2 + (m % 2) * 3]
                    for (pp, wv, sl) in ((pgb, w0, s0), (pgc, w1, s1), (pu, w2v, s2)):
                        for kc in range(KC):
                            pe(lambda e, pp=pp, wv=wv, kc=kc, half=half, T=T: e.matmul(
                                pp[:, 0:T], lhsT=wv[:, kc, half * 128:(half + 1) * 128], rhs=H[:, kc, 0:T],
                                start=(kc == 0), stop=(kc == KC - 1)), [sl.r(), xr(H, kc, T)], [pp.r(0, T)])
                    w = m % 2
                    zt = CW[:, w * 4 + 0, :]
                    gct = CW[:, w * 4 + 1, :]
                    cv = CW[:, w * 4 + 2, :]
                    base = (w * 4) * (TT + 8)
                    zr = CW.r(base, base + TT + 8)
                    gr = CW.r(base + (TT + 8), base + 2 * (TT + 8))
                    cr = CW.r(base + 2 * (TT + 8), base + 3 * (TT + 8))
                    dve(lambda e, zt=zt, m=m: e.tensor_copy(out=zt[:, 0:2], in_=ZP[:, m, 0:2]), [ZP.r(m * 4, m * 4 + 2)], [zr])
                    act(lambda e, gct=gct, pgc=pgc, T=T: e.copy(out=gct[:, 0:T], in_=pgc[:, 0:T]), [pgc.r(0, T)], [gr])
                    dve(lambda e, zt=zt, gct=gct, pu=pu, T=T: e.tensor_tensor(out=zt[:, 2:2 + T], in0=pu[:, 0:T], in1=gct[:, 0:T], op=ALU.mult),
                        [pu.r(0, T), gr], [zr])
                    dve(lambda e, zt=zt, m=m, T=T: e.tensor_copy(out=ZP[:, m, 0:2], in_=zt[:, T:T + 2]), [zr], [ZP.r(m * 4, m * 4 + 2)])
                    act(lambda e, zt=zt, cv=cv, m=m, T=T: e.activation(out=cv[:, 0:T], in_=zt[:, 0:T], func=AF.Identity, scale=wconv[:, m, 0:1]),
                        [zr, wconv.r()], [cr])
                    dve(lambda e, zt=zt, cv=cv, m=m, T=T: e.scalar_tensor_tensor(out=cv[:, 0:T], in0=zt[:, 1:1 + T], scalar=wconv[:, m, 1:2], in1=cv[:, 0:T], op0=ALU.mult, op1=ALU.add),
                        [zr, wconv.r(), cr], [cr])
                    dve(lambda e, zt=zt, cv=cv, m=m, T=T: e.scalar_tensor_tensor(out=cv[:, 0:T], in0=zt[:, 2:2 + T], scalar=wconv[:, m, 2:3], in1=cv[:, 0:T], op0=ALU.mult, op1=ALU.add),
                        [zr, wconv.r(), cr], [cr])
                    dve(lambda e, cv=cv, pgb=pgb, m=m, T=T: e.tensor_tensor(out=CG[:, m, 0:T], in0=pgb[:, 0:T], in1=cv[:, 0:T], op=ALU.mult),
                        [pgb.r(0, T), cr], [xr(CG, m, T)])
        outproj(cs, "wBout", di["wB_out"][0], CG)

    CG = MAIN.sub("conv_g", SCR0 + 8 * (TT + 8) * 4, [128, KC, TT], BF16)

    def outproj(cs, name, w2d, src):
        for mb in range(D // WB):
            sl, wv = colblock(name, w2d, D, mb * WB)
            for half in range(2):
                m = 2 * mb + half
                for c in cs:
                    T = c.T
                    pd = PS[4] if m % 2 == 0 else PS[7]
                    for kc in range(KC):
                        pe(lambda e, wv=wv, kc=kc, half=half, pd=pd, T=T: e.matmul(
                            pd[:, 0:T], lhsT=wv[:, kc, half * 128:(half + 1) * 128], rhs=src[:, kc, 0:T],
                            start=(kc == 0), stop=(kc == KC - 1)), [sl.r(), xr(src, kc, T)], [pd.r(0, T)])
                    act(lambda e, pd=pd, m=m, T=T: e.copy(out=O[:, m, 0:T], in_=pd[:, 0:T]), [pd.r(0, T)], [xr(O, m, T)])


    stC = [nc.dram_tensor("stC%d" % j, [128, 8 * DV], F32, kind="Internal").ap() for j in range(2)]
    stCt = [P.dram("stCt%d" % j, 1) for j in range(2)]
    NCOL = [P.sb("ncol%d" % j, [128, 8], F32) for j in range(2)]
    M0 = [P.sb("m0_%d" % j, [128, 1], F32) for j in range(2)]
    BG = P.sb("bg", [128, 4], F32)
    rowsD = nc.dram_tensor("rowsD", [3, 4, TT], F32, kind="Internal").ap()
    rowsT = P.dram("rowsT", 3)
    m0D = nc.dram_tensor("m0D", [1, 4], F32, kind="Internal").ap()
    m0T = P.dram("m0T", 1)
    GHN = P.sb("ghn", [128, 2, KC], F32)
    SM = P.sb("sm", [128, 48], F32)

    def mlstm_mixer(c, l, j, first, last):
        T, L, NB = c.T, c.L, c.NB
        w = di["wA_in"][j]
        A0 = SCR0
        QFM = MAIN.sub("qfm", A0, [128, 8, TT], BF16)
        KTM = MAIN.sub("ktm", A0 + 8192, [128, 4, 1024], BF16)
        VTM = MAIN.sub("vtm", A0 + 16384, [128, 4, 2048], BF16)
        LI = MAIN.sub("li", A0 + 32768, [128, TT], F32)
        BP = MAIN.sub("bp", A0 + 34816, [128, TT], F32)
        GG = MAIN.sub("gg", A0 + 36864, [128, TT], F32)
        NM = MAIN.sub("nm", A0 + 38912, [128, TT], F32)
        GB = MAIN.sub("gb", A0 + 40960, [128, 4, 128], F32)
        NML = MAIN.sub("nml", A0 + 43008, [128, 4, 128], F32)
        WT = MAIN.sub("wt", A0 + 45056, [128, 4, 128], F32)
        PT = MAIN.sub("pt", A0 + 47104, [128, 4, 128], BF16)
        IWB = MAIN.sub("iwb", A0 + 48128, [128, 4, 128], F32)
        QP = MAIN.sub("qp", A0 + 50176, [128, 8, 128], BF16)
        KFC = MAIN.sub("kfc", A0 + 52224, [128, 8, 128], BF16)
        KW = MAIN.sub("kw", A0 + 52224, [128, 1024], BF16)
        NREP = MAIN.sub("nrep", A0 + 54272 - 2048 + 2048, [128, 8, 64], BF16) if False else None
        if first:
            if c.seq == 0:
                pool(lambda e: e.memset(C32[:, :, :], 0.0), [], [C32.r()])
                pool(lambda e: e.memset(NCOL[j][:, :], 0.0), [], [NCOL[j].r()])
                pool(lambda e: e.memset(M0[j][:, :], 0.0), [], [M0[j].r()])
            else:
                dma("sp", C32[:, :, :], di["state_mlstm_C"][j].rearrange("h (dc p) v -> p (h dc) v", p=128), writes=[C32.r()])
                for blk in range(8):
                    dma("sp", NCOL[j][:, blk:blk + 1], di["state_mlstm_n"][j, blk // 2, (blk % 2) * 128:(blk % 2 + 1) * 128].rearrange("(p o) -> p o", o=1),
                        writes=[NCOL[j].r()], nonc=True)
                dma("sp", M0[j][0:4, 0:1], di["state_mlstm_m"][j, :].rearrange("(p o) -> p o", o=1), writes=[M0[j].r()], nonc=True)
        else:
            dma("sp", C32[:, :, :].rearrange("p a b -> p (a b)"), stC[j], reads=[stCt[j].r()], writes=[C32.r()])
        act(lambda e: e.copy(out=CBF[:, :, :], in_=C32[:, :, :]), [C32.r()], [CBF.r()])
        for r_ in range(2):
            dma("sp", BG[0:4, r_:r_ + 1], di["bA_gates"][j, r_ * 4:(r_ + 1) * 4].rearrange("(p o) -> p o", o=1), writes=[BG.r()], nonc=True)
        dve(lambda e: e.tensor_scalar(out=BG[0:4, 2:3], in0=BG[0:4, 1:2], scalar1=-1.0, scalar2=0.0, op0=ALU.mult, op1=ALU.add), [BG.r()], [BG.r()])
        dma("sp", GHN[:, j, :], di["gA_hnorm"][j, :].rearrange("(kc p) -> p kc", p=128), writes=[GHN.r()], nonc=True)
        for hb in range(4):
            sl, wv = colblock("wAin%d" % j, w, A_IN, hb * WB)
            for dc in range(2):
                bank = PS[dc]
                for kc in range(KC):
                    pe(lambda e, wv=wv, kc=kc, dc=dc, bank=bank: e.matmul(bank[:, 0:T], lhsT=wv[:, kc, dc * 128:(dc + 1) * 128], rhs=H[:, kc, 0:T],
                                                                          start=(kc == 0), stop=(kc == KC - 1)), [sl.r(), xr(H, kc, T)], [bank.r(0, T)])
                blk = hb * 2 + dc
                act(lambda e, bank=bank, blk=blk: e.activation(out=QFM[:, blk, 0:T], in_=bank[:, 0:T], func=AF.Identity, scale=DQK ** -0.5),
                    [bank.r(0, T)], [xr(QFM, blk, T)])
        for (dst, c0, nb_, wid) in ((KTM, 1024, 4, 1024), (VTM, 2048, 8, 2048)):
            for bb in range(nb_):
                sl, wv = colblock("wAin%d" % j, w, A_IN, c0 + bb * WB)
                for tb in range(NB):
                    bank = PS[(bb * NB + tb) % 4]
                    for kc in range(KC):
                        pe(lambda e, wv=wv, kc=kc, tb=tb, bank=bank: e.matmul(bank[0:L, 0:WB], lhsT=H[:, kc, tb * L:(tb + 1) * L], rhs=wv[:, kc, :],
                                                                              start=(kc == 0), stop=(kc == KC - 1)), [sl.r(), xr(H, kc, T)], [bank.r(0, WB)])
                    o_ = dst[0:L, tb, bb * WB:(bb + 1) * WB]
                    act(lambda e, bank=bank, o_=o_: e.copy(out=o_, in_=bank[0:L, 0:WB]), [bank.r(0, WB)],
                        [dst.r(tb * wid + bb * WB, tb * wid + (bb + 1) * WB)])
        srcg = w[:, 6144:6152].rearrange("(kc p) c -> p kc c", p=128)
        slg, wg_ = wblock("wAg%d" % j, 1, 0, srcg, KC * 8, [KC, 8])
        for gi in range(2):
            bank = PS[4 + gi * 3]
            for kc in range(KC):
                pe(lambda e, kc=kc, gi=gi, bank=bank: e.matmul(bank[0:4, 0:T], lhsT=wg_[:, kc, gi * 4:(gi + 1) * 4], rhs=H[:, kc, 0:T],
                                                                start=(kc == 0), stop=(kc == KC - 1)), [slg.r(), xr(H, kc, T)], [bank.r(0, T)])
        act(lambda e: e.activation(out=LI[0:4, 0:T], in_=PS[4][0:4, 0:T], func=AF.Identity, bias=BG[0:4, 0:1], scale=1.0), [PS[4].r(0, T), BG.r()], [LI.r()])
        act(lambda e: e.activation(out=NM[0:4, 0:T], in_=PS[7][0:4, 0:T], func=AF.Exp, bias=BG[0:4, 2:3], scale=-1.0), [PS[7].r(0, T), BG.r()], [NM.r()])
        act(lambda e: e.activation(out=GG[0:4, 0:T], in_=NM[0:4, 0:T], func=AF.Ln, bias=1.0, scale=1.0), [NM.r()], [GG.r()])
        dve(lambda e: e.tensor_tensor_scan(out=BP[0:4, 0:T], data0=ones_f[0:4, 0:T], data1=GG[0:4, 0:T], initial=0.0, op0=ALU.mult, op1=ALU.add),
            [ones_f.r(), GG.r()], [BP.r()])
        dve(lambda e: e.tensor_tensor(out=LI[0:4, 0:T], in0=LI[0:4, 0:T], in1=BP[0:4, 0:T], op=ALU.add), [LI.r(), BP.r()], [LI.r()])
        dve(lambda e: e.tensor_tensor_scan(out=GG[0:4, 0:T], data0=LI[0:4, 0:T], data1=LI[0:4, 0:T], initial=M0[j][0:4, 0:1], op0=ALU.max, op1=ALU.max),
            [LI.r(), M0[j].r()], [GG.r()])
        dve(lambda e: e.tensor_tensor(out=NM[0:4, 0:T], in0=BP[0:4, 0:T], in1=GG[0:4, 0:T], op=ALU.subtract), [BP.r(), GG.r()], [NM.r()])
        dma("sp", m0D[0, :].rearrange("(p o) -> p o", o=1), M0[j][0:4, 0:1], reads=[M0[j].r()], writes=[m0T.r()], nonc=True)
        dma("sp", SM[:, 12:16], m0D[0:1, :].partition_broadcast(128), reads=[m0T.r()], writes=[SM.r(12, 16)], nonc=True)
        for qi, rb in enumerate((GG, NM, LI)):
            dma("sp", rowsD[qi, :, 0:T], rb[0:4, 0:T], reads=[rb.r()], writes=[rowsT.r(qi, qi + 1)])
        dve(lambda e: e.tensor_tensor(out=M0[j][0:4, 0:1], in0=GG[0:4, T - 1:T], in1=BP[0:4, T - 1:T], op=ALU.subtract), [GG.r(), BP.r()], [M0[j].r()])
        NREPt = P_nrep
        dve(lambda e: e.tensor_tensor(out=NREPt[:, :, :], in0=ones_f[:, 0:1024].rearrange("p (a b) -> p a b", a=8) if False else ones8,
                                      in1=NCOL[j][:, :].unsqueeze(2).to_broadcast([128, 8, 128]), op=ALU.mult), [ones_f.r(), NCOL[j].r()], [NREPt.r()])
        def do_chunk(cc):
            t0 = cc * L
            dma("sp", GB[:, :, 0:L], rowsD[0:1, :, t0:t0 + L].partition_broadcast(128), reads=[rowsT.r(0, 1)], writes=[GB.r()], nonc=True)
            dma("sp", NML[:, :, 0:L], rowsD[1:2, :, t0:t0 + L].partition_broadcast(128), reads=[rowsT.r(1, 2)], writes=[NML.r()], nonc=True)
            dma("sp", SM[0:L, 0:4], rowsD[2, :, t0:t0 + L].rearrange("h s -> s h"), reads=[rowsT.r(2, 3)], writes=[SM.r(0, 4)], nonc=True)
            for half in range(2):
                bank = PS[half]
                for q in range(4):
                    blk = half * 4 + q
                    pe(lambda e, blk=blk, q=q, bank=bank: e.matmul(bank[:, q * 128:q * 128 + L], lhsT=KTM[0:L, cc, blk * 128:(blk + 1) * 128], rhs=ident[0:L, 0:L],
                                                                    start=True, stop=True), [KTM.r(cc * 1024, (cc + 1) * 1024), ident.r()], [bank.r(q * 128, q * 128 + L)])
                src = bank.t[:, :].rearrange("p (q t) -> p q t", q=4)[:, :, 0:L]
                act(lambda e, src=src, half=half: e.copy(out=KFC[:, half * 4:(half + 1) * 4, 0:L], in_=src), [bank.r()], [KFC.r(half * 512, (half + 1) * 512)])
            for h in range(4):
                for dc in range(2):
                    blk = 2 * h + dc
                    pe(lambda e, h=h, dc=dc, blk=blk: e.matmul(PS[2][0:L, h * 128:h * 128 + L], lhsT=KFC[:, blk, 0:L], rhs=QFM[:, blk, t0:t0 + L],
                                                                start=(dc == 0), stop=(dc == 1)), [KFC.r(), xr(QFM, blk, T)], [PS[2].r(h * 128, h * 128 + L)])
            mb_ = maskbig[0:L, 0:L].unsqueeze(1).to_broadcast([L, 4, L])
            pool(lambda e, mb_=mb_: e.tensor_tensor(out=WT[0:L, :, 0:L], in0=GB[0:L, :, 0:L], in1=mb_, op=ALU.add), [GB.r(), maskbig.r()], [WT.r()])
            for h in range(4):
                act(lambda e, h=h: e.activation(out=WT[0:L, h, 0:L], in_=WT[0:L, h, 0:L], func=AF.Exp, bias=SM[0:L, h:h + 1], scale=-1.0),
                    [WT.r(), SM.r(0, 4)], [WT.r()])
            ps_s = PS[2].t[:, :].rearrange("p (h t) -> p h t", h=4)[0:L, :, 0:L]
            dve(lambda e, ps_s=ps_s: e.tensor_tensor(out=PT[0:L, :, 0:L], in0=ps_s, in1=WT[0:L, :, 0:L], op=ALU.mult), [PS[2].r(), WT.r()], [PT.r()])
            for h in range(4):
                act(lambda e, h=h: e.activation(out=IWB[:, h, 0:L], in_=GB[:, h, 0:L], func=AF.Exp, bias=SM[:, 12 + h:13 + h], scale=-1.0),
                    [GB.r(), SM.r(12, 16)], [IWB.r()])
            act(lambda e: e.activation(out=NML[:, :, 0:L], in_=NML[:, :, 0:L], func=AF.Exp), [NML.r()], [NML.r()])
            qv = QFM.t.rearrange("p (h d) t -> p h d t", d=2)
            qpv = QP.t.rearrange("p (h d) t -> p h d t", d=2)
            for dc in range(2):
                dve(lambda e, dc=dc: e.tensor_tensor(out=qpv[:, :, dc, 0:L], in0=qv[:, :, dc, t0:t0 + L], in1=IWB[:, :, 0:L], op=ALU.mult),
                    [QFM.r(), IWB.r()], [QP.r()])
            for h in range(4):
                pe(lambda e, h=h: e.matmul(PS[3][:, h * 128:h * 128 + L], lhsT=ones_bf[0:L, :], rhs=PT[0:L, h, 0:L], start=True, stop=False),
                   [ones_bf.r(), PT.r()], [PS[3].r(h * 128, h * 128 + L)])
                for dc in range(2):
                    blk = 2 * h + dc
                    pe(lambda e, h=h, dc=dc, blk=blk: e.matmul(PS[3][:, h * 128:h * 128 + L], lhsT=NREPt[:, blk, :], rhs=QP[:, blk, 0:L], start=False, stop=(dc == 1)),
                       [NREPt.r(), QP.r()], [PS[3].r(h * 128, h * 128 + L)])
            FL = WT
            pd_ = PS[3].t[:, :].rearrange("p (h t) -> p h t", h=4)[:, :, 0:L]
            act(lambda e, pd_=pd_: e.copy(out=FL[:, :, 0:L], in_=pd_), [PS[3].r(), PT.r()], [FL.r()])
            dve(lambda e: e.scalar_tensor_tensor(out=FL[:, :, 0:L], in0=FL[:, :, 0:L], scalar=-1.0, in1=FL[:, :, 0:L], op0=ALU.mult, op1=ALU.max), [FL.r()], [FL.r()])
            dve(lambda e: e.tensor_tensor(out=FL[:, :, 0:L], in0=FL[:, :, 0:L], in1=NML[:, :, 0:L], op=ALU.max), [FL.r(), NML.r()], [FL.r()])
            dve(lambda e: e.reciprocal(out=FL[:, :, 0:L], in_=FL[:, :, 0:L]), [FL.r()], [FL.r()])
            for h in range(4):
                bank = PS[4] if h % 2 == 0 else PS[7]
                for vc in range(4):
                    pe(lambda e, h=h, vc=vc, bank=bank: e.matmul(bank[:, vc * 128:vc * 128 + L], lhsT=VTM[0:L, cc, h * 512 + vc * 128:h * 512 + (vc + 1) * 128],
                                                                  rhs=PT[0:L, h, 0:L], start=True, stop=False),
                       [VTM.r(cc * 2048, (cc + 1) * 2048), PT.r()], [bank.r(vc * 128, vc * 128 + L)])
                    for dc in range(2):
                        blk = 2 * h + dc
                        pe(lambda e, h=h, vc=vc, dc=dc, blk=blk, bank=bank: e.matmul(bank[:, vc * 128:vc * 128 + L], lhsT=CBF[:, blk, vc * 128:(vc + 1) * 128],
                                                                                      rhs=QP[:, blk, 0:L], start=False, stop=(dc == 1)),
                           [CBF.r(), QP.r()], [bank.r(vc * 128, vc * 128 + L)])
                pn_ = bank.t[:, :].rearrange("p (v t) -> p v t", v=4)[:, :, 0:L]
                rf_ = FL[:, h, 0:L].unsqueeze(1).to_broadcast([128, 4, L])
                dve(lambda e, pn_=pn_, rf_=rf_, h=h: e.tensor_tensor(out=O[:, 4 * h:4 * h + 4, t0:t0 + L], in0=pn_, in1=rf_, op=ALU.mult),
                    [bank.r(), FL.r()], [xr(O, 4 * h, T, 4)])
            dve(lambda e: e.tensor_scalar(out=SM[:, 8:12], in0=GB[:, :, L - 1], scalar1=-1.0, scalar2=0.0, op0=ALU.mult, op1=ALU.add), [GB.r()], [SM.r(8, 12)])
            for h in range(4):
                act(lambda e, h=h: e.activation(out=SM[0:L, 4 + h:5 + h], in_=SM[0:L, h:h + 1], func=AF.Exp, bias=SM[0:L, 8 + h:9 + h], scale=1.0),
                    [SM.r(0, 4), SM.r(8, 12)], [SM.r(4, 8)])
            for h in range(4):
                act(lambda e, h=h: e.activation(out=KW[0:L, h * 256:(h + 1) * 256], in_=KTM[0:L, cc, h * 256:(h + 1) * 256], func=AF.Identity, scale=SM[0:L, 4 + h:5 + h]),
                    [KTM.r(cc * 1024, (cc + 1) * 1024), SM.r(4, 8), KFC.r()], [KW.r()])
            d8 = SM[:, 16:24].rearrange("p (h d) -> p h d", d=2)
            for dc in range(2):
                dve(lambda e, dc=dc, d8=d8: e.tensor_copy(out=d8[:, :, dc], in_=IWB[:, :, L - 1]), [IWB.r()], [SM.r(16, 24)])
            dve(lambda e: e.tensor_copy(out=SM[:, 12:16], in_=GB[:, :, L - 1]), [GB.r()], [SM.r(12, 16)])
            for blk in range(8):
                pe(lambda e, blk=blk: e.matmul(PS[6][:, blk:blk + 1], lhsT=KW[0:L, blk * 128:(blk + 1) * 128], rhs=ones_bf[0:L, 0:1], start=True, stop=True),
                   [KW.r(), ones_bf.r()], [PS[6].r(blk, blk + 1)])
            dve(lambda e: e.tensor_tensor(out=NCOL[j][:, :], in0=NCOL[j][:, :], in1=SM[:, 16:24], op=ALU.mult), [NCOL[j].r(), SM.r(16, 24)], [NCOL[j].r()])
            dve(lambda e: e.tensor_tensor(out=NCOL[j][:, :], in0=NCOL[j][:, :], in1=PS[6][:, 0:8], op=ALU.add), [NCOL[j].r(), PS[6].r(0, 8)], [NCOL[j].r()])
            dve(lambda e: e.tensor_tensor(out=NREPt[:, :, :], in0=ones8, in1=NCOL[j][:, :].unsqueeze(2).to_broadcast([128, 8, 128]), op=ALU.mult),
                [ones_f.r(), NCOL[j].r()], [NREPt.r()])
            for blk in range(8):
                h = blk // 2
                pc_ = PS[blk % 2]
                pe(lambda e, blk=blk, h=h, pc_=pc_: e.matmul(pc_[:, 0:DV], lhsT=KW[0:L, blk * 128:(blk + 1) * 128], rhs=VTM[0:L, cc, h * 512:(h + 1) * 512], start=True, stop=True),
                   [KW.r(), VTM.r(cc * 2048, (cc + 1) * 2048)], [pc_.r()])
                dve(lambda e, blk=blk, pc_=pc_: e.scalar_tensor_tensor(out=C32[:, blk, :], in0=C32[:, blk, :], scalar=SM[:, 16 + blk:17 + blk], in1=pc_[:, 0:DV], op0=ALU.mult, op1=ALU.add),
                    [C32.r(blk * DV, (blk + 1) * DV), SM.r(16, 24), pc_.r()], [C32.r(blk * DV, (blk + 1) * DV)])
                act(lambda e, blk=blk: e.copy(out=CBF[:, blk, :], in_=C32[:, blk, :]), [C32.r(blk * DV, (blk + 1) * DV)], [CBF.r(blk * DV, (blk + 1) * DV)])
        for cc_ in range(NB):
            do_chunk(cc_)
        if last:
            pre = "p" if c.seq == 0 else "s"
            dma("sp", do[pre + "_C"][j].rearrange("h (dc p) v -> p (h dc) v", p=128), C32[:, :, :], reads=[C32.r()])
            for blk in range(8):
                dma("sp", do[pre + "_n"][j, blk // 2, (blk % 2) * 128:(blk % 2 + 1) * 128].rearrange("(p o) -> p o", o=1), NCOL[j][:, blk:blk + 1],
                    reads=[NCOL[j].r()], nonc=True)
            dma("sp", do[pre + "_m"][j, :].rearrange("(p o) -> p o", o=1), M0[j][0:4, 0:1], reads=[M0[j].r()], nonc=True)
        else:
            dma("sp", stC[j], C32[:, :, :].rearrange("p a b -> p (a b)"), reads=[C32.r()], writes=[stCt[j].r()])
        HG = MAIN.sub("hg", A0, [128, KC, TT], BF16)
        SQ = MAIN.sub("sq", A0 + 16384, [128, 4, TT], BF16)
        RSH = MAIN.sub("rsh", A0 + 20480, [128, TT], F32)
        OS = MAIN.sub("os", A0 + 22528, [128, 2, TT], F32)
        T1 = MAIN.sub("t1", A0 + 26624, [128, 2, TT], F32)
        for h in range(4):
            for vc in range(4):
                m = 4 * h + vc
                act(lambda e, m=m, vc=vc: e.activation(out=SQ[:, vc, 0:T], in_=O[:, m, 0:T], func=AF.Square), [xr(O, m, T)], [xr(SQ, vc, T)])
                pe(lambda e, vc=vc: e.matmul(PS[5][:, 0:T], lhsT=ones_bf[:, :], rhs=SQ[:, vc, 0:T], start=(vc == 0), stop=(vc == 3)),
                   [ones_bf.r(), xr(SQ, vc, T)], [PS[5].r(0, T)])
            act(lambda e: e.activation(out=RSH[:, 0:T], in_=PS[5][:, 0:T], func=AF.Sqrt, bias=epsb[:, 0:1], scale=1.0 / DV), [PS[5].r(0, T), epsb.r()], [RSH.r()])
            dve(lambda e: e.reciprocal(out=RSH[:, 0:T], in_=RSH[:, 0:T]), [RSH.r()], [RSH.r()])
            for ob in range(2):
                sl, wv = colblock("wAin%d" % j, w, A_IN, 4096 + (2 * h + ob) * WB)
                for half in range(2):
                    m = 4 * h + ob * 2 + half
                    bank = PS[m % 2]
                    for kc in range(KC):
                        pe(lambda e, wv=wv, kc=kc, half=half, bank=bank: e.matmul(bank[:, 0:T], lhsT=wv[:, kc, half * 128:(half + 1) * 128], rhs=H[:, kc, 0:T],
                                                                                  start=(kc == 0), stop=(kc == KC - 1)), [sl.r(), xr(H, kc, T)], [bank.r(0, T)])
                    tb = m % 2
                    act(lambda e, bank=bank, tb=tb: e.activation(out=OS[:, tb, 0:T], in_=bank[:, 0:T], func=AF.Sigmoid), [bank.r(0, T)], [xr(OS, tb, T)])
                    dve(lambda e, m=m, tb=tb: e.scalar_tensor_tensor(out=T1[:, tb, 0:T], in0=O[:, m, 0:T], scalar=GHN[:, j, m:m + 1], in1=RSH[:, 0:T], op0=ALU.mult, op1=ALU.mult),
                        [xr(O, m, T), GHN.r(), RSH.r()], [xr(T1, tb, T)])
                    dve(lambda e, m=m, tb=tb: e.tensor_tensor(out=HG[:, m, 0:T], in0=T1[:, tb, 0:T], in1=OS[:, tb, 0:T], op=ALU.mult),
                        [xr(T1, tb, T), xr(OS, tb, T)], [xr(HG, m, T)])
        outproj([c], "wAout%d" % j, di["wA_out"][j], HG)


    NS5 = 8
    s5MB = nc.dram_tensor("s5MB", [S5G, 128, 128], BF16, kind="Internal").ap()
    s5MBN = nc.dram_tensor("s5MBN", [S5G, 128, 128], BF16, kind="Internal").ap()
    s5MD = nc.dram_tensor("s5MD", [S5G, 128, 128], BF16, kind="Internal").ap()
    s5KT = nc.dram_tensor("s5KT", [S5G, 128, 128], BF16, kind="Internal").ap()
    s5A8 = nc.dram_tensor("s5A8", [2, S5G, S5N], F32, kind="Internal").ap()
    s5T = P.dram("s5T", 5)
    A8S = P.sb("a8s", [128, 2, 64], F32)
    XS = P.sb("xs5", [128, 2, 64], F32)
    DFM = P.sb("dfm", [128, KC], F32)
    ZT = CST.sub("zt", 16384, [128, 8, 240], BF16)

    def s5_generate():
        G0 = 0
        LR = MAIN.sub("g_lr", G0, [128, 64], F32)
        LIm = MAIN.sub("g_li", G0 + 256, [128, 64], F32)
        DT = MAIN.sub("g_dt", G0 + 512, [128, 4], F32)
        AR = MAIN.sub("g_ar", G0 + 1024, [128, 64], F32)
        AI = MAIN.sub("g_ai", G0 + 1280, [128, 64], F32)
        T1 = MAIN.sub("g_t1", G0 + 1536, [128, 64], F32)
        T2 = MAIN.sub("g_t2", G0 + 1792, [128, 64], F32)
        T3 = MAIN.sub("g_t3", G0 + 2048, [128, 64], F32)
        FR = MAIN.sub("g_fr", G0 + 2304, [128, 64], F32)
        FI = MAIN.sub("g_fi", G0 + 2560, [128, 64], F32)
        PW = MAIN.sub("g_pw", G0 + 3072, [128, 2, 9, 64], F32)
        NP_ = MAIN.sub("g_np", G0 + 7680, [128, 2, 9, 64], F32)
        BR = MAIN.sub("g_br", G0 + 12288, [128, 64, 16], F32)
        BI = MAIN.sub("g_bi", G0 + 16384, [128, 64, 16], F32)
        BBR = MAIN.sub("g_bbr", G0 + 20480, [128, 64, 16], F32)
        BBI = MAIN.sub("g_bbi", G0 + 24576, [128, 64, 16], F32)
        CR = MAIN.sub("g_cr", G0 + 28672, [128, 16, 64], F32)
        CI = MAIN.sub("g_ci", G0 + 32768, [128, 16, 64], F32)
        W1 = MAIN.sub("g_w1", G0 + 36864, [128, 1024], F32)
        W2 = MAIN.sub("g_w2", G0 + 40960, [128, 1024], F32)
        EO = MAIN.sub("g_eo", G0 + 45056, [128, 128, 128], BF16)
        g2 = lambda ap: ap
        dma("sp", LR[:, :], di["s5_A_re"][0], writes=[LR.r()])
        dma("sp", LIm[:, :], di["s5_A_im"][0], writes=[LIm.r()])
        dma("sp", DT[:, 0:1], di["s5_log_dt"][0, :].rearrange("(p o) -> p o", o=1), writes=[DT.r()], nonc=True)
        dma("sp", BR[:, :, :], di["s5_B_re"][0], writes=[BR.r()])
        dma("sp", BI[:, :, :], di["s5_B_im"][0], writes=[BI.r()])
        dma("sp", CR[:, :, :], di["s5_C_re"][0], writes=[CR.r()])
        dma("sp", CI[:, :, :], di["s5_C_im"][0], writes=[CI.r()])
        dma("sp", DFM[:, :], di["s5_D"][0].rearrange("(j gl) p -> gl p j", gl=8), writes=[DFM.r()], nonc=True) if False else None
        for gl in range(8):
            dma("sp", DFM[gl * 16:(gl + 1) * 16, :], di["s5_D"][0].rearrange("(j gl) p -> gl p j", gl=8)[gl], writes=[DFM.r()], nonc=True)
        act(lambda e: e.activation(out=DT[:, 1:2], in_=DT[:, 0:1], func=AF.Exp), [DT.r()], [DT.r()])
        dve(lambda e: e.tensor_scalar(out=DT[:, 2:3], in0=DT[:, 1:2], scalar1=1.0 / 16, scalar2=0.0, op0=ALU.mult, op1=ALU.add), [DT.r()], [DT.r()])
        act(lambda e: e.activation(out=T1[:, :], in_=LR[:, :], func=AF.Exp, scale=DT[:, 2:3]), [LR.r(), DT.r()], [T1.r()])
        act(lambda e: e.activation(out=T2[:, :], in_=LIm[:, :], func=AF.Sin, scale=DT[:, 2:3]), [LIm.r(), DT.r()], [T2.r()])
        dve(lambda e: e.tensor_scalar(out=T3[:, :], in0=LIm[:, :], scalar1=DT[:, 2:3], scalar2=math.pi / 2, op0=ALU.mult, op1=ALU.add), [LIm.r(), DT.r()], [T3.r()])
        act(lambda e: e.activation(out=T3[:, :], in_=T3[:, :], func=AF.Sin), [T3.r()], [T3.r()])
        dve(lambda e: e.tensor_tensor(out=AR[:, :], in0=T1[:, :], in1=T3[:, :], op=ALU.mult), [T1.r(), T3.r()], [AR.r()])
        dve(lambda e: e.tensor_tensor(out=AI[:, :], in0=T1[:, :], in1=T2[:, :], op=ALU.mult), [T1.r(), T2.r()], [AI.r()])

        def cmul(or_, oi_, ar_, ai_, br_, bi_, rr, ww):
            dve(lambda e: e.tensor_tensor(out=T1[:, :], in0=ar_, in1=br_, op=ALU.mult), rr, [T1.r()])
            dve(lambda e: e.tensor_tensor(out=T2[:, :], in0=ai_, in1=bi_, op=ALU.mult), rr, [T2.r()])
            dve(lambda e: e.tensor_tensor(out=T3[:, :], in0=ar_, in1=bi_, op=ALU.mult), rr, [T3.r()])
            dve(lambda e: e.tensor_tensor(out=oi_, in0=ai_, in1=br_, op=ALU.mult), rr, ww)
            dve(lambda e: e.tensor_tensor(out=oi_, in0=oi_, in1=T3[:, :], op=ALU.add), rr + [T3.r()], ww)
            dve(lambda e: e.tensor_tensor(out=or_, in0=T1[:, :], in1=T2[:, :], op=ALU.subtract), [T1.r(), T2.r()], ww)
        for _ in range(4):
            cmul(FR[:, :], FI[:, :], AR[:, :], AI[:, :], AR[:, :], AI[:, :], [AR.r(), AI.r()], [FR.r(), FI.r()])
            dve(lambda e: e.tensor_copy(out=AR[:, :], in_=FR[:, :]), [FR.r()], [AR.r()])
            dve(lambda e: e.tensor_copy(out=AI[:, :], in_=FI[:, :]), [FI.r()], [AI.r()])
        pool(lambda e: e.memset(PW[:, 0, 0, :], 1.0), [], [PW.r()])
        pool(lambda e: e.memset(PW[:, 1, 0, :], 0.0), [], [PW.r()])
        for k in range(1, 9):
            cmul(PW[:, 0, k, :], PW[:, 1, k, :], PW[:, 0, k - 1, :], PW[:, 1, k - 1, :], AR[:, :], AI[:, :], [PW.r(), AR.r(), AI.r()], [PW.r()])
        dve(lambda e: e.tensor_tensor(out=T1[:, :], in0=AR[:, :], in1=AR[:, :], op=ALU.mult), [AR.r()], [T1.r()])
        dve(lambda e: e.tensor_tensor(out=T2[:, :], in0=AI[:, :], in1=AI[:, :], op=ALU.mult), [AI.r()], [T2.r()])
        dve(lambda e: e.tensor_tensor(out=T1[:, :], in0=T1[:, :], in1=T2[:, :], op=ALU.add), [T1.r(), T2.r()], [T1.r()])
        dve(lambda e: e.reciprocal(out=T1[:, :], in_=T1[:, :]), [T1.r()], [T1.r()])
        dve(lambda e: e.tensor_tensor(out=NP_[:, 0, 1, :], in0=AR[:, :], in1=T1[:, :], op=ALU.mult), [AR.r(), T1.r()], [NP_.r()])
        dve(lambda e: e.scalar_tensor_tensor(out=NP_[:, 1, 1, :], in0=AI[:, :], scalar=-1.0, in1=T1[:, :], op0=ALU.mult, op1=ALU.mult), [AI.r(), T1.r()], [NP_.r()])
        dve(lambda e: e.tensor_copy(out=FR[:, :], in_=NP_[:, 0, 1, :]), [NP_.r()], [FR.r()])
        dve(lambda e: e.tensor_copy(out=FI[:, :], in_=NP_[:, 1, 1, :]), [NP_.r()], [FI.r()])
        for k in range(2, 9):
            cmul(NP_[:, 0, k, :], NP_[:, 1, k, :], NP_[:, 0, k - 1, :], NP_[:, 1, k - 1, :], FR[:, :], FI[:, :], [NP_.r(), FR.r(), FI.r()], [NP_.r()])
        dma("sp", s5A8[0], PW[:, 0, 8, :], reads=[PW.r()], writes=[s5T.r(4, 5)])
        dma("sp", s5A8[1], PW[:, 1, 8, :], reads=[PW.r()], writes=[s5T.r(4, 5)])
        for ri in range(2):
            for par in range(2):
                dma("sp", A8S[par * 64:(par + 1) * 64, ri, :], s5A8[ri].rearrange("(gp par) n -> par n gp", par=2)[par],
                    reads=[s5T.r(4, 5)], writes=[A8S.r()], nonc=True)
        dve(lambda e: e.tensor_tensor(out=T1[:, :], in0=LR[:, :], in1=LR[:, :], op=ALU.mult), [LR.r()], [T1.r()])
        dve(lambda e: e.tensor_tensor(out=T2[:, :], in0=LIm[:, :], in1=LIm[:, :], op=ALU.mult), [LIm.r()], [T2.r()])
        dve(lambda e: e.tensor_tensor(out=T1[:, :], in0=T1[:, :], in1=T2[:, :], op=ALU.add), [T1.r(), T2.r()], [T1.r()])
        dve(lambda e: e.reciprocal(out=T1[:, :], in_=T1[:, :]), [T1.r()], [T1.r()])
        dve(lambda e: e.tensor_scalar(out=T2[:, :], in0=AR[:, :], scalar1=1.0, scalar2=-1.0, op0=ALU.mult, op1=ALU.add), [AR.r()], [T2.r()])
        dve(lambda e: e.tensor_tensor(out=FR[:, :], in0=T2[:, :], in1=LR[:, :], op=ALU.mult), [T2.r(), LR.r()], [FR.r()])
        dve(lambda e: e.tensor_tensor(out=T3[:, :], in0=AI[:, :], in1=LIm[:, :], op=ALU.mult), [AI.r(), LIm.r()], [T3.r()])
        dve(lambda e: e.tensor_tensor(out=FR[:, :], in0=FR[:, :], in1=T3[:, :], op=ALU.add), [FR.r(), T3.r()], [FR.r()])
        dve(lambda e: e.tensor_tensor(out=FR[:, :], in0=FR[:, :], in1=T1[:, :], op=ALU.mult), [FR.r(), T1.r()], [FR.r()])
        dve(lambda e: e.tensor_tensor(out=FI[:, :], in0=AI[:, :], in1=LR[:, :], op=ALU.mult), [AI.r(), LR.r()], [FI.r()])
        dve(lambda e: e.tensor_tensor(out=T3[:, :], in0=T2[:, :], in1=LIm[:, :], op=ALU.mult), [T2.r(), LIm.r()], [T3.r()])
        dve(lambda e: e.tensor_tensor(out=FI[:, :], in0=FI[:, :], in1=T3[:, :], op=ALU.subtract), [FI.r(), T3.r()], [FI.r()])
        dve(lambda e: e.tensor_tensor(out=FI[:, :], in0=FI[:, :], in1=T1[:, :], op=ALU.mult), [FI.r(), T1.r()], [FI.r()])

        def bc_np(ap):
            return ap.unsqueeze(2).to_broadcast([128, 64, 16])

        def bc_pn(ap):
            return ap.unsqueeze(1).to_broadcast([128, 16, 64])
        w1n = W1.t.rearrange("p (n q) -> p n q", q=16)
        w2n = W2.t.rearrange("p (n q) -> p n q", q=16)
        w1p = W1.t.rearrange("p (q n) -> p q n", n=64)
        w2p = W2.t.rearrange("p (q n) -> p q n", n=64)

        def cmul_b(or_, oi_, ar_, ai_, br_, bi_, rr, ww, w1, w2, sgn_i=1.0, sgn_r=1.0):
            dve(lambda e: e.tensor_tensor(out=w1, in0=br_, in1=ar_, op=ALU.mult), rr, [W1.r()])
            dve(lambda e: e.tensor_tensor(out=w2, in0=bi_, in1=ai_, op=ALU.mult), rr, [W2.r()])
            if sgn_r > 0:
                dve(lambda e: e.tensor_tensor(out=or_, in0=w1, in1=w2, op=ALU.subtract), [W1.r(), W2.r()], ww)
            else:
                dve(lambda e: e.tensor_tensor(out=or_, in0=w2, in1=w1, op=ALU.subtract), [W1.r(), W2.r()], ww)
            dve(lambda e: e.tensor_tensor(out=w1, in0=bi_, in1=ar_, op=ALU.mult), rr + ww, [W1.r()])
            dve(lambda e: e.tensor_tensor(out=w2, in0=br_, in1=ai_, op=ALU.mult), rr + ww, [W2.r()])
            if sgn_i > 0:
                dve(lambda e: e.tensor_tensor(out=oi_, in0=w1, in1=w2, op=ALU.add), [W1.r(), W2.r()], ww)
            else:
                dve(lambda e: e.scalar_tensor_tensor(out=oi_, in0=w1, scalar=-1.0, in1=w2, op0=ALU.mult, op1=ALU.subtract), [W1.r(), W2.r()], ww)
        cmul_b(BBR[:, :, :], BBI[:, :, :], bc_np(FR[:, :]), bc_np(FI[:, :]), BR[:, :, :], BI[:, :, :], [FR.r(), FI.r(), BR.r(), BI.r()], [BBR.r(), BBI.r()], w1n, w2n)
        eo_sp = EO.t.rearrange("g (s q) (r n) -> g s q r n", q=16, n=64)
        bbr_pn = BBR.t.rearrange("g n q -> g q n")
        bbi_pn = BBI.t.rearrange("g n q -> g q n")
        for s_ in range(8):
            k = 7 - s_
            cmul_b(eo_sp[:, s_, :, 0, :], eo_sp[:, s_, :, 1, :], bc_pn(PW[:, 0, k, :]), bc_pn(PW[:, 1, k, :]), bbr_pn, bbi_pn,
                   [PW.r(), BBR.r(), BBI.r()], [EO.r()], w1p, w2p)
        dma("sp", s5MB.rearrange("g a b -> g (a b)"), EO.t.rearrange("g a b -> g (a b)"), reads=[EO.r()], writes=[s5T.r(0, 1)])
        eo_rn = EO.t.rearrange("g (r n) (s q) -> g r n s q", n=64, q=16)
        for s_ in range(8):
            k = s_ + 1
            cmul_b(eo_rn[:, 0, :, s_, :], eo_rn[:, 1, :, s_, :], bc_np(NP_[:, 0, k, :]), bc_np(NP_[:, 1, k, :]), BBR[:, :, :], BBI[:, :, :],
                   [NP_.r(), BBR.r(), BBI.r()], [EO.r()], w1n, w2n)
        dma("sp", s5MBN.rearrange("g a b -> g (a b)"), EO.t.rearrange("g a b -> g (a b)"), reads=[EO.r()], writes=[s5T.r(1, 2)])
        cr_np = CR.t.rearrange("g q n -> g n q")
        ci_np = CI.t.rearrange("g q n -> g n q")
        for i_ in range(8):
            k = i_ + 1
            cmul_b(eo_rn[:, 0, :, i_, :], eo_rn[:, 1, :, i_, :], bc_np(PW[:, 0, k, :]), bc_np(PW[:, 1, k, :]), cr_np, ci_np,
                   [PW.r(), CR.r(), CI.r()], [EO.r()], w1n, w2n, sgn_i=-1.0)
        dma("sp", s5MD.rearrange("g a b -> g (a b)"), EO.t.rearrange("g a b -> g (a b)"), reads=[EO.r()], writes=[s5T.r(2, 3)])
        KMASK = MAIN.sub("g_kmask", G0 + 512, [128, 128], BF16)
        ES = MAIN.sub("g_es", G0, [128, 8, 16], BF16)
        EI = MAIN.sub("g_ei", G0 + 256, [128, 8, 16], BF16)
        pool(lambda e: e.memset(ES[0:8, :, :], 1.0), [LR.r(), LIm.r(), DT.r(), AR.r()], [ES.r()])
        pool(lambda e: e.memset(EI[0:8, :, :], 1.0), [LR.r(), LIm.r(), DT.r(), AR.r()], [EI.r()])
        pool(lambda e: e.affine_select(out=ES[0:8, :, :], in_=ES[0:8, :, :], pattern=[[1, 8], [0, 16]], compare_op=ALU.is_equal, fill=0.0, base=0, channel_multiplier=-1),
             [ES.r()], [ES.r()])
        pool(lambda e: e.affine_select(out=EI[0:8, :, :], in_=EI[0:8, :, :], pattern=[[1, 8], [0, 16]], compare_op=ALU.is_ge, fill=0.0, base=0, channel_multiplier=-1),
             [EI.r()], [EI.r()])
        pe(lambda e: e.matmul(PS[0][:, 0:128], lhsT=ES[0:8, :, :].rearrange("k a b -> k (a b)"), rhs=EI[0:8, :, :].rearrange("k a b -> k (a b)"), start=True, stop=True),
           [ES.r(), EI.r()], [PS[0].r(0, 128)])
        act(lambda e: e.copy(out=KMASK[:, :], in_=PS[0][:, 0:128]), [PS[0].r(0, 128)], [KMASK.r()])
        LN = MAIN.sub("g_ln", G0 + 1024, [128, 8, 128], BF16)
        LD = MAIN.sub("g_ld", G0 + 3072, [128, 8, 128], BF16)
        KO = MAIN.sub("g_ko", G0 + 5120, [128, 8, 128], BF16)
        for jj in range(16):
            dma("sp", LN[:, :, :], s5MBN[8 * jj:8 * jj + 8].rearrange("g x y -> x g y"), reads=[s5T.r(1, 2)], writes=[LN.r()], nonc=True)
            dma("sp", LD[:, :, :], s5MD[8 * jj:8 * jj + 8].rearrange("g x y -> x g y"), reads=[s5T.r(2, 3)], writes=[LD.r()], nonc=True)
            for hb in range(2):
                bank = PS[1 + hb]
                for q in range(4):
                    gl = hb * 4 + q
                    pe(lambda e, gl=gl, q=q, bank=bank: e.matmul(bank[:, q * 128:(q + 1) * 128], lhsT=LN[:, gl, :], rhs=LD[:, gl, :], start=True, stop=True),
                       [LN.r(), LD.r()], [bank.r(q * 128, (q + 1) * 128)])
                km = KMASK[:, :].unsqueeze(1).to_broadcast([128, 4, 128])
                dve(lambda e, bank=bank, hb=hb, km=km: e.tensor_tensor(out=KO[:, hb * 4:(hb + 1) * 4, :], in0=bank.t[:, :].rearrange("p (q t) -> p q t", q=4), in1=km, op=ALU.mult),
                    [bank.r(), KMASK.r()], [KO.r(hb * 512, (hb + 1) * 512)])
            dma("sp", s5KT[8 * jj:8 * jj + 8].rearrange("g x y -> x g y"), KO[:, :, :], reads=[KO.r()], writes=[s5T.r(3, 4)], nonc=True)

    def s5_mixer(c, first, last):
        T = c.T
        NCK = T // NS5
        A0 = SCR0
        UP = MAIN.sub("s5_up", A0, [128, KC, 8, 64], BF16)
        SR = MAIN.sub("s5_sr", A0 + 16384, [128, 2, 65, 64], F32)
        YP = MAIN.sub("s5_yp", A0 + 49664, [128, 8, 64], BF16)
        XB = MAIN.sub("s5_xb", A0 + 50688, [128, 2, 64, 4], BF16)
        TT1 = MAIN.sub("s5_t1", A0 + 51712, [128, 4, 64], F32)
        YG = CST.sub("s5_yg", 0, [128, KC, TT], BF16)
        GT = P_gelu
        pool(lambda e: e.memset(ZT[:, :, :], 0.0), [], [ZT.r()])
        for a_ in range(8):
            pool(lambda e, a_=a_: e.tensor_copy(out=ZT[:, a_, 112:128], in_=ident[:, a_ * 16:(a_ + 1) * 16]), [ident.r(), ZT.r()], [ZT.r()])
        if first:
            if c.seq == 0:
                pool(lambda e: e.memset(XS[:, :, :], 0.0), [], [XS.r()])
            else:
                for ri, nm in enumerate(("state_s5_re", "state_s5_im")):
                    for par in range(2):
                        dma("sp", XS[par * 64:(par + 1) * 64, ri, :], di[nm][0].rearrange("(gp par) n -> par n gp", par=2)[par], writes=[XS.r()], nonc=True)
        dve(lambda e: e.tensor_copy(out=SR[:, :, 0, :], in_=XS[:, :, :]), [XS.r()], [SR.r()])
        for j in range(KC):
            slot = RING[ring_i[0] % NSLOT]
            ring_i[0] += 1
            PUb = PS[0] if j % 2 == 0 else PS[3]
            PBb = (PS[1], PS[2]) if j % 2 == 0 else (PS[5], PS[6])
            mbv = slot.t[:, 0:1024].rearrange("p (g x) -> p g x", g=8)
            dma("sp", mbv, s5MB[8 * j:8 * j + 8].rearrange("g x y -> x g y"), reads=[s5T.r(0, 1)], writes=[slot.r(0, 1024)], nonc=True)
            for gl in range(8):
                for s_ in range(8):
                    pe(lambda e, j=j, gl=gl, s_=s_, PUb=PUb: e.matmul(PUb[:, gl * 64:gl * 64 + NCK], lhsT=ZT[:, gl, (7 - s_) * 16:(7 - s_) * 16 + 128], rhs=H[:, j, s_:T:8],
                                                             start=(s_ == 0), stop=(s_ == 7)), [ZT.r(), xr(H, j, T)], [PUb.r(gl * 64, gl * 64 + NCK)])
            pu = PUb.t[:, :].rearrange("p (g c) -> p g c", g=8)[:, :, 0:NCK]
            act(lambda e, j=j, pu=pu: e.copy(out=UP[:, j, :, 0:NCK], in_=pu), [PUb.r()], [UP.r(j * 512, (j + 1) * 512)])
            for gl in range(8):
                par, gpl = gl % 2, gl // 2
                for ri in range(2):
                    pe(lambda e, j=j, gl=gl, par=par, gpl=gpl, ri=ri, mbv=mbv, PBb=PBb: e.matmul(
                        PBb[ri][par * 64:(par + 1) * 64, gpl * 64:gpl * 64 + NCK], lhsT=mbv[:, gl, ri * 64:(ri + 1) * 64], rhs=UP[:, j, gl, 0:NCK], start=True, stop=True),
                       [slot.r(0, 1024), UP.r(j * 512, (j + 1) * 512)], [PBb[ri].r(gpl * 64, gpl * 64 + NCK)])
            for ri in range(2):
                src = PBb[ri].t[:, 0:256].rearrange("p (g c) -> p c g", g=4)[:, 0:NCK, :]
                dve(lambda e, j=j, ri=ri, src=src: e.tensor_copy(out=SR[:, ri, 1:1 + NCK, 4 * j:4 * j + 4], in_=src), [PBb[ri].r(0, 256)], [SR.r()])
        def srr(ri, slot):
            o_ = (ri * 65 + slot) * 64
            return SR.r(o_, o_ + 64)
        for cc in range(NCK):
            xr_, xi_ = SR[:, 0, cc, :], SR[:, 1, cc, :]
            nr_, ni_ = SR[:, 0, cc + 1, :], SR[:, 1, cc + 1, :]
            dve(lambda e, xr_=xr_: e.tensor_tensor(out=TT1[:, 0, :], in0=xr_, in1=A8S[:, 0, :], op=ALU.mult), [srr(0, cc), A8S.r()], [TT1.r(0, 64)])
            dve(lambda e, xi_=xi_: e.tensor_tensor(out=TT1[:, 1, :], in0=xi_, in1=A8S[:, 1, :], op=ALU.mult), [srr(1, cc), A8S.r()], [TT1.r(64, 128)])
            dve(lambda e, xi_=xi_: e.tensor_tensor(out=TT1[:, 2, :], in0=xi_, in1=A8S[:, 0, :], op=ALU.mult), [srr(1, cc), A8S.r()], [TT1.r(128, 192)])
            dve(lambda e, xr_=xr_: e.tensor_tensor(out=TT1[:, 3, :], in0=xr_, in1=A8S[:, 1, :], op=ALU.mult), [srr(0, cc), A8S.r()], [TT1.r(192, 256)])
            dve(lambda e, nr_=nr_: e.tensor_tensor(out=nr_, in0=nr_, in1=TT1[:, 0, :], op=ALU.add), [srr(0, cc + 1), TT1.r(0, 64)], [srr(0, cc + 1)])
            dve(lambda e, nr_=nr_: e.tensor_tensor(out=nr_, in0=nr_, in1=TT1[:, 1, :], op=ALU.subtract), [srr(0, cc + 1), TT1.r(64, 128)], [srr(0, cc + 1)])
            dve(lambda e, ni_=ni_: e.tensor_tensor(out=ni_, in0=ni_, in1=TT1[:, 2, :], op=ALU.add), [srr(1, cc + 1), TT1.r(128, 192)], [srr(1, cc + 1)])
            dve(lambda e, ni_=ni_: e.tensor_tensor(out=ni_, in0=ni_, in1=TT1[:, 3, :], op=ALU.add), [srr(1, cc + 1), TT1.r(192, 256)], [srr(1, cc + 1)])
        dve(lambda e: e.tensor_copy(out=XS[:, :, :], in_=SR[:, :, NCK, :]), [SR.r()], [XS.r()])
        if last:
            pre = "p" if c.seq == 0 else "s"
            for ri, nm in enumerate(("_re", "_im")):
                for par in range(2):
                    dma("sp", do[pre + nm][0].rearrange("(gp par) n -> par n gp", par=2)[par], XS[par * 64:(par + 1) * 64, ri, :], reads=[XS.r()], nonc=True)
        for j in range(KC):
            slot = RING[ring_i[0] % NSLOT]
            ring_i[0] += 1
            PYb = PS[3] if j % 2 == 0 else PS[0]
            ktv = slot.t[:, 0:1024].rearrange("p (g x) -> p g x", g=8)
            mdv = slot.t[:, 1024:2048].rearrange("p (g x) -> p g x", g=8)
            dma("sp", ktv, s5KT[8 * j:8 * j + 8].rearrange("g x y -> x g y"), reads=[s5T.r(3, 4)], writes=[slot.r(0, 1024)], nonc=True)
            mdi = slot.t[:, 2048:3072].rearrange("p (g x) -> p g x", g=8)
            for par in range(2):
                srcg = s5MD[8 * j:8 * j + 8].rearrange("(gp par) x y -> par x gp y", par=2)[par]
                dma("sp", mdv[par * 64:(par + 1) * 64, 0:4, :], srcg[0:64], reads=[s5T.r(2, 3)], writes=[slot.r(1024, 2048)], nonc=True)
                dma("sp", mdi[par * 64:(par + 1) * 64, 0:4, :], srcg[64:128], reads=[s5T.r(2, 3)], writes=[slot.r(2048, 3072)], nonc=True)
            for ri in range(2):
                dve(lambda e, j=j, ri=ri: e.tensor_copy(out=XB[:, ri, 0:NCK, :], in_=SR[:, ri, 0:NCK, 4 * j:4 * j + 4]), [SR.r()], [XB.r(ri * 256, (ri + 1) * 256)])
            for gl in range(8):
                par, gpl = gl % 2, gl // 2
                pe(lambda e, j=j, gl=gl, ktv=ktv, PYb=PYb: e.matmul(PYb[:, gl * 64:gl * 64 + NCK], lhsT=ktv[:, gl, :], rhs=UP[:, j, gl, 0:NCK], start=True, stop=False),
                   [slot.r(0, 1024), UP.r(j * 512, (j + 1) * 512)], [PYb.r(gl * 64, gl * 64 + NCK)])
                pe(lambda e, gl=gl, par=par, gpl=gpl, mdv=mdv, PYb=PYb: e.matmul(PYb[:, gl * 64:gl * 64 + NCK], lhsT=mdv[par * 64:(par + 1) * 64, gpl, :],
                                                                       rhs=XB[par * 64:(par + 1) * 64, 0, 0:NCK, gpl], start=False, stop=False),
                   [slot.r(1024, 2048), XB.r()], [PYb.r(gl * 64, gl * 64 + NCK)])
                pe(lambda e, gl=gl, par=par, gpl=gpl, mdi=mdi, PYb=PYb: e.matmul(PYb[:, gl * 64:gl * 64 + NCK], lhsT=mdi[par * 64:(par + 1) * 64, gpl, :],
                                                                       rhs=XB[par * 64:(par + 1) * 64, 1, 0:NCK, gpl], start=False, stop=True),
                   [slot.r(2048, 3072), XB.r()], [PYb.r(gl * 64, gl * 64 + NCK)])
            py = PYb.t[:, :].rearrange("p (g c) -> p g c", g=8)[:, :, 0:NCK]
            act(lambda e, py=py: e.copy(out=YP[:, :, 0:NCK], in_=py), [PYb.r()], [YP.r()])
            bank = PS[4] if j % 2 == 0 else PS[7]
            for i_ in range(8):
                for gl in range(8):
                    pe(lambda e, i_=i_, gl=gl, bank=bank: e.matmul(bank[:, i_ * 64:i_ * 64 + NCK], lhsT=ZT[:, i_, (7 - gl) * 16:(7 - gl) * 16 + 128], rhs=YP[:, gl, 0:NCK],
                                                                    start=(gl == 0), stop=(gl == 7)), [ZT.r(), YP.r()], [bank.r(i_ * 64, i_ * 64 + NCK)])
            po = bank.t[:, :].rearrange("p (i c) -> p c i", i=8)[:, 0:NCK, :]
            hv = H[:, j, 0:T].rearrange("p (c i) -> p c i", i=8)
            g0 = GT[:, 0, 0:T].rearrange("p (c i) -> p c i", i=8)
            dve(lambda e, j=j, po=po, hv=hv, g0=g0: e.scalar_tensor_tensor(out=g0, in0=hv, scalar=DFM[:, j:j + 1], in1=po, op0=ALU.mult, op1=ALU.add),
                [xr(H, j, T), DFM.r(), bank.r()], [GT.r(0, TT)])
            act(lambda e: e.activation(out=GT[:, 1, 0:T], in_=GT[:, 0, 0:T], func=AF.Square), [GT.r(0, TT)], [GT.r(TT, 2 * TT)])
            dve(lambda e: e.tensor_scalar(out=GT[:, 1, 0:T], in0=GT[:, 1, 0:T], scalar1=0.044715, scalar2=1.0, op0=ALU.mult, op1=ALU.add), [GT.r(TT, 2 * TT)], [GT.r(TT, 2 * TT)])
            dve(lambda e: e.tensor_tensor(out=GT[:, 1, 0:T], in0=GT[:, 1, 0:T], in1=GT[:, 0, 0:T], op=ALU.mult), [GT.r(0, 2 * TT)], [GT.r(TT, 2 * TT)])
            act(lambda e: e.activation(out=GT[:, 1, 0:T], in_=GT[:, 1, 0:T], func=AF.Sigmoid, scale=2.0 * math.sqrt(2.0 / math.pi)), [GT.r(TT, 2 * TT)], [GT.r(TT, 2 * TT)])
            dve(lambda e, j=j: e.tensor_tensor(out=YG[:, j, 0:T], in0=GT[:, 1, 0:T], in1=GT[:, 0, 0:T], op=ALU.mult), [GT.r(0, 2 * TT)], [xr(YG, j, T)])
        wc = di["wC_out"][0]
        for mb in range(D // WB):
            sa, wa = colblock("wCout", wc, 2 * D, mb * WB)
            sb_, wb_ = colblock("wCout", wc, 2 * D, D + mb * WB)
            for half in range(2):
                m = 2 * mb + half
                pa = PS[0 + (m % 2) * 2]
                pb = PS[1 + (m % 2) * 2]
                for (pp, wv, sl) in ((pa, wa, sa), (pb, wb_, sb_)):
                    for kc in range(KC):
                        pe(lambda e, pp=pp, wv=wv, kc=kc, half=half: e.matmul(pp[:, 0:T], lhsT=wv[:, kc, half * 128:(half + 1) * 128], rhs=YG[:, kc, 0:T],
                                                                              start=(kc == 0), stop=(kc == KC - 1)), [sl.r(), xr(YG, kc, T)], [pp.r(0, T)])
                tb = m % 2
                act(lambda e, pb=pb, tb=tb: e.activation(out=tmpA[:, tb, 0:T], in_=pb[:, 0:T], func=AF.Sigmoid), [pb.r(0, T)], [tmpA.r(tb * TT, tb * TT + T)])
                dve(lambda e, pa=pa, tb=tb, m=m: e.tensor_tensor(out=O[:, m, 0:T], in0=pa[:, 0:T], in1=tmpA[:, tb, 0:T], op=ALU.mult),
                    [pa.r(0, T), tmpA.r(tb * TT, tb * TT + T)], [xr(O, m, T)])

    for l_ in range(n_layers):
        for _ in mod_layer(l_):
            pass
    if n_layers > 2 and mix[2] == "r":
        s5_generate()
    dma("sp", wconv[:, :, :], di["wB_conv"][0].rearrange("(kc p) t -> p kc t", p=128), writes=[wconv.r()], nonc=True)

    passes = []
    for ti in range(n_tiles):
        passes.append(mk_ctx(TT, 0, di["x_prompt"][ti * TT:(ti + 1) * TT, :], do["y_prompt"][ti * TT:(ti + 1) * TT, :]))
    if with_sample:
        passes.append(mk_ctx(TS, 1, di["x_sample"], do["y_sample"]))

    prev_seq = None
    for c in passes:
        if c.seq != prev_seq:
            if c.seq == 0:
                pool(lambda e: e.memset(ZP[:, :, :], 0.0), [], [ZP.r()])
            else:
                for r in range(2):
                    dma("sp", ZP[:, :, r], di["state_conv"][0, r].rearrange("(kc p) -> p kc", p=128), writes=[ZP.r()], nonc=True)
            prev_seq = c.seq
        load_x(c)
        for l in range(n_layers):
            mod_side = None
            prenorm(c, l, 0)
            kind = l % 3
            idx_c = passes.index(c)
            first = idx_c == 0 or passes[idx_c - 1].seq != c.seq
            lastt = idx_c == len(passes) - 1 or passes[idx_c + 1].seq != c.seq
            if mix[l] == "r" and kind == 1:
                conv_mixer([c], l)
            elif mix[l] == "r" and kind == 0:
                mlstm_mixer(c, l, l // 3, first, lastt)
            elif mix[l] == "r" and kind == 2:
                s5_mixer(c, first, lastt)
            else:
                for kc in range(KC):
                    act(lambda e, kc=kc, T=c.T: e.copy(out=O[:, kc, 0:T], in_=H[:, kc, 0:T]), [xr(H, kc, c.T)], [xr(O, kc, c.T)])
            postnorm(c, l, 0)
            prenorm(c, l, 1)
            ffn([c], l, side=(mod_side if c is passes[0] else None))
            if mod_side is not None:
                for _ in mod_side:
                    pass
            postnorm(c, l, 1, final=(l == n_layers - 1))
        store_y(c)
        last_of_seq = (c is passes[-1]) or (passes[passes.index(c) + 1].seq != c.seq)
        if last_of_seq:
            pre = "p" if c.seq == 0 else "s"
            for r in range(2):
                dma("sp", do[pre + "_conv"][0, r].rearrange("(kc p) -> p kc", p=128), ZP[:, :, r], reads=[ZP.r()], nonc=True)

    P.emit()
    return nc, P


STATE_KEYS = ("state_mlstm_C", "state_mlstm_n", "state_mlstm_m", "state_conv", "state_s5_re", "state_s5_im")


def core_inputs(inputs, b, seq, n_layers=DEPTH):
    m = {}
    for name, shp in IN_SPECS:
        a = inputs[name]
        if shp is not None and shp[0] == DEPTH:
            a = a[:max(n_layers, 1)]
        if name == "x_prompt":
            a = a[b, :seq]
        elif name == "x_sample":
            a = a[b]
        elif name in STATE_KEYS:
            a = a[:, b]
        elif name in ("c_prompt", "c_sample"):
            a = a[b:b + 1]
        m[name] = np.ascontiguousarray(a, dtype=np.float32)
    return m


_PROG = {}


def kernel(**inputs):
    n = 8
    seq = inputs["x_prompt"].shape[1]
    n_tiles = seq // TT
    key = n_tiles
    nc, _ = build_program(n_tiles=n_tiles, n_layers=DEPTH, with_sample=True, mix="rrrr")
    in_maps = [core_inputs(inputs, b, seq) for b in range(n)]
    res = run_bass_kernel_spmd(nc, in_maps, core_ids=list(range(n)))
    R = res.results

    def stack(name, axis):
        return np.stack([np.asarray(R[b][name], dtype=np.float32) for b in range(n)], axis=axis)

    outs = [stack("y_prompt", 0), stack("y_sample", 0)]
    for pre in ("p", "s"):
        for nm in ("_C", "_n", "_m", "_conv", "_re", "_im"):
            outs.append(stack(pre + nm, 1))
    return tuple(outs)
```
